# Optimizing a Trainium2 kernel written in Bass

```python
import jax, jax.numpy as jnp
from jax import lax
import numpy as np

D_MODEL = 1024
BATCH = 2
SEQ = 16384
DEPTH = 2
DEC_BATCH = 16
DEC_SEQ = 32
PAST_LEN = 2048

CHUNK = 64
Q_BLOCK = 128
EPS = 1e-6
MLA_HEADS = 8
QK_NOPE = 64
QK_ROPE = 32
V_HEAD = 64
Q_LORA = 256
KV_LORA = 256
ROPE_THETA = 10000.0
QK_DIM = QK_NOPE + QK_ROPE
ATTN_SCALE = QK_DIM ** -0.5
GLA_HEADS = 4
GLA_DK = 64
GLA_DV = 128
GATE_LR = 16
GATE_TAU = 16.0
MIX_WIDTH = MLA_HEADS * V_HEAD + GLA_HEADS * GLA_DV
IN_SIZES = (Q_LORA, KV_LORA, QK_ROPE, GLA_HEADS * GLA_DK, GLA_HEADS * GLA_DK,
            GLA_HEADS * GLA_DV, GLA_HEADS * GLA_DV, GATE_LR)
IN_WIDTH = int(sum(IN_SIZES))
IN_SPLITS = tuple(int(s) for s in np.cumsum(IN_SIZES)[:-1])
D_FF = -(-8 * D_MODEL // (3 * 256)) * 256

kernel_name = "hymba_mla_gla_streaming_step"


def rmsnorm(x, g):
    xf = x.astype(jnp.float32)
    y = xf * lax.rsqrt(jnp.mean(xf * xf, axis=-1, keepdims=True) + EPS)
    return (y * g.astype(jnp.float32)).astype(x.dtype)


def rope(x, pos):
    half = QK_ROPE // 2
    inv = ROPE_THETA ** (-jnp.arange(half, dtype=jnp.float32) / half)
    ang = pos.astype(jnp.float32)[:, None] * inv[None, :]
    ang = ang.reshape((ang.shape[0],) + (1,) * (x.ndim - 3) + (half,))
    cos, sin = jnp.cos(ang), jnp.sin(ang)
    x1 = x[..., :half].astype(jnp.float32)
    x2 = x[..., half:].astype(jnp.float32)
    return jnp.concatenate([x1 * cos - x2 * sin, x1 * sin + x2 * cos], axis=-1).astype(x.dtype)


def mixer_proj(h, pos, w_in, g_qn, w_uq, g_kvn, w_a2, b_a2):
    B, T, _ = h.shape
    z = h @ w_in
    c_q, c_kv, k_r, gq, gk, gv, gg, ga = jnp.split(z, IN_SPLITS, axis=-1)
    q = (rmsnorm(c_q, g_qn) @ w_uq).reshape(B, T, MLA_HEADS, QK_DIM)
    q = jnp.concatenate([q[..., :QK_NOPE], rope(q[..., QK_NOPE:], pos)], axis=-1)
    c_kv = rmsnorm(c_kv, g_kvn)
    k_r = rope(k_r, pos)
    gq = gq.reshape(B, T, GLA_HEADS, GLA_DK) * (GLA_DK ** -0.5)
    gk = gk.reshape(B, T, GLA_HEADS, GLA_DK)
    gv = gv.reshape(B, T, GLA_HEADS, GLA_DV)
    gg = gg.reshape(B, T, GLA_HEADS, GLA_DV)
    lg = jax.nn.log_sigmoid((ga @ w_a2 + b_a2).astype(jnp.float32)) / GATE_TAU
    lg = lg.reshape(B, T, GLA_HEADS, GLA_DK)
    return q, c_kv, k_r, gq, gk, gv, gg, lg


def mla_kv(c_kv, k_r, w_ukv):
    B, T, _ = c_kv.shape
    kv = (c_kv @ w_ukv).reshape(B, T, MLA_HEADS, QK_NOPE + V_HEAD)
    k_nope, v = kv[..., :QK_NOPE], kv[..., QK_NOPE:]
    k = jnp.concatenate([k_nope, jnp.broadcast_to(k_r[:, :, None, :], (B, T, MLA_HEADS, QK_ROPE))], axis=-1)
    return k, v


def chunk_causal_attn(q, k, v, q_pos, k_pos):
    s = jnp.einsum('bqhd,bkhd->bhqk', q, k, preferred_element_type=jnp.float32) * ATTN_SCALE
    allowed = (k_pos[None, :] // CHUNK) <= (q_pos[:, None] // CHUNK)
    s = jnp.where(allowed[None, None], s, -jnp.inf)
    p = jax.nn.softmax(s, axis=-1).astype(v.dtype)
    return jnp.einsum('bhqk,bkhv->bqhv', p, v)


def mla_prompt(q, k, v, pos):
    B, T, H, D = q.shape
    nb = T // Q_BLOCK
    qb = q.reshape(B, nb, Q_BLOCK, H, D).swapaxes(0, 1)
    pb = pos.reshape(nb, Q_BLOCK)
    o = lax.map(lambda a: chunk_causal_attn(a[0], k, v, a[1], pos), (qb, pb))
    return o.swapaxes(0, 1).reshape(B, T, H, V_HEAD)


def gla_chunk(S, q, k, v, lg):
    q = q.astype(jnp.float32)
    k = k.astype(jnp.float32)
    v = v.astype(jnp.float32)
    L = q.shape[1]
    b = jnp.cumsum(lg, axis=1)
    o_inter = jnp.einsum('blhk,bhkv->blhv', q * jnp.exp(b), S)
    mask = jnp.tril(jnp.ones((L, L), dtype=bool))
    diff = b[:, :, None] - b[:, None, :]
    decay = jnp.exp(jnp.where(mask[None, :, :, None, None], diff, -jnp.inf))
    a = jnp.einsum('bihk,bjhk,bijhk->bhij', q, k, decay)
    o_intra = jnp.einsum('bhij,bjhv->bihv', a, v)
    b_last = b[:, -1]
    S_new = jnp.exp(b_last)[..., None] * S + jnp.einsum(
        'bjhk,bjhv->bhkv', k * jnp.exp(b_last[:, None] - b), v)
    return S_new, o_inter + o_intra


def gla_prompt(q, k, v, lg):
    B, T, H, DK = q.shape
    nc = T // CHUNK

    def to_chunks(a):
        return a.reshape((B, nc, CHUNK) + a.shape[2:]).swapaxes(0, 1)

    S0 = jnp.zeros((B, H, DK, GLA_DV), jnp.float32)
    S_fin, o = lax.scan(lambda S, xs: gla_chunk(S, *xs), S0,
                        (to_chunks(q), to_chunks(k), to_chunks(v), to_chunks(lg)))
    return S_fin, o.swapaxes(0, 1).reshape(B, T, H, GLA_DV)


def mixer_out(o_mla, o_gla, gate, g_on, w_o):
    B, T = o_mla.shape[:2]
    o_g = (rmsnorm(o_gla, g_on) * jax.nn.silu(gate.astype(jnp.float32))).astype(o_mla.dtype)
    cat = jnp.concatenate([o_mla.reshape(B, T, MLA_HEADS * V_HEAD),
                           o_g.reshape(B, T, GLA_HEADS * GLA_DV)], axis=-1)
    return cat @ w_o


def swiglu(h, w_gu, w_down):
    a, u = jnp.split(h @ w_gu, 2, axis=-1)
    return (jax.nn.silu(a) * u) @ w_down


def setup_inputs(seed: int = 0) -> dict:
    key = jax.random.key(seed)
    ks = jax.random.split(key, 20)

    def nrm(k, shape, scale):
        return jax.random.normal(k, shape, jnp.float32) * scale

    def gain(k, shape):
        return 1.0 + 0.05 * jax.random.normal(k, shape, jnp.float32)

    return {
        "x_prompt": nrm(ks[0], (BATCH, SEQ, D_MODEL), 1.0),
        "x_sample": nrm(ks[1], (DEC_BATCH, DEC_SEQ, D_MODEL), 1.0),
        "cache_ckv": nrm(ks[2], (DEPTH, DEC_BATCH, PAST_LEN, KV_LORA), 1.0),
        "cache_krope": nrm(ks[3], (DEPTH, DEC_BATCH, PAST_LEN, QK_ROPE), 1.0),
        "state_gla": nrm(ks[4], (DEPTH, DEC_BATCH, GLA_HEADS, GLA_DK, GLA_DV), 0.5),
        "g_attn": gain(ks[5], (DEPTH, D_MODEL)),
        "w_in": nrm(ks[6], (DEPTH, D_MODEL, IN_WIDTH), D_MODEL ** -0.5),
        "g_qn": gain(ks[7], (DEPTH, Q_LORA)),
        "w_uq": nrm(ks[8], (DEPTH, Q_LORA, MLA_HEADS * QK_DIM), Q_LORA ** -0.5),
        "g_kvn": gain(ks[9], (DEPTH, KV_LORA)),
        "w_ukv": nrm(ks[10], (DEPTH, KV_LORA, MLA_HEADS * (QK_NOPE + V_HEAD)), KV_LORA ** -0.5),
        "w_a2": nrm(ks[11], (DEPTH, GATE_LR, GLA_HEADS * GLA_DK), GATE_LR ** -0.5),
        "b_a2": nrm(ks[12], (DEPTH, GLA_HEADS * GLA_DK), 0.1),
        "g_gla_on": gain(ks[13], (DEPTH, GLA_DV)),
        "w_o": nrm(ks[14], (DEPTH, MIX_WIDTH, D_MODEL), MIX_WIDTH ** -0.5),
        "g_ffn": gain(ks[15], (DEPTH, D_MODEL)),
        "w_gu": nrm(ks[16], (DEPTH, D_MODEL, 2 * D_FF), D_MODEL ** -0.5),
        "w_down": nrm(ks[17], (DEPTH, D_FF, D_MODEL), D_FF ** -0.5),
        "g_final": gain(ks[18], (D_MODEL,)),
    }


def reference(x_prompt, x_sample, cache_ckv, cache_krope, state_gla,
              g_attn, w_in, g_qn, w_uq, g_kvn, w_ukv, w_a2, b_a2, g_gla_on, w_o,
              g_ffn, w_gu, w_down, g_final):
    seq = x_prompt.shape[1]
    dec_seq = x_sample.shape[1]
    past = cache_ckv.shape[2]
    pos_p = jnp.arange(seq, dtype=jnp.int32)
    pos_s = past + jnp.arange(dec_seq, dtype=jnp.int32)
    pos_all = jnp.arange(past + dec_seq, dtype=jnp.int32)

    xp, xs = x_prompt, x_sample
    ckv_p, kr_p, st_p, ckv_s, kr_s, st_s = [], [], [], [], [], []
    for l in range(DEPTH):
        proj = (w_in[l], g_qn[l], w_uq[l], g_kvn[l], w_a2[l], b_a2[l])
        h = rmsnorm(xp, g_attn[l])
        q, c_kv, k_r, gq, gk, gv, gg, lg = mixer_proj(h, pos_p, *proj)
        k, v = mla_kv(c_kv, k_r, w_ukv[l])
        o_mla = mla_prompt(q, k, v, pos_p)
        S_fin, o_gla = gla_prompt(gq, gk, gv, lg)
        xp = xp + mixer_out(o_mla, o_gla, gg, g_gla_on[l], w_o[l])
        xp = xp + swiglu(rmsnorm(xp, g_ffn[l]), w_gu[l], w_down[l])
        ckv_p.append(c_kv)
        kr_p.append(k_r)
        st_p.append(S_fin)
        h = rmsnorm(xs, g_attn[l])
        q, c_kv, k_r, gq, gk, gv, gg, lg = mixer_proj(h, pos_s, *proj)
        ckv_full = jnp.concatenate([cache_ckv[l].astype(c_kv.dtype), c_kv], axis=1)
        kr_full = jnp.concatenate([cache_krope[l].astype(k_r.dtype), k_r], axis=1)
        k, v = mla_kv(ckv_full, kr_full, w_ukv[l])
        o_mla = chunk_causal_attn(q, k, v, pos_s, pos_all)
        S_new, o_gla = gla_chunk(state_gla[l].astype(jnp.float32), gq, gk, gv, lg)
        xs = xs + mixer_out(o_mla, o_gla, gg, g_gla_on[l], w_o[l])
        xs = xs + swiglu(rmsnorm(xs, g_ffn[l]), w_gu[l], w_down[l])
        ckv_s.append(c_kv)
        kr_s.append(k_r)
        st_s.append(S_new)

    y_prompt = rmsnorm(xp, g_final)
    y_sample = rmsnorm(xs, g_final)
    new_ckv_prompt = jnp.stack(ckv_p)
    new_krope_prompt = jnp.stack(kr_p)
    new_gla_prompt = jnp.stack(st_p)
    new_ckv_sample = jnp.stack(ckv_s)
    new_krope_sample = jnp.stack(kr_s)
    new_gla_sample = jnp.stack(st_s)
    return (y_prompt, y_sample, new_ckv_prompt, new_krope_prompt, new_gla_prompt,
            new_ckv_sample, new_krope_sample, new_gla_sample)
```

```python
import math
from contextlib import ExitStack

import numpy as np
import concourse.bass as bass
import concourse.mybir as mybir
from concourse.bass_utils import run_bass_kernel_spmd

F32 = mybir.dt.float32
BF16 = mybir.dt.bfloat16
I32 = mybir.dt.int32
AF = mybir.ActivationFunctionType
ALU = mybir.AluOpType
AX = mybir.AxisListType

D = 1024
SEQ = 16384
NTOK = 4096
NS = 64
NROW = NTOK + NS
PAST = 2048
DFF = 2816
EPS = 1e-6
ATTN_SCALE = 96 ** -0.5
OCQ, OCKV, OKR, OGQ, OGK, OGV, OGG, OGA, OKRP = 0, 256, 512, 544, 800, 1056, 1568, 2080, 2096
WIN = 2128
TWO_PI = 2.0 * math.pi
C1 = 6.28125
C2 = TWO_PI - C1


class Trk:
    __slots__ = ("name", "w", "r")

    def __init__(self, name):
        self.name = name
        self.w = None
        self.r = []


class Eng:
    def __init__(self, fw, name, eng):
        self.name = name
        self.eng = eng
        self.sem = fw.new_sem("e_" + name)
        self.count = 0
        self.waited = {}


class FW:
    def __init__(self, nc, stack):
        self.nc = nc
        self.stack = stack
        self.nsem = 0
        self.pe = Eng(self, "pe", nc.tensor)
        self.dve = Eng(self, "dve", nc.vector)
        self.act = Eng(self, "act", nc.scalar)
        self.pool = Eng(self, "pool", nc.gpsimd)
        self.sp = Eng(self, "sp", nc.sync)
        self.engs = [self.pe, self.dve, self.act, self.pool, self.sp]
        self.dma_sems = {}
        self.n_instr = 0

    def new_sem(self, name):
        self.nsem += 1
        return self.stack.enter_context(self.nc.semaphore(name))

    def _need(self, E, dep):
        if dep is None:
            return
        key, val = dep
        if E.waited.get(key, 0) >= val:
            return
        sem = key.sem if isinstance(key, Eng) else self.dma_sems[key][0]
        E.eng.wait_ge(sem, val)
        E.waited[key] = val
        self.n_instr += 1

    def _deps(self, E, reads, writes, is_dma=False):
        reads = [t for t in reads if t.name != "dram"]
        writes = [t for t in writes if t.name != "dram"]
        for t in reads:
            if t.w is not None and (is_dma or not (t.w[0] is E and E is self.pe)):
                self._need(E, t.w)
        for t in writes:
            if t.w is not None and (is_dma or t.w[0] is not E):
                self._need(E, t.w)
            for r in t.r:
                if is_dma or r[0] is not E:
                    self._need(E, r)

    def _mark(self, stamp, reads, writes):
        reads = [t for t in reads if t.name != "dram"]
        writes = [t for t in writes if t.name != "dram"]
        for t in reads:
            t.r.append(stamp)
            if len(t.r) > 8:
                last = {}
                for k, v in t.r:
                    if last.get(k, 0) < v:
                        last[k] = v
                t.r = list(last.items())
        for t in writes:
            t.w = stamp
            t.r = []

    def op(self, E, fn, reads=(), writes=()):
        self._deps(E, reads, writes)
        ins = fn(E.eng)
        E.count += 1
        ins.then_inc(E.sem, 1)
        self._mark((E, E.count), reads, writes)
        self.n_instr += 1
        return ins

    def dma(self, Q, out, in_, key, reads=(), writes=()):
        if key not in self.dma_sems:
            self.dma_sems[key] = [self.new_sem("d_" + key), 0, 16]
        self._deps(Q, reads, writes, is_dma=True)
        ent = self.dma_sems[key]
        ins = Q.eng.dma_start(out=out, in_=in_)
        ent[1] += 16
        ins.then_inc(ent[0], 16)
        self._mark((key, ent[1]), reads, writes)
        self.n_instr += 1
        return ins

    def coll(self, fn, key, reads=(), writes=()):
        if key not in self.dma_sems:
            self.dma_sems[key] = [self.new_sem("c_" + key), 0, 1]
        Q = self.pool
        self._deps(Q, reads, writes, is_dma=True)
        ent = self.dma_sems[key]
        ins = fn(Q.eng)
        ent[1] += 1
        ins.then_inc(ent[0], 1)
        self._mark((key, ent[1]), reads, writes)
        self.n_instr += 1
        return ins

    def barrier_all(self):
        for E in self.engs:
            for P in self.engs:
                if P.count > 0 and P is not E:
                    self._need(E, (P, P.count))
            for key, ent in self.dma_sems.items():
                if ent[1] > 0:
                    self._need(E, (key, ent[1]))


import os
KSTOP = os.environ.get("KSTOP", "")


class _Stop(Exception):
    pass


def build_program():
    nc = bass.Bass("TRN2", target_bir_lowering=False)

    def din(name, shape, dt=F32):
        return nc.dram_tensor(name, list(shape), dt, kind="ExternalInput").ap()

    def dout(name, shape, dt=F32):
        return nc.dram_tensor(name, list(shape), dt, kind="ExternalOutput").ap()

    def dscr(name, shape, dt):
        return nc.dram_tensor(name, list(shape), dt)

    x_in = din("x", [NROW, D])
    cckv = din("cckv", [2, 2, PAST, 256])
    ckr = din("ckr", [2, 2, PAST, 32])
    sgla = din("sgla", [2, 2, 4, 64, 128])
    w_in = din("w_in", [2, D, WIN])
    w_uq_loc = din("w_uq_loc", [2, 256, 2, 2, 96])
    w_uq_all = din("w_uq_all", [2, 256, 8, 2, 96])
    w_uk_loc = din("w_uk_loc", [2, 256, 2, 64])
    w_uv_loc = din("w_uv_loc", [2, 256, 2, 64])
    w_uk_all = din("w_uk_all", [2, 256, 8, 64])
    w_uv_all = din("w_uv_all", [2, 256, 8, 64])
    w_a2 = din("w_a2", [2, 16, 256])
    b_a2 = din("b_a2", [2, 1, 256])
    g_attn = din("g_attn", [2, 1, D])
    g_ffn = din("g_ffn", [2, 1, D])
    g_final = din("g_final", [1, D])
    g_cat = din("g_cat", [2, 1, 512])
    g_on = din("g_on", [2, 1, 128])
    w_o = din("w_o", [2, D, D])
    w_gu = din("w_gu", [2, 22, 128, 8, 256])
    w_dn = din("w_dn", [2, DFF, D])
    ident = din("ident", [128, 128])
    trilt = din("trilt", [64, 64])
    triut = din("triut", [64, 64])
    selm = din("selm", [128, 8])
    pos_g = din("pos_g", [1, SEQ])
    pos_l = din("pos_l", [1, NROW])
    ropec = din("ropec", [96, 2])
    selmat = din("selmat", [128, 64])

    y_out = dout("y", [NROW, D])
    ckv_out = dout("ckv_o", [2, NROW, 256])
    kr_out = dout("kr_o", [2, NROW, 32])
    gla_out = dout("gla_o", [2, 3, 64, 512])

    xs = dscr("xs", [NROW, D], F32).ap()
    latL = [dscr("latL%d" % i, [544, 512], BF16).ap() for i in range(8)]
    latG = [dscr("latG%d" % i, [4 * 544, 512], BF16).ap() for i in range(8)]
    latS = dscr("latS", [544, NS], BF16).ap()
    glaL = dscr("glaL", [64, 516], F32)
    glaG = dscr("glaG", [256, 516], F32)
    oloc = dscr("oloc", [NROW, 512], F32).ap()
    qhT = dscr("qhT", [64, 4, NROW], BF16).ap()
    sgg = dscr("sgg", [NROW, 512], F32).ap()
    oL = [dscr("oL%d" % i, [128, NTOK], BF16).ap() for i in range(4)]
    oG = [dscr("oG%d" % i, [512, NTOK], BF16).ap() for i in range(4)]
    oS = dscr("oS", [512, NS], BF16).ap()
    tabGC = dscr("tabGC", [96, SEQ], F32).ap()
    tabGS = dscr("tabGS", [96, SEQ], F32).ap()
    tabLC = dscr("tabLC", [96, NROW], F32).ap()
    tabLS = dscr("tabLS", [96, NROW], F32).ap()
    glaLa, glaGa = glaL.ap(), glaG.ap()
    wguB = [dscr("wguB%d" % i, [22, 128, 2048], BF16).ap() for i in range(2)]
    woB = [dscr("woB%d" % i, [D, D], BF16).ap() for i in range(2)]
    wdB = [dscr("wdB%d" % i, [DFF, D], BF16).ap() for i in range(2)]

    GROUPS = [[0, 1, 2, 3], [4, 5, 6, 7]]

    with ExitStack() as top:
        fw = FW(nc, top)
        pe, dve, act, pool, sp = fw.pe, fw.dve, fw.act, fw.pool, fw.sp

        uniq = [0]

        def mk(stack, kind, name, shape, dt):
            f = nc.sbuf_tensor if kind == "sb" else nc.psum_tensor
            uniq[0] += 1
            return stack.enter_context(f("%s_%d" % (name, uniq[0]), list(shape), dt)), Trk(name)

        def V(fn, r=(), w=()):
            return fw.op(dve, fn, r, w)

        def A(fn, r=(), w=()):
            return fw.op(act, fn, r, w)

        def G(fn, r=(), w=()):
            return fw.op(pool, fn, r, w)

        def P(fn, r=(), w=()):
            return fw.op(pe, fn, r, w)

        def ld(out, in_, key, w, r=()):
            return fw.dma(sp, out, in_, key, reads=r, writes=w)

        def ldc(out, in_, key, w, r=()):
            return fw.dma(pool, out, in_, key, reads=r, writes=w)

        def stq(out, in_, key, r, w=()):
            return fw.dma(sp, out, in_, key, reads=r, writes=w)

        pb = []
        for i in range(7):
            pb.append(mk(top, "ps", "pb%d" % i, [128, 512], F32))
        pT, TpT = mk(top, "ps", "pbT", [128, 1024], BF16)

        idb, Tidb = mk(top, "sb", "idb", [128, 128], BF16)
        idf, Tidf = mk(top, "sb", "idf", [128, 128], F32)
        tlt, Ttlt = mk(top, "sb", "tlt", [64, 64], F32)
        tut, Ttut = mk(top, "sb", "tut", [64, 64], F32)
        sel, Tsel = mk(top, "sb", "sel", [128, 8], F32)
        smat, Tsmat = mk(top, "sb", "smat", [128, 64], F32)
        gfin, Tgfin = mk(top, "sb", "gfin", [128, D], F32)
        Sst, TSst = mk(top, "sb", "Sst", [64, 512], F32)
        Sbf, TSbf = mk(top, "sb", "Sbf", [64, 512], BF16)
        Bc, TBc = mk(top, "sb", "Bc", [64, 4], F32)
        eBc, TeBc = mk(top, "sb", "eBc", [64, 4], F32)
        Twc = Trk("wconv")
        TlatLa = [Trk("latLa%d" % i) for i in range(8)]
        TlatLb = [Trk("latLb%d" % i) for i in range(8)]
        TlatG = [Trk("latG%d" % i) for i in range(8)]
        ToL = [Trk("oL%d" % i) for i in range(4)]
        ToG = [Trk("oG%d" % i) for i in range(4)]
        TglaL, TglaG = Trk("glaL"), Trk("glaG")

        def allgather(src, dst, r, w):
            r = list(r) + [Twc]
            fw.coll(lambda e: e.collective_compute("AllGather", ALU.bypass, replica_groups=GROUPS, ins=[src.opt()], outs=[dst.opt()]), "cc", r, w)

        Rb, TRb = mk(top, "sb", "Rb", [64, 512], BF16)
        Tdram = Trk("dram")

        ld(idf[:], ident[:, :], "c0", [Tidf])
        ldc(idb[:], ident[:, :], "c1", [Tidb])
        ld(tlt[:], trilt[:, :], "c2", [Ttlt])
        ld(tut[:], triut[:, :], "c3", [Ttut])
        ld(sel[:], selm[:, :], "c4", [Tsel])
        ld(smat[:], selmat[:, :], "c5", [Tsmat])
        ld(gfin[:], g_final[0:1, :].partition_broadcast(128), "c6", [Tgfin])

        def rstd_inplace(ap, T_, inv_n):
            V(lambda e: e.tensor_scalar(out=ap, in0=ap, scalar1=inv_n, scalar2=EPS, op0=ALU.mult, op1=ALU.add), [T_], [T_])
            A(lambda e: e.activation(out=ap, in_=ap, func=AF.Ln), [T_], [T_])
            A(lambda e: e.activation(out=ap, in_=ap, func=AF.Exp, scale=-0.5), [T_], [T_])

        tg = ExitStack()
        TGW = 1024
        rc, Trc = mk(tg, "sb", "rc", [96, 2], F32)
        ptl, Tptl = mk(tg, "sb", "ptl", [96, TGW], F32)
        ang, Tang = mk(tg, "sb", "ang", [96, TGW], F32)
        aa, Taa = mk(tg, "sb", "aa", [96, TGW], F32)
        tt, Ttt = mk(tg, "sb", "tt", [96, TGW], F32)
        ki, Tki = mk(tg, "sb", "ki", [96, TGW], I32)
        mm, Tmm = mk(tg, "sb", "mm", [96, TGW], F32)
        res, Tres = mk(tg, "sb", "res", [96, TGW], F32)
        ld(rc[:], ropec[:, :], "p0", [Trc])

        TtabG = [Trk("tabG%d" % i) for i in range(SEQ // 1024)]

        def table_steps(pos_ap, ntot, dC, dS, chunk_trks=None):
            steps = []

            def wrap_and_sin(dst_dram, n, use_sign, wtrk=None):
                steps.append(lambda: V(lambda e: e.tensor_scalar(out=tt[:, 0:n], in0=mm[:, 0:n], scalar1=math.pi, scalar2=None, op0=ALU.is_gt), [Tmm], [Ttt]))
                steps.append(lambda: V(lambda e: e.scalar_tensor_tensor(out=mm[:, 0:n], in0=tt[:, 0:n], scalar=-TWO_PI, in1=mm[:, 0:n], op0=ALU.mult, op1=ALU.add), [Ttt, Tmm], [Tmm]))
                steps.append(lambda: V(lambda e: e.tensor_scalar(out=tt[:, 0:n], in0=mm[:, 0:n], scalar1=-math.pi, scalar2=None, op0=ALU.is_lt), [Tmm], [Ttt]))
                steps.append(lambda: V(lambda e: e.scalar_tensor_tensor(out=mm[:, 0:n], in0=tt[:, 0:n], scalar=TWO_PI, in1=mm[:, 0:n], op0=ALU.mult, op1=ALU.add), [Ttt, Tmm], [Tmm]))
                steps.append(lambda: V(lambda e: e.tensor_scalar(out=aa[:, 0:n], in0=mm[:, 0:n], scalar1=3.1415925, scalar2=-3.1415925, op0=ALU.min, op1=ALU.max), [Tmm], [Taa]))
                steps.append(lambda: A(lambda e: e.activation(out=res[:, 0:n], in_=aa[:, 0:n], func=AF.Sin), [Taa], [Tres]))
                if use_sign:
                    steps.append(lambda: V(lambda e: e.tensor_scalar(out=res[:, 0:n], in0=res[:, 0:n], scalar1=rc[:, 1:2], scalar2=None, op0=ALU.mult), [Tres, Trc], [Tres]))
                steps.append(lambda: stq(dst_dram, res[:, 0:n], "p2", [Tres], [wtrk if wtrk is not None else Tdram]))

            for c0 in range(0, ntot, TGW):
                n = min(TGW, ntot - c0)
                steps.append(lambda c0=c0, n=n: ld(ptl[:, 0:n], pos_ap[0:1, c0:c0 + n].partition_broadcast(96), "p1", [Tptl]))
                steps.append(lambda n=n: V(lambda e: e.tensor_scalar(out=ang[:, 0:n], in0=ptl[:, 0:n], scalar1=rc[:, 0:1], scalar2=None, op0=ALU.mult), [Tptl, Trc], [Tang]))
                steps.append(lambda n=n: V(lambda e: e.tensor_scalar(out=tt[:, 0:n], in0=ang[:, 0:n], scalar1=1.0 / TWO_PI, scalar2=None, op0=ALU.mult), [Tang], [Ttt]))
                steps.append(lambda n=n: V(lambda e: e.tensor_copy(out=ki[:, 0:n], in_=tt[:, 0:n]), [Ttt], [Tki]))
                steps.append(lambda n=n: V(lambda e: e.tensor_copy(out=tt[:, 0:n], in_=ki[:, 0:n]), [Tki], [Ttt]))
                steps.append(lambda n=n: V(lambda e: e.scalar_tensor_tensor(out=mm[:, 0:n], in0=tt[:, 0:n], scalar=-C1, in1=ang[:, 0:n], op0=ALU.mult, op1=ALU.add), [Ttt, Tang], [Tmm]))
                steps.append(lambda n=n: V(lambda e: e.scalar_tensor_tensor(out=mm[:, 0:n], in0=tt[:, 0:n], scalar=-C2, in1=mm[:, 0:n], op0=ALU.mult, op1=ALU.add), [Ttt, Tmm], [Tmm]))
                wtrk_ = chunk_trks[c0 // TGW] if chunk_trks is not None else None
                wrap_and_sin(dS[:, c0:c0 + n], n, True, wtrk_)
                steps.append(lambda n=n: V(lambda e: e.tensor_scalar(out=mm[:, 0:n], in0=mm[:, 0:n], scalar1=math.pi / 2, scalar2=None, op0=ALU.add), [Tmm], [Tmm]))
                wrap_and_sin(dC[:, c0:c0 + n], n, False, wtrk_)
            return steps

        for st_ in table_steps(pos_l, NROW, tabLC, tabLS):
            st_()
        fw.barrier_all()
        bg_steps = table_steps(pos_g, SEQ, tabGC, tabGS, TtabG)
        STEPS_PER_CHUNK = len(bg_steps) // (SEQ // TGW)

        def bg_run(k):
            for _ in range(k):
                if bg_steps:
                    bg_steps.pop(0)()

        if KSTOP == "pro":
            return nc

        TILES = [(t * 512, 512, 128, 64, False) for t in range(NTOK // 512)] + [(NTOK, NS, 64, 32, True)]

        for l in range(2):
            with ExitStack() as ph:
                win, Twin = mk(ph, "sb", "win", [128, 8, WIN], BF16)
                wa2, Twa2 = mk(ph, "sb", "wa2", [16, 256], BF16)
                ba2, Tba2 = mk(ph, "sb", "ba2", [64, 256], F32)
                gat, Tgat = mk(ph, "sb", "gat", [128, D], F32)
                gct, Tgct = mk(ph, "sb", "gct", [64, 512], F32)
                xt, Txt = mk(ph, "sb", "xt", [128, 4, D], F32)
                junk, Tjunk = mk(ph, "sb", "junk", [128, D], F32)
                ss, Tss = mk(ph, "sb", "ss", [128, 1], F32)
                hb_b = [mk(ph, "sb", "hb%d" % i, [128, D], BF16) for i in range(2)]
                hT_b = [mk(ph, "sb", "hT%d" % i, [128, 8, 512], BF16) for i in range(2)]
                gqT, TgqT = mk(ph, "sb", "gqT", [64, 4, 512], F32)
                gkT, TgkT = mk(ph, "sb", "gkT", [64, 4, 512], F32)
                gaT, TgaT = mk(ph, "sb", "gaT", [16, 512], BF16)
                tbC, TtbC = mk(ph, "sb", "tbC", [32, 512], F32)
                tbS, TtbS = mk(ph, "sb", "tbS", [32, 512], F32)
                kr1, Tkr1 = mk(ph, "sb", "kr1", [32, 512], F32)
                kr2, Tkr2 = mk(ph, "sb", "kr2", [32, 512], F32)
                krb, Tkrb = mk(ph, "sb", "krb", [32, 512], BF16)
                kro, Tkro = mk(ph, "sb", "kro", [128, 4, 32], F32)
                sq, Tsq = mk(ph, "sb", "sq", [64, 512], F32)
                ss2, Tss2 = mk(ph, "sb", "ss2", [64, 2], F32)
                nf_b = [mk(ph, "sb", "nf%d" % i, [64, 512], F32) for i in range(2)]
                zsb_b = [mk(ph, "sb", "zsb%d" % i, [64, 512], F32) for i in range(2)]
                ksb_b = [mk(ph, "sb", "ksb%d" % i, [64, 256], F32) for i in range(2)]
                gsb_b = [mk(ph, "sb", "gsb%d" % i, [64, 512], F32) for i in range(2)]
                lgt_b = [mk(ph, "sb", "lgt%d" % i, [64, 256], F32) for i in range(2)]
                vb_b = [mk(ph, "sb", "vb%d" % i, [64, 512], BF16) for i in range(2)]
                sgo_b = [mk(ph, "sb", "sgo%d" % i, [64, 512], F32) for i in range(2)]
                ol_b = [mk(ph, "sb", "ol%d" % i, [64, 512], F32) for i in range(2)]
                nb_b = [mk(ph, "sb", "nb%d" % i, [64, 512], BF16) for i in range(2)]
                latT, TlatT = mk(ph, "sb", "latT", [128, 4, 512], BF16)
                eb, Teb = mk(ph, "sb", "eb", [64, 4, 64], F32)
                enb, Tenb = mk(ph, "sb", "enb", [64, 4, 64], F32)
                ec, Tec = mk(ph, "sb", "ec", [64, 256], F32)
                qt, Tqt = mk(ph, "sb", "qt", [64, 4, 64], BF16)
                kt, Tkt = mk(ph, "sb", "kt", [64, 4, 64], BF16)
                kh, Tkh = mk(ph, "sb", "kh", [64, 256], BF16)
                sgt, Tsgt = mk(ph, "sb", "sgt", [64, 512], F32)
                qhat, Tqhat = mk(ph, "sb", "qhat", [64, 4, 512], BF16)
                At, TAt = mk(ph, "sb", "At", [64, 4, 64], BF16)
                sout, Tsout = mk(ph, "sb", "sout", [64, 516], F32)
                SsS, TSsS = mk(ph, "sb", "SsS", [64, 512], F32)
                SbS, TSbS = mk(ph, "sb", "SbS", [64, 512], BF16)

                ldc(win[:], w_in[l].rearrange("(kc p) n -> p kc n", p=128), "a0", [Twin])
                ldc(wa2[:], w_a2[l], "a1", [Twa2])
                if l == 0:
                    for l_ in range(2):
                        fw.dma(pool, woB[l_][:, :], w_o[l_], "wc", writes=[Twc])
                        fw.dma(pool, wdB[l_][:, :], w_dn[l_], "wc", writes=[Twc])
                        for hm in range(2):
                            fw.dma(pool, wguB[l_][hm * 11:(hm + 1) * 11], w_gu[l_, hm * 11:(hm + 1) * 11].rearrange("m p kc c -> m p (kc c)"), "wc", writes=[Twc])
                ld(ba2[:], b_a2[l, 0:1, :].partition_broadcast(64), "a2", [Tba2])
                ld(gat[:], g_attn[l, 0:1, :].partition_broadcast(128), "a3", [Tgat])
                ld(gct[:], g_cat[l, 0:1, :].partition_broadcast(64), "a4", [Tgct])
                V(lambda e: e.memset(Sst[:], 0.0), [], [TSst])
                V(lambda e: e.memset(Sbf[:], 0.0), [], [TSbf])
                V(lambda e: e.memset(Bc[:], 0.0), [], [TBc])
                V(lambda e: e.memset(eBc[:], 1.0), [], [TeBc])

                deferred_ag = []
                bg_budget = [0]
                stage_banks = [4, 5, 6, 3]
                stage_i = [0]

                def stage():
                    b = stage_banks[stage_i[0] % len(stage_banks)]
                    stage_i[0] += 1
                    return pb[b]

                def load_x(tidx):
                    row0_, T_, PS_, CL_, is_s_ = TILES[tidx]
                    src_ = x_in if l == 0 else xs
                    ld(xt[0:PS_, 0:T_ // PS_, :], src_[row0_:row0_ + T_, :].rearrange("(s p) d -> p s d", p=PS_), "a5", [Txt], [Tdram])

                pTv = pT[:].rearrange("p (c t) -> p c t", c=8)

                def norm_a(ti, s_):
                    PS_ = TILES[ti][2]
                    hb, Thb = hb_b[s_ % 2]
                    A(lambda e: e.activation(out=junk[0:PS_, :], in_=xt[0:PS_, s_, :], func=AF.Square, accum_out=ss[0:PS_, 0:1]), [Txt], [Tjunk, Tss])
                    rstd_inplace(ss[0:PS_, 0:1], Tss, 1.0 / D)
                    V(lambda e: e.scalar_tensor_tensor(out=hb[0:PS_, :], in0=xt[0:PS_, s_, :], scalar=ss[0:PS_, 0:1], in1=gat[0:PS_, :], op0=ALU.mult, op1=ALU.mult), [Txt, Tss, Tgat], [Thb])

                def norm_b(ti, s_):
                    PS_ = TILES[ti][2]
                    hb, Thb = hb_b[s_ % 2]
                    hTn, ThTn = hT_b[ti % 2]
                    for c_ in range(8):
                        P(lambda e: e.transpose(out=pTv[:, c_, 0:PS_], in_=hb[0:PS_, c_ * 128:(c_ + 1) * 128], identity=idb[0:PS_, 0:PS_]), [Thb, Tidb], [TpT])
                    A(lambda e: e.activation(out=hTn[:, :, s_ * PS_:(s_ + 1) * PS_], in_=pTv[:, :, 0:PS_], func=AF.Copy), [TpT], [ThTn])

                load_x(0)
                for s_ in range(TILES[0][1] // TILES[0][2]):
                    norm_a(0, s_)
                    norm_b(0, s_)
                load_x(1)
                for tidx, (row0, T, PS, CL, is_s) in enumerate(TILES):
                    hT, ThT = hT_b[tidx % 2]
                    nhooks = {}
                    if tidx + 1 < len(TILES):
                        nsub_n = TILES[tidx + 1][1] // TILES[tidx + 1][2]
                        for s_ in range(nsub_n):
                            nhooks.setdefault(1 + s_, []).append(lambda s_=s_, ti=tidx + 1: norm_a(ti, s_))
                            nhooks.setdefault(2 + s_, []).append(lambda s_=s_, ti=tidx + 1: norm_b(ti, s_))
                        if tidx + 2 < len(TILES):
                            nhooks.setdefault(nsub_n, []).append(lambda ti=tidx + 2: load_x(ti))
                    nsub = T // PS
                    nch = T // CL
                    SstX, TSstX, SbfX, TSbfX = (SsS, TSsS, SbS, TSbS) if is_s else (Sst, TSst, Sbf, TSbf)
                    ld(tbC[:, 0:T], tabLC[0:32, row0:row0 + T], "a6", [TtbC], [Tdram])
                    ld(tbS[:, 0:T], tabLS[0:32, row0:row0 + T], "a7", [TtbS], [Tdram])
                    def fm_proj(col0, M):
                        bank, Tb = stage()
                        for kc in range(8):
                            P(lambda e: e.matmul(bank[0:M, 0:T], lhsT=win[:, kc, col0:col0 + M], rhs=hT[:, kc, 0:T], start=(kc == 0), stop=(kc == 7)), [Twin, ThT], [Tb])
                        return bank, Tb

                    for h in range(4):
                        bank, Tb = fm_proj(OGQ + h * 64, 64)
                        V(lambda e: e.tensor_scalar(out=gqT[:, h, 0:T], in0=bank[0:64, 0:T], scalar1=0.125, scalar2=None, op0=ALU.mult), [Tb], [TgqT])
                        bank, Tb = fm_proj(OGK + h * 64, 64)
                        V(lambda e: e.tensor_copy(out=gkT[:, h, 0:T], in_=bank[0:64, 0:T]), [Tb], [TgkT])
                    bank, Tb = fm_proj(OGA, 16)
                    A(lambda e: e.activation(out=gaT[:, 0:T], in_=bank[0:16, 0:T], func=AF.Copy), [Tb], [TgaT])
                    bank, Tb = fm_proj(OKR, 32)
                    V(lambda e: e.tensor_tensor(out=kr1[:, 0:T], in0=bank[0:32, 0:T], in1=tbC[:, 0:T], op=ALU.mult), [Tb, TtbC], [Tkr1])
                    bank, Tb = fm_proj(OKRP, 32)
                    V(lambda e: e.tensor_tensor(out=kr2[:, 0:T], in0=bank[0:32, 0:T], in1=tbS[:, 0:T], op=ALU.mult), [Tb, TtbS], [Tkr2])
                    V(lambda e: e.tensor_tensor(out=kr1[:, 0:T], in0=kr1[:, 0:T], in1=kr2[:, 0:T], op=ALU.add), [Tkr1, Tkr2], [Tkr1])
                    V(lambda e: e.tensor_copy(out=krb[:, 0:T], in_=kr1[:, 0:T]), [Tkr1], [Tkrb])
                    if is_s:
                        stq(latS[512:544, 0:T], krb[:, 0:T], "a8", [Tkrb], [Tdram])
                    else:
                        stq(latL[row0 // 512][512:544, 0:T], krb[:, 0:T], "a8", [Tkrb], [TlatLa[row0 // 512]])
                    bank, Tb = stage()
                    for s in range(nsub):
                        P(lambda e: e.transpose(out=bank[0:PS, s * 32:(s + 1) * 32], in_=kr1[0:32, s * PS:(s + 1) * PS], identity=idf[0:32, 0:32]), [Tkr1, Tidf], [Tb])
                    V(lambda e: e.tensor_copy(out=kro[0:PS, 0:nsub, :], in_=bank[0:PS, 0:nsub * 32].rearrange("p (s d) -> p s d", d=32)), [Tb], [Tkro])
                    stq(kr_out[l, row0:row0 + T, :].rearrange("(s p) d -> p s d", p=PS), kro[0:PS, 0:nsub, :], "a9", [Tkro])

                    b0, Tb0 = pb[0]
                    b1, Tb1 = pb[1]
                    b2, Tb2 = pb[2]
                    b3, Tb3 = pb[3]
                    b4, Tb4 = pb[4]
                    b5, Tb5 = pb[5]
                    b6, Tb6 = pb[6]
                    pbT3 = b6[0:64, 0:256].rearrange("p (h t) -> p h t", h=4)

                    def PGmm(c, g):
                        c0 = c * CL
                        bi, ncols, wc0 = [(0, 512, OCQ), (1, 256, OGK), (2, 512, OGV), (3, 512, OGG)][g]
                        bank, Tb = pb[bi]
                        for kc in range(8):
                            P(lambda e: e.matmul(bank[0:CL, 0:ncols], lhsT=hT[:, kc, c0:c0 + CL], rhs=win[:, kc, wc0:wc0 + ncols], start=(kc == 0), stop=(kc == 7)), [ThT, Twin], [Tb])
                        if g == 1:
                            P(lambda e: e.matmul(b1[0:CL, 256:512], lhsT=gaT[0:16, c0:c0 + CL], rhs=wa2[0:16, :], start=True, stop=True), [TgaT, Twa2], [Tb1])

                    def PGev(c, g):
                        par = c % 2
                        if g == 0:
                            A(lambda e: e.activation(out=zsb_b[par][0][0:CL, :], in_=b0[0:CL, :], func=AF.Copy), [Tb0], [zsb_b[par][1]])
                        elif g == 1:
                            V(lambda e: e.tensor_copy(out=ksb_b[par][0][0:CL, :], in_=b1[0:CL, 0:256]), [Tb1], [ksb_b[par][1]])
                            V(lambda e: e.tensor_tensor(out=lgt_b[par][0][0:CL, :], in0=b1[0:CL, 256:512], in1=ba2[0:CL, :], op=ALU.add), [Tb1, Tba2], [lgt_b[par][1]])
                        elif g == 2:
                            A(lambda e: e.activation(out=vb_b[par][0][0:CL, :], in_=b2[0:CL, :], func=AF.Copy), [Tb2], [vb_b[par][1]])
                        else:
                            A(lambda e: e.activation(out=gsb_b[par][0][0:CL, :], in_=b3[0:CL, :], func=AF.Copy), [Tb3], [gsb_b[par][1]])

                    def LATa(c):
                        c0 = c * CL
                        par = c % 2
                        zsb, Tzsb = zsb_b[par]
                        nfc, Tnfc = nf_b[par]
                        A(lambda e: e.activation(out=sq[0:CL, :], in_=zsb[0:CL, :], func=AF.Square), [Tzsb], [Tsq])
                        V(lambda e: e.tensor_reduce(out=ss2[0:CL, :], in_=sq[0:CL, :].rearrange("p (g d) -> p g d", g=2), axis=AX.X, op=ALU.add), [Tsq], [Tss2])
                        rstd_inplace(ss2[0:CL, :], Tss2, 1.0 / 256)
                        V(lambda e: e.tensor_tensor(out=nfc[0:CL, :].rearrange("p (g d) -> p g d", g=2), in0=zsb[0:CL, :].rearrange("p (g d) -> p g d", g=2), in1=ss2[0:CL, :].unsqueeze(2).broadcast_to([CL, 2, 256]), op=ALU.mult), [Tzsb, Tss2], [Tnfc])
                        V(lambda e: e.tensor_tensor(out=nfc[0:CL, :], in0=nfc[0:CL, :], in1=gct[0:CL, :], op=ALU.mult), [Tnfc, Tgct], [Tnfc])
                        nb, Tnb = nb_b[par]
                        A(lambda e: e.activation(out=nb[0:CL, :], in_=nfc[0:CL, :], func=AF.Copy), [Tnfc], [Tnb])
                        stq(ckv_out[l, row0 + c0:row0 + c0 + CL, :], nfc[0:CL, 256:512], "a12_%d" % par, [Tnfc])

                    def LATb(c):
                        c0 = c * CL
                        nb, Tnb = nb_b[c % 2]
                        pTl = pT[:, 0:512].rearrange("p (b t) -> p b t", b=4)
                        for b_ in range(4):
                            P(lambda e: e.transpose(out=pTl[:, b_, 0:CL], in_=nb[0:CL, b_ * 128:(b_ + 1) * 128], identity=idb[0:CL, 0:CL]), [Tnb, Tidb], [TpT])
                        V(lambda e: e.tensor_copy(out=latT[:, :, c0:c0 + CL], in_=pTl[:, :, 0:CL]), [TpT], [TlatT])

                    def SILU(c):
                        c0 = c * CL
                        par = c % 2
                        gsb, Tgsb = gsb_b[par]
                        sgc_, Tsgc_ = sgo_b[par]
                        A(lambda e: e.activation(out=sgt[0:CL, :], in_=gsb[0:CL, :], func=AF.Exp, scale=-1.0), [Tgsb], [Tsgt])
                        A(lambda e: e.activation(out=sgt[0:CL, :], in_=sgt[0:CL, :], func=AF.Ln, bias=1.0), [Tsgt], [Tsgt])
                        A(lambda e: e.activation(out=sgt[0:CL, :], in_=sgt[0:CL, :], func=AF.Exp, scale=-1.0), [Tsgt], [Tsgt])
                        V(lambda e: e.tensor_tensor(out=sgc_[0:CL, :], in0=gsb[0:CL, :], in1=sgt[0:CL, :], op=ALU.mult), [Tgsb, Tsgt], [Tsgc_])
                        stq(sgg[row0 + c0:row0 + c0 + CL, :], sgc_[0:CL, :], "a14_%d" % par, [Tsgc_], [Tdram])

                    def G1a(c):
                        par = c % 2
                        lgt, Tlgt = lgt_b[par]
                        A(lambda e: e.activation(out=lgt[0:CL, :], in_=lgt[0:CL, :], func=AF.Exp, scale=-1.0), [Tlgt], [Tlgt])
                        A(lambda e: e.activation(out=lgt[0:CL, :], in_=lgt[0:CL, :], func=AF.Ln, bias=1.0), [Tlgt], [Tlgt])
                        V(lambda e: e.tensor_scalar(out=lgt[0:CL, :], in0=lgt[0:CL, :], scalar1=-1.0 / 16.0, scalar2=None, op0=ALU.mult), [Tlgt], [Tlgt])

                    def G1b(c):
                        par = c % 2
                        lgt, Tlgt = lgt_b[par]
                        if is_s:
                            ld(SstX[:].rearrange("k (h v) -> k h v", h=4), sgla[l, c].rearrange("h k v -> k h v"), "a10", [TSstX])
                            V(lambda e: e.tensor_copy(out=SbfX[:], in_=SstX[:]), [TSstX], [TSbfX])
                        for h in range(4):
                            P(lambda e: e.matmul(pbT3[:, h, 0:CL], lhsT=lgt[0:CL, h * 64:(h + 1) * 64], rhs=tlt[0:CL, 0:CL], start=True, stop=True), [Tlgt, Ttlt], [Tb6])
                        P(lambda e: e.matmul(b6[0:CL, 256:512], lhsT=tut[0:CL, 0:CL], rhs=lgt[0:CL, :], start=True, stop=True), [Tlgt, Ttut], [Tb6])

                    def G2(c):
                        c0 = c * CL
                        par = c % 2
                        ksb, Tksb = ksb_b[par]
                        A(lambda e: e.activation(out=eb[:, :, 0:CL], in_=pbT3[:, :, 0:CL], func=AF.Exp), [Tb6], [Teb])
                        A(lambda e: e.activation(out=enb[:, :, 0:CL], in_=pbT3[:, :, 0:CL], func=AF.Exp, scale=-1.0), [Tb6], [Tenb])
                        A(lambda e: e.activation(out=ec[0:CL, :], in_=b6[0:CL, 256:512], func=AF.Exp), [Tb6], [Tec])
                        V(lambda e: e.tensor_tensor(out=qt[:, :, 0:CL], in0=gqT[:, :, c0:c0 + CL], in1=eb[:, :, 0:CL], op=ALU.mult), [TgqT, Teb], [Tqt])
                        V(lambda e: e.tensor_tensor(out=kt[:, :, 0:CL], in0=gkT[:, :, c0:c0 + CL], in1=enb[:, :, 0:CL], op=ALU.mult), [TgkT, Tenb], [Tkt])
                        if not is_s:
                            V(lambda e: e.tensor_tensor(out=Bc[:, :], in0=Bc[:, :], in1=pbT3[:, :, CL - 1], op=ALU.add), [TBc, Tb6], [TBc])
                        for h in range(4):
                            P(lambda e: e.matmul(b5[0:CL, h * 128:(h + 1) * 128], lhsT=qt[:, h, 0:CL], rhs=SbfX[:, h * 128:(h + 1) * 128], start=(h == 0), stop=False, skip_group_check=True), [Tqt, TSbfX], [Tb5])
                        pA3 = b4[0:64, 0:256].rearrange("p (h t) -> p h t", h=4)
                        for h in range(4):
                            P(lambda e: e.matmul(pA3[0:CL, h, 0:CL], lhsT=kt[:, h, 0:CL], rhs=qt[:, h, 0:CL], start=True, stop=True), [Tkt, Tqt], [Tb4])
                        V(lambda e: e.tensor_tensor(out=kh[0:CL, :], in0=ksb[0:CL, :], in1=ec[0:CL, :], op=ALU.mult), [Tksb, Tec], [Tkh])
                        if not is_s:
                            V(lambda e: e.tensor_tensor(out=qhat[:, :, c0:c0 + CL], in0=qt[:, :, 0:CL], in1=eBc[:, :].unsqueeze(2).broadcast_to([64, 4, CL]), op=ALU.mult), [Tqt, TeBc], [Tqhat])
                            A(lambda e: e.activation(out=eBc[:, :], in_=Bc[:, :], func=AF.Exp), [TBc], [TeBc])

                    def G3(c):
                        c0 = c * CL
                        par = c % 2
                        vb, Tvb = vb_b[par]
                        olc_, Tolc_ = ol_b[par]
                        pA3 = b4[0:64, 0:256].rearrange("p (h t) -> p h t", h=4)
                        V(lambda e: e.tensor_tensor(out=At[0:CL, :, 0:CL], in0=pA3[0:CL, :, 0:CL], in1=tlt[0:CL, 0:CL].unsqueeze(1).broadcast_to([CL, 4, CL]), op=ALU.mult), [Tb4, Ttlt], [TAt])
                        for h in range(4):
                            P(lambda e: e.matmul(b5[0:CL, h * 128:(h + 1) * 128], lhsT=At[0:CL, h, 0:CL], rhs=vb[0:CL, h * 128:(h + 1) * 128], start=False, stop=(h == 3), skip_group_check=True), [TAt, Tvb], [Tb5])
                        for h in range(4):
                            P(lambda e: e.matmul(b6[0:64, h * 128:(h + 1) * 128], lhsT=kh[0:CL, h * 64:(h + 1) * 64], rhs=vb[0:CL, h * 128:(h + 1) * 128], start=True, stop=True), [Tkh, Tvb], [Tb6])
                        A(lambda e: e.activation(out=olc_[0:CL, :], in_=b5[0:CL, :], func=AF.Copy), [Tb5], [Tolc_])
                        stq(oloc[row0 + c0:row0 + c0 + CL, :], olc_[0:CL, :], "a13_%d" % par, [Tolc_], [Tdram])

                    def G4(c):
                        for h in range(4):
                            V(lambda e: e.scalar_tensor_tensor(out=SstX[:, h * 128:(h + 1) * 128], in0=SstX[:, h * 128:(h + 1) * 128], scalar=eb[:, h, CL - 1:CL], in1=b6[0:64, h * 128:(h + 1) * 128], op0=ALU.mult, op1=ALU.add), [TSstX, Teb, Tb6], [TSstX])
                        if is_s:
                            stq(gla_out[l, 1 + c], SstX[:], "a11", [TSstX])
                        else:
                            V(lambda e: e.tensor_copy(out=SbfX[:], in_=SstX[:]), [TSstX], [TSbfX])

                    for g in range(4):
                        PGmm(0, g)
                        PGev(0, g)
                    G1a(0)
                    for c in range(nch):
                        nxt = c + 1 < nch
                        G1b(c)
                        if nxt:
                            PGmm(c + 1, 0)
                            PGmm(c + 1, 1)
                            PGmm(c + 1, 2)
                        G2(c)
                        if c >= 1:
                            LATa(c - 1)
                            SILU(c - 1)
                        if nxt:
                            PGev(c + 1, 0)
                            PGev(c + 1, 1)
                            G1a(c + 1)
                            PGev(c + 1, 2)
                            PGmm(c + 1, 3)
                        if c >= 2:
                            LATb(c - 2)
                        G3(c)
                        if nxt:
                            PGev(c + 1, 3)
                        G4(c)
                        if l == 0 and bg_budget[0] > 0:
                            bg_run(2)
                            bg_budget[0] -= 2
                        for fn_ in nhooks.pop(c, []):
                            fn_()
                    LATa(nch - 1)
                    SILU(nch - 1)
                    if nch >= 2:
                        LATb(nch - 2)
                    LATb(nch - 1)
                    for k_ in sorted(nhooks):
                        for fn_ in nhooks[k_]:
                            fn_()

                    if is_s:
                        stq(latS[0:512, 0:T].rearrange("(b p) t -> p b t", p=128), latT[:, :, 0:T], "a15", [TlatT], [Tdram])
                    else:
                        stq(latL[row0 // 512][0:512, 0:T].rearrange("(b p) t -> p b t", p=128), latT[:, :, 0:T], "a15", [TlatT], [TlatLb[row0 // 512]])
                        stq(qhT[:, :, row0:row0 + T], qhat[:, :, 0:T], "a16", [Tqhat], [Tdram])
                        ti_ = row0 // 512
                        deferred_ag.append((latL[ti_], latG[ti_], [TlatLa[ti_], TlatLb[ti_]], [TlatG[ti_]]))
                    if not is_s and row0 + T == NTOK:
                        V(lambda e: e.tensor_copy(out=sout[:, 0:512], in_=Sst[:]), [TSst], [Tsout])
                        V(lambda e: e.tensor_copy(out=sout[:, 512:516], in_=eBc[:, :]), [TeBc], [Tsout])
                        stq(glaLa[:, :], sout[:], "a17", [Tsout], [TglaL])
                        deferred_ag.append((glaLa, glaGa, [TglaL], [TglaG]))
                fw.barrier_all()

            fw.barrier_all()
            for args_ in deferred_ag:
                allgather(*args_)
            if l == 0:
                bg_run(len(bg_steps))
            if l == 0:
                bg_run(max(0, bg_budget[0]))
            fw.barrier_all()
            if KSTOP == "A":
                return nc

            if KSTOP == "AG":
                return nc
            with ExitStack() as ph:
                KT, TKT = mk(ph, "sb", "KT", [96, SEQ], BF16)
                Vg, TVg = mk(ph, "sb", "Vg", [128, 128, 128], BF16)
                wuq, Twuq = mk(ph, "sb", "wuq", [128, 2, 2 * 96], BF16)
                wuk, Twuk = mk(ph, "sb", "wuk", [128, 2, 96], BF16)
                wuv, Twuv = mk(ph, "sb", "wuv", [128, 2, 64], BF16)
                wuq2, Twuq2 = mk(ph, "sb", "wuq2", [128, 2, 2 * 96], BF16)
                wuk2, Twuk2 = mk(ph, "sb", "wuk2", [128, 2, 96], BF16)
                wuv2, Twuv2 = mk(ph, "sb", "wuv2", [128, 2, 64], BF16)
                lt_bufs = [mk(ph, "sb", "ltb%d" % i, [128, 2, 512], BF16) for i in range(4)]
                cq_bufs = [mk(ph, "sb", "cqb%d" % i, [128, 2, 512], BF16) for i in range(2)]
                tC_bufs = [mk(ph, "sb", "tC%d" % i, [96, 512], F32) for i in range(2)]
                tS_bufs = [mk(ph, "sb", "tS%d" % i, [96, 512], F32) for i in range(2)]
                q1, Tq1 = mk(ph, "sb", "q1", [96, 512], F32)
                q2, Tq2 = mk(ph, "sb", "q2", [96, 512], F32)
                qb_bufs = [mk(ph, "sb", "qbb%d" % i, [96, 512], BF16) for i in range(2)]
                pt_bufs = [mk(ph, "sb", "ptb%d" % i, [128, 512], BF16) for i in range(4)]
                of_b = [mk(ph, "sb", "of%d" % i, [128, 512], F32) for i in range(2)]
                orc, Torc = mk(ph, "sb", "orc", [64, 512], F32)
                ob_b = [mk(ph, "sb", "ob%d" % i, [64, 512], BF16) for i in range(2)]
                ckt, Tckt = mk(ph, "sb", "ckt", [128, 16, 256], BF16)
                krp, Tkrp = mk(ph, "sb", "krp", [128, 17, 96], BF16)
                cT, TcT = mk(ph, "sb", "cT", [128, 2, 2080], BF16)
                cqS, TcqS = mk(ph, "sb", "cqS", [128, 2, NS], BF16)
                krS, TkrS = mk(ph, "sb", "krS", [32, NS], BF16)
                KTs, TKTs = mk(ph, "sb", "KTs", [96, 2080], BF16)
                Vs, TVs = mk(ph, "sb", "Vs", [128, 17, 128], BF16)
                pall_bufs = [mk(ph, "sb", "pall%d" % i, [128, 544], BF16) for i in range(2)]
                tCs, TtCs = mk(ph, "sb", "tCs", [96, NS], F32)
                tSs, TtSs = mk(ph, "sb", "tSs", [96, NS], F32)

                G(lambda e: e.memset(Vg[:, :, 64:128], 1.0), [], [TVg])
                G(lambda e: e.memset(Vs[:, :, 64:128], 1.0), [], [TVs])
                G(lambda e: e.memset(wuk[:], 0.0), [], [Twuk])
                G(lambda e: e.memset(wuk2[:], 0.0), [], [Twuk2])
                G(lambda e: e.memset(krp[:], 0.0), [], [Tkrp])
                for of_, Tof_ in of_b:
                    G(lambda e: e.memset(of_[:], 0.0), [], [Tof_])

                def make_q(wq_tile, Twq, wcol, cq_ap, Tcq, tCa, tSa, Ttabs, N, qb, Tqb):
                    br, Tbr = pb[4]
                    for kc in range(2):
                        P(lambda e: e.matmul(br[0:96, 0:N], lhsT=wq_tile[:, kc, wcol:wcol + 96], rhs=cq_ap(kc), start=(kc == 0), stop=(kc == 1)), [Twq, Tcq], [Tbr])
                    V(lambda e: e.tensor_copy(out=qb[0:64, 0:N], in_=br[0:64, 0:N]), [Tbr], [Tqb])
                    V(lambda e: e.tensor_tensor(out=q1[64:96, 0:N], in0=br[64:96, 0:N], in1=tCa, op=ALU.mult), [Tbr] + Ttabs, [Tq1])
                    for kc in range(2):
                        P(lambda e: e.matmul(br[0:96, 0:N], lhsT=wq_tile[:, kc, wcol + 96:wcol + 192], rhs=cq_ap(kc), start=(kc == 0), stop=(kc == 1)), [Twq, Tcq], [Tbr])
                    V(lambda e: e.tensor_tensor(out=q2[64:96, 0:N], in0=br[64:96, 0:N], in1=tSa, op=ALU.mult), [Tbr] + Ttabs, [Tq2])
                    V(lambda e: e.tensor_tensor(out=qb[64:96, 0:N], in0=q1[64:96, 0:N], in1=q2[64:96, 0:N], op=ALU.add), [Tq1, Tq2], [Tqb])

                def finish_o1(obank, Tobank, N, par):
                    of, Tof = of_b[par]
                    V(lambda e: e.tensor_copy(out=of[0:64, 0:N], in_=obank[0:64, 0:N]), [Tobank], [Tof])
                    V(lambda e: e.reciprocal(out=of[64:128, 0:N], in_=obank[64:128, 0:N]), [Tobank], [Tof])

                def finish_o2(N, par, dst_dram, Tdst=None):
                    of, Tof = of_b[par]
                    ob, Tob = ob_b[par]
                    bs, Tbs = pb[5]
                    P(lambda e: e.matmul(bs[0:64, 0:N], lhsT=smat[64:128, 0:64], rhs=of[64:128, 0:N], start=True, stop=True), [Tsmat, Tof], [Tbs])
                    V(lambda e: e.tensor_copy(out=orc[:, 0:N], in_=bs[0:64, 0:N]), [Tbs], [Torc])
                    V(lambda e: e.tensor_tensor(out=ob[:, 0:N], in0=of[0:64, 0:N], in1=orc[:, 0:N], op=ALU.mult), [Tof, Torc], [Tob])
                    stq(dst_dram, ob[:, 0:N], "b9_%d" % par, [Tob], [Tdst if Tdst is not None else Tdram])

                def finish_o(obank, Tobank, N, dst_dram, Tdst=None, par=0):
                    finish_o1(obank, Tobank, N, par)
                    finish_o2(N, par, dst_dram, Tdst)

                for hh in range(2):
                    ldc(wuq[:], w_uq_loc[l, :, hh].rearrange("(kc p) v d -> p kc (v d)", p=128), "b0", [Twuq])
                    ldc(wuk[:, :, 0:64], w_uk_loc[l, :, hh, :].rearrange("(kc p) d -> p kc d", p=128), "b1", [Twuk])
                    ldc(wuv[:], w_uv_loc[l, :, hh, :].rearrange("(kc p) d -> p kc d", p=128), "b2", [Twuv])
                    for t in range(SEQ // 512):
                        rk, ti = t // 8, t % 8
                        ld(KT[64:96, t * 512:(t + 1) * 512], latG[ti][rk * 544 + 512:rk * 544 + 544, :], "b3", [TKT], [TlatG[ti]])
                    for t in range(SEQ // 512):
                        rk, ti = t // 8, t % 8
                        ltb, Tltb = lt_bufs[t % 4]
                        ld(ltb[:], latG[ti][rk * 544 + 256:rk * 544 + 512, :].rearrange("(kc p) t -> p kc t", p=128), "b4_%d" % (t % 4), [Tltb], [TlatG[ti]])
                        bk, Tbk = pb[t % 2]
                        for kc in range(2):
                            P(lambda e: e.matmul(bk[0:64, :], lhsT=wuk[:, kc, 0:64], rhs=ltb[:, kc, :], start=(kc == 0), stop=(kc == 1)), [Twuk, Tltb], [Tbk])
                        A(lambda e: e.activation(out=KT[0:64, t * 512:(t + 1) * 512], in_=bk[0:64, :], func=AF.Copy), [Tbk], [TKT])
                        bv, Tbv = pb[2 + t % 2]
                        for j in range(4):
                            for kc in range(2):
                                P(lambda e: e.matmul(bv[:, j * 64:(j + 1) * 64], lhsT=ltb[:, kc, j * 128:(j + 1) * 128], rhs=wuv[:, kc, :], start=(kc == 0), stop=(kc == 1)), [Tltb, Twuv], [Tbv])
                        V(lambda e: e.tensor_copy(out=Vg[:, t * 4:(t + 1) * 4, 0:64], in_=bv[:, 0:256].rearrange("p (j d) -> p j d", j=4)), [Tbv], [TVg])
                    NQ = SEQ // 512
                    units = [(qi, kti) for qi in range(NQ) for kti in range(4 * qi + 4)]
                    LOOK = 2

                    def q_loads(qi):
                        rk, ti = qi // 8, qi % 8
                        cqb, Tcqb = cq_bufs[qi % 2]
                        tCb, TtCb = tC_bufs[qi % 2]
                        tSb, TtSb = tS_bufs[qi % 2]
                        ld(cqb[:], latG[ti][rk * 544:rk * 544 + 256, :].rearrange("(kc p) t -> p kc t", p=128), "b5_%d" % (qi % 2), [Tcqb], [TlatG[ti]])
                        ld(tCb[64:96, :], tabGC[64:96, qi * 512:(qi + 1) * 512], "b6_%d" % (qi % 2), [TtCb], [TtabG[qi // 2]])
                        ld(tSb[64:96, :], tabGS[64:96, qi * 512:(qi + 1) * 512], "b7_%d" % (qi % 2), [TtSb], [TtabG[qi // 2]])

                    def q_make(qi):
                        cqb, Tcqb = cq_bufs[qi % 2]
                        tCb, TtCb = tC_bufs[qi % 2]
                        tSb, TtSb = tS_bufs[qi % 2]
                        qb, Tqb = qb_bufs[qi % 2]
                        make_q(wuq, Twuq, 0, lambda kc: cqb[:, kc, :], Tcqb, tCb[64:96, :], tSb[64:96, :], [TtCb, TtSb], 512, qb, Tqb)

                    s_banks = [pb[2], pb[3], pb[6]]

                    def emit_qk(ui):
                        qi, kti = units[ui]
                        d = kti - 4 * qi
                        cs = 0 if d < 0 else d * 128
                        qb, Tqb = qb_bufs[qi % 2]
                        sbank, Tsbank = s_banks[ui % 3]
                        P(lambda e: e.matmul(sbank[:, cs:512], lhsT=KT[0:96, kti * 128:(kti + 1) * 128], rhs=qb[0:96, cs:512], start=True, stop=True), [TKT, Tqb], [Tsbank])

                    def emit_exp_pv(ui):
                        qi, kti = units[ui]
                        nkt = 4 * qi + 4
                        d = kti - 4 * qi
                        cs = 0 if d < 0 else d * 128
                        sbank, Tsbank = s_banks[ui % 3]
                        ptb, Tptb = pt_bufs[ui % 4]
                        obank, Tobank = pb[qi % 2]
                        A(lambda e: e.activation(out=ptb[:, cs:512], in_=sbank[:, cs:512], func=AF.Exp, scale=ATTN_SCALE), [Tsbank], [Tptb])
                        if d >= 0:
                            V(lambda e: e.memset(ptb[64:128, cs:cs + 64], 0.0), [], [Tptb])
                        P(lambda e: e.matmul(obank[:, cs:512], lhsT=Vg[:, kti, :], rhs=ptb[:, cs:512], start=(kti == 0), stop=(kti == nkt - 1), skip_group_check=True), [TVg, Tptb], [Tobank])

                    def do_finish1(fq):
                        finish_o1(pb[fq % 2][0], pb[fq % 2][1], 512, fq % 2)

                    def do_finish2(fq):
                        finish_o2(512, fq % 2, oL[fq // 8][hh * 64:(hh + 1) * 64, (fq % 8) * 512:(fq % 8 + 1) * 512], ToL[fq // 8])
                        if hh == 1 and fq % 8 == 7:
                            allgather(oL[fq // 8], oG[fq // 8], [ToL[fq // 8]], [ToG[fq // 8]])

                    if l == 0 and hh == 1:
                        bg_run(len(bg_steps))
                    q_loads(0)
                    q_loads(1)
                    q_make(0)
                    for ui in range(min(LOOK, len(units))):
                        emit_qk(ui)
                    pending = []
                    for ui, (qi, kti) in enumerate(units):
                        if kti == 0:
                            if qi + 2 < NQ:
                                q_loads(qi + 2)
                            if qi + 1 < NQ:
                                q_make(qi + 1)
                        if ui + LOOK < len(units):
                            emit_qk(ui + LOOK)
                        emit_exp_pv(ui)
                        if l == 0 and hh == 0 and ui % 2 == 0:
                            bg_run(1)
                        if kti == 4 * qi + 3:
                            pending.append((ui + 2, 0, qi))
                            pending.append((ui + 12, 1, qi))
                            pending.sort()
                        while pending and pending[0][0] <= ui:
                            _, kind, fq = pending.pop(0)
                            (do_finish1 if kind == 0 else do_finish2)(fq)
                    for _, kind, fq in sorted(pending):
                        (do_finish1 if kind == 0 else do_finish2)(fq)

                fw._need(fw.pool, ("cc", fw.dma_sems["cc"][1]))
                ld(cqS[:], latS[0:256, :].rearrange("(kc p) t -> p kc t", p=128), "s0", [TcqS], [Tdram])
                ld(tCs[64:96, :], tabLC[64:96, NTOK:NROW], "s1", [TtCs], [Tdram])
                ld(tSs[64:96, :], tabLS[64:96, NTOK:NROW], "s2", [TtSs], [Tdram])
                for s in range(2):
                    ldc(ckt[:], cckv[l, s].rearrange("(t p) d -> p t d", p=128), "s3", [Tckt])
                    ldc(krp[:, 0:16, 64:96], ckr[l, s].rearrange("(t p) d -> p t d", p=128), "s4", [Tkrp])
                    ld(cT[:, :, PAST:PAST + 32], latS[256:512, s * 32:(s + 1) * 32].rearrange("(kc p) t -> p kc t", p=128), "s5", [TcT], [Tdram])
                    ld(krS[:, 0:32], latS[512:544, s * 32:(s + 1) * 32], "s6", [TkrS], [Tdram])
                    pTc = pT[:].rearrange("p (b t) -> p b t", b=8)
                    for t4 in range(4):
                        for tt_ in range(4):
                            for kc in range(2):
                                P(lambda e: e.transpose(out=pTc[:, tt_ * 2 + kc, :], in_=ckt[:, t4 * 4 + tt_, kc * 128:(kc + 1) * 128], identity=idb[:]), [Tckt, Tidb], [TpT])
                        for kc in range(2):
                            V(lambda e: e.tensor_copy(out=cT[:, kc, t4 * 512:(t4 + 1) * 512].rearrange("p (t c) -> p t c", t=4), in_=pTc[:, kc::2, :]), [TpT], [TcT])
                    for h in range(8):
                        (swq, Tswq, swk, Tswk, swv, Tswv) = (wuq, Twuq, wuk, Twuk, wuv, Twuv) if h % 2 == 0 else (wuq2, Twuq2, wuk2, Twuk2, wuv2, Twuv2)
                        ldc(swq[:], w_uq_all[l, :, h].rearrange("(kc p) v d -> p kc (v d)", p=128), "b0_%d" % (h % 2), [Tswq])
                        ldc(swk[:, :, 0:64], w_uk_all[l, :, h, :].rearrange("(kc p) d -> p kc d", p=128), "b1_%d" % (h % 2), [Tswk])
                        ldc(swv[:], w_uv_all[l, :, h, :].rearrange("(kc p) d -> p kc d", p=128), "b2_%d" % (h % 2), [Tswv])
                        for t4 in range(4):
                            bk, Tbk = pb[t4 % 2]
                            for j in range(4):
                                tix = t4 * 4 + j
                                for kc in range(2):
                                    P(lambda e: e.matmul(bk[0:96, j * 128:(j + 1) * 128], lhsT=swk[:, kc, 0:96], rhs=cT[:, kc, tix * 128:(tix + 1) * 128], start=(kc == 0), stop=False, skip_group_check=True), [Tswk, TcT], [Tbk])
                                P(lambda e: e.matmul(bk[0:96, j * 128:(j + 1) * 128], lhsT=krp[:, tix, :], rhs=idb[:], start=False, stop=True, skip_group_check=True), [Tkrp, Tidb], [Tbk])
                            A(lambda e: e.activation(out=KTs[:, t4 * 512:(t4 + 1) * 512], in_=bk[0:96, :], func=AF.Copy), [Tbk], [TKTs])
                        bk, Tbk = pb[0]
                        for kc in range(2):
                            P(lambda e: e.matmul(bk[0:64, 0:32], lhsT=swk[:, kc, 0:64], rhs=cT[:, kc, PAST:PAST + 32], start=(kc == 0), stop=(kc == 1)), [Tswk, TcT], [Tbk])
                        A(lambda e: e.activation(out=KTs[0:64, PAST:PAST + 32], in_=bk[0:64, 0:32], func=AF.Copy), [Tbk], [TKTs])
                        fw.dma(sp, KTs[64:96, PAST:PAST + 32], latS[512:544, s * 32:(s + 1) * 32], "s7", reads=[Tdram], writes=[TKTs])
                        for t4 in range(5):
                            bv, Tbv = pb[2 + t4 % 2]
                            nt = 4 if t4 < 4 else 1
                            for j in range(nt):
                                tix = t4 * 4 + j
                                kp = 128 if tix < 16 else 32
                                for kc in range(2):
                                    P(lambda e: e.matmul(bv[0:kp, j * 64:(j + 1) * 64], lhsT=cT[:, kc, tix * 128:tix * 128 + kp], rhs=swv[:, kc, :], start=(kc == 0), stop=(kc == 1)), [TcT, Tswv], [Tbv])
                            if t4 < 4:
                                V(lambda e: e.tensor_copy(out=Vs[:, t4 * 4:(t4 + 1) * 4, 0:64], in_=bv[:, 0:256].rearrange("p (j d) -> p j d", j=4)), [Tbv], [TVs])
                            else:
                                V(lambda e: e.tensor_copy(out=Vs[0:32, 16, 0:64], in_=bv[0:32, 0:64]), [Tbv], [TVs])
                        qb, Tqb = qb_bufs[h % 2]
                        make_q(swq, Tswq, 0, lambda kc: cqS[:, kc, s * 32:(s + 1) * 32], TcqS, tCs[64:96, s * 32:(s + 1) * 32], tSs[64:96, s * 32:(s + 1) * 32], [TtCs, TtSs], 32, qb, Tqb)
                        obank, Tobank = pb[h % 2]
                        sA, TsA = pb[2]
                        sB, TsB = pb[3]
                        pall, Tpall = pall_bufs[h % 2]
                        for kti in range(16):
                            P(lambda e: e.matmul(sA[:, kti * 32:(kti + 1) * 32], lhsT=KTs[0:96, kti * 128:(kti + 1) * 128], rhs=qb[0:96, 0:32], start=True, stop=True, skip_group_check=True), [TKTs, Tqb], [TsA])
                        P(lambda e: e.matmul(sB[0:32, 0:32], lhsT=KTs[0:96, PAST:PAST + 32], rhs=qb[0:96, 0:32], start=True, stop=True), [TKTs, Tqb], [TsB])
                        A(lambda e: e.activation(out=pall[:, 0:512], in_=sA[:, :], func=AF.Exp, scale=ATTN_SCALE), [TsA], [Tpall])
                        A(lambda e: e.activation(out=pall[0:32, 512:544], in_=sB[0:32, 0:32], func=AF.Exp, scale=ATTN_SCALE), [TsB], [Tpall])
                        for kti in range(17):
                            kp = 128 if kti < 16 else 32
                            P(lambda e: e.matmul(obank[:, 0:32], lhsT=Vs[0:kp, kti, :], rhs=pall[0:kp, kti * 32:(kti + 1) * 32], start=(kti == 0), stop=(kti == 16)), [TVs, Tpall], [Tobank])
                        finish_o(obank, Tobank, 32, oS[h * 64:(h + 1) * 64, s * 32:(s + 1) * 32])
                fw.barrier_all()
            if l == 0:
                tg.close()

            if KSTOP == "B":
                return nc

            with ExitStack() as ph0:
                gg_, Tgg = mk(ph0, "sb", "ggl", [64, 4, 516], F32)
                Rs, TRs = mk(ph0, "sb", "Rs", [64, 512], F32)
                cf, Tcf = mk(ph0, "sb", "cf", [64, 4], F32)
                se, Tse = mk(ph0, "sb", "se", [64, 512], F32)
                ld(gg_[:], glaGa.rearrange("(j k) c -> k j c", j=4), "c14", [Tgg], [TglaG])
                V(lambda e: e.memset(Rs[:], 0.0), [], [TRs])
                for j in range(4):
                    V(lambda e: e.tensor_scalar(out=cf[:], in0=gg_[:, j, 512:516], scalar1=-1.0, scalar2=None, op0=ALU.add), [Tgg], [Tcf])
                    V(lambda e: e.tensor_scalar(out=cf[:], in0=cf[:], scalar1=sel[0:64, 4 + j:5 + j], scalar2=None, op0=ALU.mult), [Tcf, Tsel], [Tcf])
                    V(lambda e: e.tensor_scalar(out=cf[:], in0=cf[:], scalar1=1.0, scalar2=None, op0=ALU.add), [Tcf], [Tcf])
                    V(lambda e: e.tensor_tensor(out=Rs[:].rearrange("k (h v) -> k h v", h=4), in0=Rs[:].rearrange("k (h v) -> k h v", h=4), in1=cf[:, :].unsqueeze(2).broadcast_to([64, 4, 128]), op=ALU.mult), [TRs, Tcf], [TRs])
                    V(lambda e: e.scalar_tensor_tensor(out=Rs[:], in0=gg_[:, j, 0:512], scalar=sel[0:64, 4 + j:5 + j], in1=Rs[:], op0=ALU.mult, op1=ALU.add), [Tgg, Tsel, TRs], [TRs])
                V(lambda e: e.tensor_copy(out=Rb[:], in_=Rs[:]), [TRs], [TRb])
                V(lambda e: e.tensor_tensor(out=se[:].rearrange("k (h v) -> k h v", h=4), in0=Rs[:].rearrange("k (h v) -> k h v", h=4), in1=eBc[:, :].unsqueeze(2).broadcast_to([64, 4, 128]), op=ALU.mult), [TRs, TeBc], [Tse])
                V(lambda e: e.tensor_tensor(out=se[:], in0=se[:], in1=Sst[:], op=ALU.add), [Tse, TSst], [Tse])
                stq(gla_out[l, 0], se[:], "c15", [Tse])
                fw.barrier_all()

            with ExitStack() as ph:
                wo, Two = mk(ph, "sb", "wo", [128, 8, D], BF16)
                wd, Twd = mk(ph, "sb", "wd", [128, 22, D], BF16)
                NWG = 4
                wg_bufs = [mk(ph, "sb", "wgb%d" % i, [128, 8, 256], BF16) for i in range(NWG)]
                gft, Tgft = mk(ph, "sb", "gft", [128, D], F32)
                gon, Tgon = mk(ph, "sb", "gon", [128, 128], F32)
                xt_b = [mk(ph, "sb", "xtc%d" % i, [128, 4, D], F32)[0] for i in range(2)]
                Txt_b = [[Trk("xt%d_%d" % (i, s_)) for s_ in range(4)] for i in range(2)]
                cand_bufs = [mk(ph, "sb", "cand%d" % i, [128, 4, 512], BF16) for i in range(2)]
                cat_b = [mk(ph, "sb", "catT%d" % i, [128, 8, 512], BF16)[0] for i in range(2)]
                Tcat_b = [[Trk("cat%d_%d" % (i, s_)) for s_ in range(4)] for i in range(2)]
                hid, Thid = mk(ph, "sb", "hid", [128, 22, 512], BF16)
                olc, Tolc = mk(ph, "sb", "olc", [128, 512], F32)
                sgc, Tsgc = mk(ph, "sb", "sgc", [128, 512], F32)
                qhc_b = [mk(ph, "sb", "qhc%d" % i, [64, 4, 512], BF16) for i in range(2)]
                sq, Tsq = mk(ph, "sb", "sqc", [128, 512], F32)
                ss4, Tss4 = mk(ph, "sb", "ss4", [128, 4], F32)
                ogb_b = [mk(ph, "sb", "ogb%d" % i, [128, 512], BF16) for i in range(2)]
                junk, Tjunk = mk(ph, "sb", "junkc", [128, D], BF16)
                ss, Tss = mk(ph, "sb", "ssc", [128, 1], F32)
                hb_b = [mk(ph, "sb", "hbc%d" % i, [128, D], BF16) for i in range(2)]
                sa_bufs = [mk(ph, "sb", "sa%d" % i, [128, 512], F32) for i in range(2)]
                yt_b = [mk(ph, "sb", "yt%d" % i, [128, D], F32) for i in range(2)]

                ld(wo[:], woB[l].rearrange("(kc p) n -> p kc n", p=128), "c10", [Two], [Twc])
                ld(wd[:], wdB[l].rearrange("(kc p) n -> p kc n", p=128), "c11", [Twd], [Twc])
                ld(gft[:], g_ffn[l, 0:1, :].partition_broadcast(128), "c12", [Tgft])
                ld(gon[:], g_on[l, 0:1, :].partition_broadcast(128), "c13", [Tgon])

                wg_ctr = [0]
                NT = len(TILES)

                def P_load(i):
                    row0, T, PS, CL, is_s = TILES[i]
                    nsub = T // PS
                    src = x_in if l == 0 else xs
                    xt = xt_b[i % 2]
                    ld(xt[0:PS, 0:nsub, :], src[row0:row0 + T, :].rearrange("(s p) d -> p s d", p=PS), "c16_%d" % (i % 2), Txt_b[i % 2][0:nsub], [Tdram])
                    if not is_s:
                        qhc, Tqhc = qhc_b[i % 2]
                        ld(qhc[:, :, 0:T], qhT[:, :, row0:row0 + T], "c19_%d" % (i % 2), [Tqhc], [Tdram])

                def P_select(i):
                    row0, T, PS, CL, is_s = TILES[i]
                    nsub = T // PS
                    catT = cat_b[i % 2]
                    Tc = Tcat_b[i % 2][0:nsub]
                    if is_s:
                        ld(catT[:, 0:4, 0:T], oS.rearrange("(kc p) t -> p kc t", p=128), "c17", Tc, [Tdram])
                    else:
                        for j in range(4):
                            cand, Tcand = cand_bufs[j % 2]
                            ld(cand[:], oG[j][:, row0:row0 + T].rearrange("(kc p) t -> p kc t", p=128), "c18_%d" % (j % 2), [Tcand], [ToG[j]])
                            if j == 0:
                                V(lambda e: e.tensor_scalar(out=catT[:, 0:4, :], in0=cand[:], scalar1=sel[:, 0:1], scalar2=None, op0=ALU.mult), [Tcand, Tsel], Tc)
                            else:
                                V(lambda e: e.scalar_tensor_tensor(out=catT[:, 0:4, :], in0=cand[:], scalar=sel[:, j:j + 1], in1=catT[:, 0:4, :], op0=ALU.mult, op1=ALU.add), [Tcand, Tsel] + Tc, Tc)

                def P1(i, s):
                    row0, T, PS, CL, is_s = TILES[i]
                    r0 = row0 + s * PS
                    ogb, Togb = ogb_b[s % 2]
                    ld(olc[0:PS, :], oloc[r0:r0 + PS, :], "c20", [Tolc], [Tdram])
                    ld(sgc[0:PS, :], sgg[r0:r0 + PS, :], "c21", [Tsgc], [Tdram])
                    if not is_s:
                        qhc, Tqhc = qhc_b[i % 2]
                        bc_, Tbc_ = pb[6]
                        for h in range(4):
                            P(lambda e: e.matmul(bc_[0:PS, h * 128:(h + 1) * 128], lhsT=qhc[:, h, s * PS:(s + 1) * PS], rhs=Rb[:, h * 128:(h + 1) * 128], start=True, stop=True), [Tqhc, TRb], [Tbc_])
                        V(lambda e: e.tensor_tensor(out=olc[0:PS, :], in0=olc[0:PS, :], in1=bc_[0:PS, :], op=ALU.add), [Tolc, Tbc_], [Tolc])
                    V(lambda e: e.tensor_tensor(out=sq[0:PS, :], in0=olc[0:PS, :], in1=olc[0:PS, :], op=ALU.mult), [Tolc], [Tsq])
                    V(lambda e: e.tensor_reduce(out=ss4[0:PS, :], in_=sq[0:PS, :].rearrange("p (h d) -> p h d", h=4), axis=AX.X, op=ALU.add), [Tsq], [Tss4])
                    rstd_inplace(ss4[0:PS, :], Tss4, 1.0 / 128)
                    V(lambda e: e.tensor_tensor(out=olc[0:PS, :].rearrange("p (h d) -> p h d", h=4), in0=olc[0:PS, :].rearrange("p (h d) -> p h d", h=4), in1=ss4[0:PS, :].unsqueeze(2).broadcast_to([PS, 4, 128]), op=ALU.mult), [Tolc, Tss4], [Tolc])
                    V(lambda e: e.tensor_tensor(out=sgc[0:PS, :].rearrange("p (h d) -> p h d", h=4), in0=sgc[0:PS, :].rearrange("p (h d) -> p h d", h=4), in1=gon[0:PS, :].unsqueeze(1).broadcast_to([PS, 4, 128]), op=ALU.mult), [Tsgc, Tgon], [Tsgc])
                    V(lambda e: e.tensor_tensor(out=ogb[0:PS, :], in0=olc[0:PS, :], in1=sgc[0:PS, :], op=ALU.mult), [Tolc, Tsgc], [Togb])

                def P2(i, s):
                    row0, T, PS, CL, is_s = TILES[i]
                    ogb, Togb = ogb_b[s % 2]
                    catT = cat_b[i % 2]
                    pTl = pT[:, 0:512].rearrange("p (b t) -> p b t", b=4)
                    for b_ in range(4):
                        P(lambda e: e.transpose(out=pTl[:, b_, 0:PS], in_=ogb[0:PS, b_ * 128:(b_ + 1) * 128], identity=idb[0:PS, 0:PS]), [Togb, Tidb], [TpT])
                    V(lambda e: e.tensor_copy(out=catT[:, 4:8, s * PS:(s + 1) * PS], in_=pTl[:, :, 0:PS]), [TpT], [Tcat_b[i % 2][s]])

                def M(i, hooks):
                    row0, T, PS, CL, is_s = TILES[i]
                    nsub = T // PS
                    xt = xt_b[i % 2]
                    Txs = Txt_b[i % 2]
                    catT = cat_b[i % 2]
                    Tcs = Tcat_b[i % 2]
                    h2T = catT
                    pTv = pT[:].rearrange("p (c t) -> p c t", c=8)

                    def WO(s):
                        for n in range(2):
                            bo, Tbo = pb[(2 * s + n) % 4]
                            for kc in range(8):
                                P(lambda e: e.matmul(bo[0:PS, :], lhsT=catT[:, kc, s * PS:(s + 1) * PS], rhs=wo[:, kc, n * 512:(n + 1) * 512], start=(kc == 0), stop=(kc == 7)), [Tcs[s], Two], [Tbo])
                            V(lambda e: e.tensor_tensor(out=xt[0:PS, s, n * 512:(n + 1) * 512], in0=xt[0:PS, s, n * 512:(n + 1) * 512], in1=bo[0:PS, :], op=ALU.add), [Txs[s], Tbo], [Txs[s]])

                    def N_(s):
                        hb, Thb = hb_b[s % 2]
                        A(lambda e: e.activation(out=junk[0:PS, :], in_=xt[0:PS, s, :], func=AF.Square, accum_out=ss[0:PS, 0:1]), [Txs[s]], [Tjunk, Tss])
                        rstd_inplace(ss[0:PS, 0:1], Tss, 1.0 / D)
                        V(lambda e: e.scalar_tensor_tensor(out=hb[0:PS, :], in0=xt[0:PS, s, :], scalar=ss[0:PS, 0:1], in1=gft[0:PS, :], op0=ALU.mult, op1=ALU.mult), [Txs[s], Tss, Tgft], [Thb])

                    def T_(s):
                        hb, Thb = hb_b[s % 2]
                        for c in range(8):
                            P(lambda e: e.transpose(out=pTv[:, c, 0:PS], in_=hb[0:PS, c * 128:(c + 1) * 128], identity=idb[0:PS, 0:PS]), [Thb, Tidb], [TpT])
                        A(lambda e: e.activation(out=h2T[:, :, s * PS:(s + 1) * PS], in_=pTv[:, :, 0:PS], func=AF.Copy), [TpT], [Tcs[s]])

                    for s in range(nsub):
                        WO(s)
                        if s >= 1:
                            N_(s - 1)
                        if s >= 2:
                            T_(s - 2)
                    N_(nsub - 1)
                    if nsub >= 2:
                        T_(nsub - 2)
                    T_(nsub - 1)
                    for m in range(22):
                        wgb, Twgb = wg_bufs[wg_ctr[0] % NWG]
                        fw.dma(pool, wgb[:], wguB[l][m].rearrange("p (kc c) -> p kc c", kc=8), "c22_%d" % (wg_ctr[0] % NWG), reads=[Twc], writes=[Twgb])
                        wg_ctr[0] += 1
                        ba, Tba = pb[4 + (m % 2)]
                        bu, Tbu = pb[2 * (m % 2)]
                        for kc in range(8):
                            P(lambda e: e.matmul(ba[:, 0:T], lhsT=wgb[:, kc, 0:128], rhs=h2T[:, kc, 0:T], start=(kc == 0), stop=(kc == 7)), [Twgb] + Tcs[0:nsub], [Tba])
                        for kc in range(8):
                            P(lambda e: e.matmul(bu[:, 0:T], lhsT=wgb[:, kc, 128:256], rhs=h2T[:, kc, 0:T], start=(kc == 0), stop=(kc == 7)), [Twgb] + Tcs[0:nsub], [Tbu])
                        sa, Tsa = sa_bufs[m % 2]
                        A(lambda e: e.activation(out=sa[:, 0:T], in_=ba[:, 0:T], func=AF.Silu), [Tba], [Tsa])
                        V(lambda e: e.tensor_tensor(out=hid[:, m, 0:T], in0=sa[:, 0:T], in1=bu[:, 0:T], op=ALU.mult), [Tsa, Tbu], [Thid])
                        for fn in hooks.get(m, []):
                            fn()
                    for s in range(nsub):
                        for n in range(2):
                            bo, Tbo = pb[1 + 2 * ((2 * s + n) % 2)]
                            for m in range(22):
                                P(lambda e: e.matmul(bo[0:PS, :], lhsT=hid[:, m, s * PS:(s + 1) * PS], rhs=wd[:, m, n * 512:(n + 1) * 512], start=(m == 0), stop=(m == 21)), [Thid, Twd], [Tbo])
                            V(lambda e: e.tensor_tensor(out=xt[0:PS, s, n * 512:(n + 1) * 512], in0=xt[0:PS, s, n * 512:(n + 1) * 512], in1=bo[0:PS, :], op=ALU.add), [Txs[s], Tbo], [Txs[s]])
                        if s == 0:
                            for fn in hooks.get(22, []):
                                fn()
                    if l == 0:
                        stq(xs[row0:row0 + T, :].rearrange("(s p) d -> p s d", p=PS), xt[0:PS, 0:nsub, :], "c23_%d" % (i % 2), Txs[0:nsub], [Tdram])
                    else:
                        for s in range(nsub):
                            yt, Tyt = yt_b[s % 2]
                            A(lambda e: e.activation(out=junk[0:PS, :], in_=xt[0:PS, s, :], func=AF.Square, accum_out=ss[0:PS, 0:1]), [Txs[s]], [Tjunk, Tss])
                            rstd_inplace(ss[0:PS, 0:1], Tss, 1.0 / D)
                            V(lambda e: e.scalar_tensor_tensor(out=yt[0:PS, :], in0=xt[0:PS, s, :], scalar=ss[0:PS, 0:1], in1=gfin[0:PS, :], op0=ALU.mult, op1=ALU.mult), [Txs[s], Tss, Tgfin], [Tyt])
                            stq(y_out[row0 + s * PS:row0 + (s + 1) * PS, :], yt[0:PS, :], "c24_%d" % (s % 2), [Tyt])

                P_load(0)
                P_select(0)
                for s in range(TILES[0][1] // TILES[0][2]):
                    P1(0, s)
                    P2(0, s)
                for i in range(NT):
                    hooks = {}
                    if i + 1 < NT:
                        nsub_n = TILES[i + 1][1] // TILES[i + 1][2]
                        hooks.setdefault(0, []).append(lambda i=i: P_load(i + 1))
                        hooks.setdefault(1, []).append(lambda i=i: P_select(i + 1))
                        for s in range(nsub_n):
                            hooks.setdefault(3 + 5 * s, []).append(lambda i=i, s=s: P1(i + 1, s))
                            hooks.setdefault(3 + 5 * s + 4, []).append(lambda i=i, s=s: P2(i + 1, s))
                    M(i, hooks)
                fw.barrier_all()

        fw.barrier_all()
        print("[kernel] instructions:", fw.n_instr, "semaphores:", fw.nsem)
    return nc


_NC_CACHE = {}


def _f32(a):
    return np.ascontiguousarray(np.asarray(a, dtype=np.float32))


def kernel(x_prompt, x_sample, cache_ckv, cache_krope, state_gla,
           g_attn, w_in, g_qn, w_uq, g_kvn, w_ukv, w_a2, b_a2, g_gla_on, w_o,
           g_ffn, w_gu, w_down, g_final):
    x_prompt = _f32(x_prompt); x_sample = _f32(x_sample)
    cache_ckv = _f32(cache_ckv); cache_krope = _f32(cache_krope); state_gla = _f32(state_gla)
    w_in = _f32(w_in); w_uq = _f32(w_uq); w_ukv = _f32(w_ukv); w_gu = _f32(w_gu)

    perm = np.concatenate([np.arange(16, 32), np.arange(0, 16)])
    w_in_x = np.concatenate([w_in, w_in[:, :, OKR:OKR + 32][:, :, perm]], axis=2)
    wq = w_uq.reshape(2, 256, 8, 96)
    wq_raw = wq
    wq_perm = np.concatenate([wq[..., :64], wq[..., 64:][..., perm]], axis=-1)
    w_uq_all = np.ascontiguousarray(np.stack([wq_raw, wq_perm], axis=3))
    wkv = w_ukv.reshape(2, 256, 8, 128)
    w_uk_all = np.ascontiguousarray(wkv[..., :64])
    w_uv_all = np.ascontiguousarray(wkv[..., 64:])
    wa = w_gu[:, :, :DFF].reshape(2, 8, 128, 22, 128)
    wu = w_gu[:, :, DFF:].reshape(2, 8, 128, 22, 128)
    w_gu_t = np.ascontiguousarray(np.concatenate([wa, wu], axis=-1).transpose(0, 3, 2, 1, 4))
    ident = np.eye(128, dtype=np.float32)
    jj, ii = np.meshgrid(np.arange(64), np.arange(64), indexing="ij")
    trilt = (jj <= ii).astype(np.float32)
    triut = (jj > ii).astype(np.float32)
    selmat = np.zeros((128, 64), np.float32)
    selmat[64 + np.arange(64), np.arange(64)] = 1.0
    half = 16
    inv = (10000.0 ** (-np.arange(half, dtype=np.float32) / half)).astype(np.float32)
    ropec = np.zeros((96, 2), np.float32)
    for r_ in range(96):
        ropec[r_, 0] = inv[r_ % 16]
        ropec[r_, 1] = -1.0 if (r_ % 32) < 16 else 1.0
    pos_g = np.arange(SEQ, dtype=np.float32)[None, :]
    g_cat = np.concatenate([_f32(g_qn), _f32(g_kvn)], axis=1)[:, None, :]

    shared = {
        "w_in": w_in_x, "w_uq_all": w_uq_all, "w_uk_all": w_uk_all, "w_uv_all": w_uv_all,
        "w_a2": _f32(w_a2), "b_a2": _f32(b_a2)[:, None, :], "g_attn": _f32(g_attn)[:, None, :],
        "g_ffn": _f32(g_ffn)[:, None, :], "g_final": _f32(g_final)[None, :], "g_cat": _f32(g_cat),
        "g_on": _f32(g_gla_on)[:, None, :], "w_o": _f32(w_o), "w_gu": w_gu_t, "w_dn": _f32(w_down),
        "ident": ident, "trilt": trilt, "triut": triut, "pos_g": pos_g, "ropec": ropec, "selmat": selmat,
    }
    in_maps = []
    for c in range(8):
        g, r = c // 4, c % 4
        xs_ = np.concatenate([x_prompt[g, r * NTOK:(r + 1) * NTOK], x_sample[2 * c:2 * c + 2].reshape(NS, D)], axis=0)
        selm = np.zeros((128, 8), np.float32)
        selm[:, r] = 1.0
        for j in range(4):
            selm[:, 4 + j] = 1.0 if j < r else 0.0
        pos_l = np.concatenate([np.arange(r * NTOK, (r + 1) * NTOK), PAST + np.arange(32), PAST + np.arange(32)]).astype(np.float32)[None, :]
        m = dict(shared)
        m.update({
            "x": np.ascontiguousarray(xs_),
            "cckv": np.ascontiguousarray(cache_ckv[:, 2 * c:2 * c + 2]),
            "ckr": np.ascontiguousarray(cache_krope[:, 2 * c:2 * c + 2]),
            "sgla": np.ascontiguousarray(state_gla[:, 2 * c:2 * c + 2]),
            "w_uq_loc": np.ascontiguousarray(w_uq_all[:, :, 2 * r:2 * r + 2]),
            "w_uk_loc": np.ascontiguousarray(w_uk_all[:, :, 2 * r:2 * r + 2]),
            "w_uv_loc": np.ascontiguousarray(w_uv_all[:, :, 2 * r:2 * r + 2]),
            "selm": selm, "pos_l": pos_l,
        })
        in_maps.append(m)

    if "nc" not in _NC_CACHE:
        _NC_CACHE["nc"] = build_program()
    res = run_bass_kernel_spmd(_NC_CACHE["nc"], in_maps, core_ids=list(range(8)))
    R = res.results

    y_p = np.zeros((2, SEQ, D), np.float32)
    y_s = np.zeros((16, 32, D), np.float32)
    ckv_p = np.zeros((2, 2, SEQ, 256), np.float32)
    kr_p = np.zeros((2, 2, SEQ, 32), np.float32)
    gla_p = np.zeros((2, 2, 4, 64, 128), np.float32)
    ckv_s = np.zeros((2, 16, 32, 256), np.float32)
    kr_s = np.zeros((2, 16, 32, 32), np.float32)
    gla_s = np.zeros((2, 16, 4, 64, 128), np.float32)
    for c in range(8):
        g, r = c // 4, c % 4
        o = R[c]
        y_p[g, r * NTOK:(r + 1) * NTOK] = o["y"][:NTOK]
        y_s[2 * c:2 * c + 2] = o["y"][NTOK:].reshape(2, 32, D)
        ckv_p[:, g, r * NTOK:(r + 1) * NTOK] = o["ckv_o"][:, :NTOK]
        kr_p[:, g, r * NTOK:(r + 1) * NTOK] = o["kr_o"][:, :NTOK]
        ckv_s[:, 2 * c:2 * c + 2] = o["ckv_o"][:, NTOK:].reshape(2, 2, 32, 256)
        kr_s[:, 2 * c:2 * c + 2] = o["kr_o"][:, NTOK:].reshape(2, 2, 32, 32)
        gl = o["gla_o"].reshape(2, 3, 64, 4, 128).transpose(0, 1, 3, 2, 4)
        if r == 3:
            gla_p[:, g] = gl[:, 0]
        gla_s[:, 2 * c] = gl[:, 1]
        gla_s[:, 2 * c + 1] = gl[:, 2]
    return (y_p, y_s, ckv_p, kr_p, gla_p, ckv_s, kr_s, gla_s)
```

```python
import math
from contextlib import ExitStack

import numpy as np
import concourse.bass as bass
import concourse.mybir as mybir
from concourse.bass_utils import run_bass_kernel_spmd

F32 = mybir.dt.float32
BF16 = mybir.dt.bfloat16
I32 = mybir.dt.int32
AF = mybir.ActivationFunctionType
ALU = mybir.AluOpType
AX = mybir.AxisListType

D = 1024
SEQ = 16384
NTOK = 4096
NS = 64
NROW = NTOK + NS
PAST = 2048
DFF = 2816
EPS = 1e-6
ATTN_SCALE = 96 ** -0.5
OCQ, OCKV, OKR, OGQ, OGK, OGV, OGG, OGA, OKRP = 0, 256, 512, 544, 800, 1056, 1568, 2080, 2096
WIN = 2128
TWO_PI = 2.0 * math.pi
C1 = 6.28125
C2 = TWO_PI - C1


class Trk:
    __slots__ = ("name", "w", "r")

    def __init__(self, name):
        self.name = name
        self.w = None
        self.r = []


class Eng:
    def __init__(self, fw, name, eng):
        self.name = name
        self.eng = eng
        self.sem = fw.new_sem("e_" + name)
        self.count = 0
        self.waited = {}


class FW:
    def __init__(self, nc, stack):
        self.nc = nc
        self.stack = stack
        self.nsem = 0
        self.pe = Eng(self, "pe", nc.tensor)
        self.dve = Eng(self, "dve", nc.vector)
        self.act = Eng(self, "act", nc.scalar)
        self.pool = Eng(self, "pool", nc.gpsimd)
        self.sp = Eng(self, "sp", nc.sync)
        self.engs = [self.pe, self.dve, self.act, self.pool, self.sp]
        self.dma_sems = {}
        self.n_instr = 0

    def new_sem(self, name):
        self.nsem += 1
        return self.stack.enter_context(self.nc.semaphore(name))

    def _need(self, E, dep):
        if dep is None:
            return
        key, val = dep
        if E.waited.get(key, 0) >= val:
            return
        sem = key.sem if isinstance(key, Eng) else self.dma_sems[key][0]
        E.eng.wait_ge(sem, val)
        E.waited[key] = val
        self.n_instr += 1

    def _deps(self, E, reads, writes, is_dma=False):
        reads = [t for t in reads if t.name != "dram"]
        writes = [t for t in writes if t.name != "dram"]
        for t in reads:
            if t.w is not None and (is_dma or not (t.w[0] is E and E is self.pe)):
                self._need(E, t.w)
        for t in writes:
            if t.w is not None and (is_dma or t.w[0] is not E):
                self._need(E, t.w)
            for r in t.r:
                if is_dma or r[0] is not E:
                    self._need(E, r)

    def _mark(self, stamp, reads, writes):
        reads = [t for t in reads if t.name != "dram"]
        writes = [t for t in writes if t.name != "dram"]
        for t in reads:
            t.r.append(stamp)
            if len(t.r) > 8:
                last = {}
                for k, v in t.r:
                    if last.get(k, 0) < v:
                        last[k] = v
                t.r = list(last.items())
        for t in writes:
            t.w = stamp
            t.r = []

    def op(self, E, fn, reads=(), writes=()):
        self._deps(E, reads, writes)
        ins = fn(E.eng)
        E.count += 1
        ins.then_inc(E.sem, 1)
        self._mark((E, E.count), reads, writes)
        self.n_instr += 1
        return ins

    def dma(self, Q, out, in_, key, reads=(), writes=()):
        if key not in self.dma_sems:
            self.dma_sems[key] = [self.new_sem("d_" + key), 0, 16]
        self._deps(Q, reads, writes, is_dma=True)
        ent = self.dma_sems[key]
        ins = Q.eng.dma_start(out=out, in_=in_)
        ent[1] += 16
        ins.then_inc(ent[0], 16)
        self._mark((key, ent[1]), reads, writes)
        self.n_instr += 1
        return ins

    def coll(self, fn, key, reads=(), writes=()):
        if key not in self.dma_sems:
            self.dma_sems[key] = [self.new_sem("c_" + key), 0, 1]
        Q = self.pool
        self._deps(Q, reads, writes, is_dma=True)
        ent = self.dma_sems[key]
        ins = fn(Q.eng)
        ent[1] += 1
        ins.then_inc(ent[0], 1)
        self._mark((key, ent[1]), reads, writes)
        self.n_instr += 1
        return ins

    def barrier_all(self):
        for E in self.engs:
            for P in self.engs:
                if P.count > 0 and P is not E:
                    self._need(E, (P, P.count))
            for key, ent in self.dma_sems.items():
                if ent[1] > 0:
                    self._need(E, (key, ent[1]))


import os
KSTOP = os.environ.get("KSTOP", "")


class _Stop(Exception):
    pass


def build_program():
    nc = bass.Bass("TRN2", target_bir_lowering=False)

    def din(name, shape, dt=F32):
        return nc.dram_tensor(name, list(shape), dt, kind="ExternalInput").ap()

    def dout(name, shape, dt=F32):
        return nc.dram_tensor(name, list(shape), dt, kind="ExternalOutput").ap()

    def dscr(name, shape, dt):
        return nc.dram_tensor(name, list(shape), dt)

    x_in = din("x", [NROW, D])
    cckv = din("cckv", [2, 2, PAST, 256])
    ckr = din("ckr", [2, 2, PAST, 32])
    sgla = din("sgla", [2, 2, 4, 64, 128])
    w_in = din("w_in", [2, D, WIN])
    w_uq_loc = din("w_uq_loc", [2, 256, 2, 2, 96])
    w_uq_all = din("w_uq_all", [2, 256, 8, 2, 96])
    w_uk_loc = din("w_uk_loc", [2, 256, 2, 64])
    w_uv_loc = din("w_uv_loc", [2, 256, 2, 64])
    w_uk_all = din("w_uk_all", [2, 256, 8, 64])
    w_uv_all = din("w_uv_all", [2, 256, 8, 64])
    w_a2 = din("w_a2", [2, 16, 256])
    b_a2 = din("b_a2", [2, 1, 256])
    g_attn = din("g_attn", [2, 1, D])
    g_ffn = din("g_ffn", [2, 1, D])
    g_final = din("g_final", [1, D])
    g_cat = din("g_cat", [2, 1, 512])
    g_on = din("g_on", [2, 1, 128])
    w_o = din("w_o", [2, D, D])
    w_gu = din("w_gu", [2, 22, 128, 8, 256])
    w_dn = din("w_dn", [2, DFF, D])
    ident = din("ident", [128, 128])
    trilt = din("trilt", [64, 64])
    triut = din("triut", [64, 64])
    selm = din("selm", [128, 8])
    pos_g = din("pos_g", [1, SEQ])
    pos_l = din("pos_l", [1, NROW])
    ropec = din("ropec", [96, 2])
    selmat = din("selmat", [128, 64])

    y_out = dout("y", [NROW, D])
    ckv_out = dout("ckv_o", [2, NROW, 256])
    kr_out = dout("kr_o", [2, NROW, 32])
    gla_out = dout("gla_o", [2, 3, 64, 512])

    xs = dscr("xs", [NROW, D], F32).ap()
    latL = [dscr("latL%d" % i, [544, 512], BF16).ap() for i in range(8)]
    latG = [dscr("latG%d" % i, [4 * 544, 512], BF16).ap() for i in range(8)]
    latS = dscr("latS", [544, NS], BF16).ap()
    glaL = dscr("glaL", [64, 516], F32)
    glaG = dscr("glaG", [256, 516], F32)
    oloc = dscr("oloc", [NROW, 512], F32).ap()
    qhT = dscr("qhT", [64, 4, NROW], BF16).ap()
    sgg = dscr("sgg", [NROW, 512], F32).ap()
    oL = [dscr("oL%d" % i, [128, NTOK], BF16).ap() for i in range(4)]
    oG = [dscr("oG%d" % i, [512, NTOK], BF16).ap() for i in range(4)]
    oS = dscr("oS", [512, NS], BF16).ap()
    tabGC = dscr("tabGC", [96, SEQ], F32).ap()
    tabGS = dscr("tabGS", [96, SEQ], F32).ap()
    tabLC = dscr("tabLC", [96, NROW], F32).ap()
    tabLS = dscr("tabLS", [96, NROW], F32).ap()
    glaLa, glaGa = glaL.ap(), glaG.ap()
    wguB = [dscr("wguB%d" % i, [22, 128, 2048], BF16).ap() for i in range(2)]
    woB = [dscr("woB%d" % i, [D, D], BF16).ap() for i in range(2)]
    wdB = [dscr("wdB%d" % i, [DFF, D], BF16).ap() for i in range(2)]

    GROUPS = [[0, 1, 2, 3], [4, 5, 6, 7]]

    with ExitStack() as top:
        fw = FW(nc, top)
        pe, dve, act, pool, sp = fw.pe, fw.dve, fw.act, fw.pool, fw.sp

        uniq = [0]

        def mk(stack, kind, name, shape, dt):
            f = nc.sbuf_tensor if kind == "sb" else nc.psum_tensor
            uniq[0] += 1
            return stack.enter_context(f("%s_%d" % (name, uniq[0]), list(shape), dt)), Trk(name)

        def V(fn, r=(), w=()):
            return fw.op(dve, fn, r, w)

        def A(fn, r=(), w=()):
            return fw.op(act, fn, r, w)

        def G(fn, r=(), w=()):
            return fw.op(pool, fn, r, w)

        def P(fn, r=(), w=()):
            return fw.op(pe, fn, r, w)

        def ld(out, in_, key, w, r=()):
            return fw.dma(sp, out, in_, key, reads=r, writes=w)

        def ldc(out, in_, key, w, r=()):
            return fw.dma(pool, out, in_, key, reads=r, writes=w)

        def stq(out, in_, key, r, w=()):
            return fw.dma(sp, out, in_, key, reads=r, writes=w)

        pb = []
        for i in range(7):
            pb.append(mk(top, "ps", "pb%d" % i, [128, 512], F32))
        pT, TpT = mk(top, "ps", "pbT", [128, 1024], BF16)

        idb, Tidb = mk(top, "sb", "idb", [128, 128], BF16)
        idf, Tidf = mk(top, "sb", "idf", [128, 128], F32)
        tlt, Ttlt = mk(top, "sb", "tlt", [64, 64], F32)
        tut, Ttut = mk(top, "sb", "tut", [64, 64], F32)
        sel, Tsel = mk(top, "sb", "sel", [128, 8], F32)
        smat, Tsmat = mk(top, "sb", "smat", [128, 64], F32)
        gfin, Tgfin = mk(top, "sb", "gfin", [128, D], F32)
        Sst, TSst = mk(top, "sb", "Sst", [64, 512], F32)
        Sbf, TSbf = mk(top, "sb", "Sbf", [64, 512], BF16)
        Bc, TBc = mk(top, "sb", "Bc", [64, 4], F32)
        eBc, TeBc = mk(top, "sb", "eBc", [64, 4], F32)
        Twc = Trk("wconv")
        TlatLa = [Trk("latLa%d" % i) for i in range(8)]
        TlatLb = [Trk("latLb%d" % i) for i in range(8)]
        TlatG = [Trk("latG%d" % i) for i in range(8)]
        ToL = [Trk("oL%d" % i) for i in range(4)]
        ToG = [Trk("oG%d" % i) for i in range(4)]
        TglaL, TglaG = Trk("glaL"), Trk("glaG")

        def allgather(src, dst, r, w):
            r = list(r) + [Twc]
            fw.coll(lambda e: e.collective_compute("AllGather", ALU.bypass, replica_groups=GROUPS, ins=[src.opt()], outs=[dst.opt()]), "cc", r, w)

        Rb, TRb = mk(top, "sb", "Rb", [64, 512], BF16)
        Tdram = Trk("dram")

        ld(idf[:], ident[:, :], "c0", [Tidf])
        ldc(idb[:], ident[:, :], "c1", [Tidb])
        ld(tlt[:], trilt[:, :], "c2", [Ttlt])
        ld(tut[:], triut[:, :], "c3", [Ttut])
        ld(sel[:], selm[:, :], "c4", [Tsel])
        ld(smat[:], selmat[:, :], "c5", [Tsmat])
        ld(gfin[:], g_final[0:1, :].partition_broadcast(128), "c6", [Tgfin])

        def rstd_inplace(ap, T_, inv_n):
            V(lambda e: e.tensor_scalar(out=ap, in0=ap, scalar1=inv_n, scalar2=EPS, op0=ALU.mult, op1=ALU.add), [T_], [T_])
            A(lambda e: e.activation(out=ap, in_=ap, func=AF.Ln), [T_], [T_])
            A(lambda e: e.activation(out=ap, in_=ap, func=AF.Exp, scale=-0.5), [T_], [T_])

        tg = ExitStack()
        TGW = 1024
        rc, Trc = mk(tg, "sb", "rc", [96, 2], F32)
        ptl, Tptl = mk(tg, "sb", "ptl", [96, TGW], F32)
        ang, Tang = mk(tg, "sb", "ang", [96, TGW], F32)
        aa, Taa = mk(tg, "sb", "aa", [96, TGW], F32)
        tt, Ttt = mk(tg, "sb", "tt", [96, TGW], F32)
        ki, Tki = mk(tg, "sb", "ki", [96, TGW], I32)
        mm, Tmm = mk(tg, "sb", "mm", [96, TGW], F32)
        res, Tres = mk(tg, "sb", "res", [96, TGW], F32)
        ld(rc[:], ropec[:, :], "p0", [Trc])

        TtabG = [Trk("tabG%d" % i) for i in range(SEQ // 1024)]

        def table_steps(pos_ap, ntot, dC, dS, chunk_trks=None):
            steps = []

            def wrap_and_sin(dst_dram, n, use_sign, wtrk=None):
                steps.append(lambda: V(lambda e: e.tensor_scalar(out=tt[:, 0:n], in0=mm[:, 0:n], scalar1=math.pi, scalar2=None, op0=ALU.is_gt), [Tmm], [Ttt]))
                steps.append(lambda: V(lambda e: e.scalar_tensor_tensor(out=mm[:, 0:n], in0=tt[:, 0:n], scalar=-TWO_PI, in1=mm[:, 0:n], op0=ALU.mult, op1=ALU.add), [Ttt, Tmm], [Tmm]))
                steps.append(lambda: V(lambda e: e.tensor_scalar(out=tt[:, 0:n], in0=mm[:, 0:n], scalar1=-math.pi, scalar2=None, op0=ALU.is_lt), [Tmm], [Ttt]))
                steps.append(lambda: V(lambda e: e.scalar_tensor_tensor(out=mm[:, 0:n], in0=tt[:, 0:n], scalar=TWO_PI, in1=mm[:, 0:n], op0=ALU.mult, op1=ALU.add), [Ttt, Tmm], [Tmm]))
                steps.append(lambda: V(lambda e: e.tensor_scalar(out=aa[:, 0:n], in0=mm[:, 0:n], scalar1=3.1415925, scalar2=-3.1415925, op0=ALU.min, op1=ALU.max), [Tmm], [Taa]))
                steps.append(lambda: A(lambda e: e.activation(out=res[:, 0:n], in_=aa[:, 0:n], func=AF.Sin), [Taa], [Tres]))
                if use_sign:
                    steps.append(lambda: V(lambda e: e.tensor_scalar(out=res[:, 0:n], in0=res[:, 0:n], scalar1=rc[:, 1:2], scalar2=None, op0=ALU.mult), [Tres, Trc], [Tres]))
                steps.append(lambda: stq(dst_dram, res[:, 0:n], "p2", [Tres], [wtrk if wtrk is not None else Tdram]))

            for c0 in range(0, ntot, TGW):
                n = min(TGW, ntot - c0)
                steps.append(lambda c0=c0, n=n: ld(ptl[:, 0:n], pos_ap[0:1, c0:c0 + n].partition_broadcast(96), "p1", [Tptl]))
                steps.append(lambda n=n: V(lambda e: e.tensor_scalar(out=ang[:, 0:n], in0=ptl[:, 0:n], scalar1=rc[:, 0:1], scalar2=None, op0=ALU.mult), [Tptl, Trc], [Tang]))
                steps.append(lambda n=n: V(lambda e: e.tensor_scalar(out=tt[:, 0:n], in0=ang[:, 0:n], scalar1=1.0 / TWO_PI, scalar2=None, op0=ALU.mult), [Tang], [Ttt]))
                steps.append(lambda n=n: V(lambda e: e.tensor_copy(out=ki[:, 0:n], in_=tt[:, 0:n]), [Ttt], [Tki]))
                steps.append(lambda n=n: V(lambda e: e.tensor_copy(out=tt[:, 0:n], in_=ki[:, 0:n]), [Tki], [Ttt]))
                steps.append(lambda n=n: V(lambda e: e.scalar_tensor_tensor(out=mm[:, 0:n], in0=tt[:, 0:n], scalar=-C1, in1=ang[:, 0:n], op0=ALU.mult, op1=ALU.add), [Ttt, Tang], [Tmm]))
                steps.append(lambda n=n: V(lambda e: e.scalar_tensor_tensor(out=mm[:, 0:n], in0=tt[:, 0:n], scalar=-C2, in1=mm[:, 0:n], op0=ALU.mult, op1=ALU.add), [Ttt, Tmm], [Tmm]))
                wtrk_ = chunk_trks[c0 // TGW] if chunk_trks is not None else None
                wrap_and_sin(dS[:, c0:c0 + n], n, True, wtrk_)
                steps.append(lambda n=n: V(lambda e: e.tensor_scalar(out=mm[:, 0:n], in0=mm[:, 0:n], scalar1=math.pi / 2, scalar2=None, op0=ALU.add), [Tmm], [Tmm]))
                wrap_and_sin(dC[:, c0:c0 + n], n, False, wtrk_)
            return steps

        for st_ in table_steps(pos_l, NROW, tabLC, tabLS):
            st_()
        fw.barrier_all()
        bg_steps = table_steps(pos_g, SEQ, tabGC, tabGS, TtabG)
        STEPS_PER_CHUNK = len(bg_steps) // (SEQ // TGW)

        def bg_run(k):
            for _ in range(k):
                if bg_steps:
                    bg_steps.pop(0)()

        if KSTOP == "pro":
            return nc

        TILES = [(t * 512, 512, 128, 64, False) for t in range(NTOK // 512)] + [(NTOK, NS, 64, 32, True)]

        for l in range(2):
            with ExitStack() as ph:
                win, Twin = mk(ph, "sb", "win", [128, 8, WIN], BF16)
                wa2, Twa2 = mk(ph, "sb", "wa2", [16, 256], BF16)
                ba2, Tba2 = mk(ph, "sb", "ba2", [64, 256], F32)
                gat, Tgat = mk(ph, "sb", "gat", [128, D], F32)
                gct, Tgct = mk(ph, "sb", "gct", [64, 512], F32)
                xt, Txt = mk(ph, "sb", "xt", [128, 4, D], F32)
                junk, Tjunk = mk(ph, "sb", "junk", [128, D], F32)
                ss, Tss = mk(ph, "sb", "ss", [128, 1], F32)
                hb_b = [mk(ph, "sb", "hb%d" % i, [128, D], BF16) for i in range(2)]
                hT_b = [mk(ph, "sb", "hT%d" % i, [128, 8, 512], BF16) for i in range(2)]
                gqT, TgqT = mk(ph, "sb", "gqT", [64, 4, 512], F32)
                gkT, TgkT = mk(ph, "sb", "gkT", [64, 4, 512], F32)
                gaT, TgaT = mk(ph, "sb", "gaT", [16, 512], BF16)
                tbC, TtbC = mk(ph, "sb", "tbC", [32, 512], F32)
                tbS, TtbS = mk(ph, "sb", "tbS", [32, 512], F32)
                kr1, Tkr1 = mk(ph, "sb", "kr1", [32, 512], F32)
                kr2, Tkr2 = mk(ph, "sb", "kr2", [32, 512], F32)
                krb, Tkrb = mk(ph, "sb", "krb", [32, 512], BF16)
                kro, Tkro = mk(ph, "sb", "kro", [128, 4, 32], F32)
                sq, Tsq = mk(ph, "sb", "sq", [64, 512], F32)
                ss2, Tss2 = mk(ph, "sb", "ss2", [64, 2], F32)
                nf_b = [mk(ph, "sb", "nf%d" % i, [64, 512], F32) for i in range(2)]
                zsb_b = [mk(ph, "sb", "zsb%d" % i, [64, 512], F32) for i in range(2)]
                ksb_b = [mk(ph, "sb", "ksb%d" % i, [64, 256], F32) for i in range(2)]
                gsb_b = [mk(ph, "sb", "gsb%d" % i, [64, 512], F32) for i in range(2)]
                lgt_b = [mk(ph, "sb", "lgt%d" % i, [64, 256], F32) for i in range(2)]
                vb_b = [mk(ph, "sb", "vb%d" % i, [64, 512], BF16) for i in range(2)]
                sgo_b = [mk(ph, "sb", "sgo%d" % i, [64, 512], F32) for i in range(2)]
                ol_b = [mk(ph, "sb", "ol%d" % i, [64, 512], F32) for i in range(2)]
                nb, Tnb = mk(ph, "sb", "nb", [64, 512], BF16)
                latT, TlatT = mk(ph, "sb", "latT", [128, 4, 512], BF16)
                eb, Teb = mk(ph, "sb", "eb", [64, 4, 64], F32)
                enb, Tenb = mk(ph, "sb", "enb", [64, 4, 64], F32)
                ec, Tec = mk(ph, "sb", "ec", [64, 256], F32)
                qt, Tqt = mk(ph, "sb", "qt", [64, 4, 64], BF16)
                kt, Tkt = mk(ph, "sb", "kt", [64, 4, 64], BF16)
                kh, Tkh = mk(ph, "sb", "kh", [64, 256], BF16)
                sgt, Tsgt = mk(ph, "sb", "sgt", [64, 512], F32)
                qhat, Tqhat = mk(ph, "sb", "qhat", [64, 4, 512], BF16)
                At, TAt = mk(ph, "sb", "At", [64, 4, 64], BF16)
                sout, Tsout = mk(ph, "sb", "sout", [64, 516], F32)
                SsS, TSsS = mk(ph, "sb", "SsS", [64, 512], F32)
                SbS, TSbS = mk(ph, "sb", "SbS", [64, 512], BF16)

                ldc(win[:], w_in[l].rearrange("(kc p) n -> p kc n", p=128), "a0", [Twin])
                ldc(wa2[:], w_a2[l], "a1", [Twa2])
                if l == 0:
                    for l_ in range(2):
                        fw.dma(pool, woB[l_][:, :], w_o[l_], "wc", writes=[Twc])
                        fw.dma(pool, wdB[l_][:, :], w_dn[l_], "wc", writes=[Twc])
                        for hm in range(2):
                            fw.dma(pool, wguB[l_][hm * 11:(hm + 1) * 11], w_gu[l_, hm * 11:(hm + 1) * 11].rearrange("m p kc c -> m p (kc c)"), "wc", writes=[Twc])
                ld(ba2[:], b_a2[l, 0:1, :].partition_broadcast(64), "a2", [Tba2])
                ld(gat[:], g_attn[l, 0:1, :].partition_broadcast(128), "a3", [Tgat])
                ld(gct[:], g_cat[l, 0:1, :].partition_broadcast(64), "a4", [Tgct])
                V(lambda e: e.memset(Sst[:], 0.0), [], [TSst])
                V(lambda e: e.memset(Sbf[:], 0.0), [], [TSbf])
                V(lambda e: e.memset(Bc[:], 0.0), [], [TBc])
                V(lambda e: e.memset(eBc[:], 1.0), [], [TeBc])

                deferred_ag = []
                bg_budget = [0]
                stage_banks = [4, 5, 6, 3]
                stage_i = [0]

                def stage():
                    b = stage_banks[stage_i[0] % len(stage_banks)]
                    stage_i[0] += 1
                    return pb[b]

                def load_x(tidx):
                    row0_, T_, PS_, CL_, is_s_ = TILES[tidx]
                    src_ = x_in if l == 0 else xs
                    ld(xt[0:PS_, 0:T_ // PS_, :], src_[row0_:row0_ + T_, :].rearrange("(s p) d -> p s d", p=PS_), "a5", [Txt], [Tdram])

                pTv = pT[:].rearrange("p (c t) -> p c t", c=8)

                def norm_a(ti, s_):
                    PS_ = TILES[ti][2]
                    hb, Thb = hb_b[s_ % 2]
                    A(lambda e: e.activation(out=junk[0:PS_, :], in_=xt[0:PS_, s_, :], func=AF.Square, accum_out=ss[0:PS_, 0:1]), [Txt], [Tjunk, Tss])
                    rstd_inplace(ss[0:PS_, 0:1], Tss, 1.0 / D)
                    V(lambda e: e.scalar_tensor_tensor(out=hb[0:PS_, :], in0=xt[0:PS_, s_, :], scalar=ss[0:PS_, 0:1], in1=gat[0:PS_, :], op0=ALU.mult, op1=ALU.mult), [Txt, Tss, Tgat], [Thb])

                def norm_b(ti, s_):
                    PS_ = TILES[ti][2]
                    hb, Thb = hb_b[s_ % 2]
                    hTn, ThTn = hT_b[ti % 2]
                    for c_ in range(8):
                        P(lambda e: e.transpose(out=pTv[:, c_, 0:PS_], in_=hb[0:PS_, c_ * 128:(c_ + 1) * 128], identity=idb[0:PS_, 0:PS_]), [Thb, Tidb], [TpT])
                    A(lambda e: e.activation(out=hTn[:, :, s_ * PS_:(s_ + 1) * PS_], in_=pTv[:, :, 0:PS_], func=AF.Copy), [TpT], [ThTn])

                load_x(0)
                for s_ in range(TILES[0][1] // TILES[0][2]):
                    norm_a(0, s_)
                    norm_b(0, s_)
                load_x(1)
                for tidx, (row0, T, PS, CL, is_s) in enumerate(TILES):
                    hT, ThT = hT_b[tidx % 2]
                    nhooks = {}
                    if tidx + 1 < len(TILES):
                        nsub_n = TILES[tidx + 1][1] // TILES[tidx + 1][2]
                        for s_ in range(nsub_n):
                            nhooks.setdefault(1 + s_, []).append(lambda s_=s_, ti=tidx + 1: norm_a(ti, s_))
                            nhooks.setdefault(2 + s_, []).append(lambda s_=s_, ti=tidx + 1: norm_b(ti, s_))
                        if tidx + 2 < len(TILES):
                            nhooks.setdefault(nsub_n, []).append(lambda ti=tidx + 2: load_x(ti))
                    nsub = T // PS
                    nch = T // CL
                    SstX, TSstX, SbfX, TSbfX = (SsS, TSsS, SbS, TSbS) if is_s else (Sst, TSst, Sbf, TSbf)
                    ld(tbC[:, 0:T], tabLC[0:32, row0:row0 + T], "a6", [TtbC], [Tdram])
                    ld(tbS[:, 0:T], tabLS[0:32, row0:row0 + T], "a7", [TtbS], [Tdram])
                    def fm_proj(col0, M):
                        bank, Tb = stage()
                        for kc in range(8):
                            P(lambda e: e.matmul(bank[0:M, 0:T], lhsT=win[:, kc, col0:col0 + M], rhs=hT[:, kc, 0:T], start=(kc == 0), stop=(kc == 7)), [Twin, ThT], [Tb])
                        return bank, Tb

                    for h in range(4):
                        bank, Tb = fm_proj(OGQ + h * 64, 64)
                        V(lambda e: e.tensor_scalar(out=gqT[:, h, 0:T], in0=bank[0:64, 0:T], scalar1=0.125, scalar2=None, op0=ALU.mult), [Tb], [TgqT])
                        bank, Tb = fm_proj(OGK + h * 64, 64)
                        V(lambda e: e.tensor_copy(out=gkT[:, h, 0:T], in_=bank[0:64, 0:T]), [Tb], [TgkT])
                    bank, Tb = fm_proj(OGA, 16)
                    A(lambda e: e.activation(out=gaT[:, 0:T], in_=bank[0:16, 0:T], func=AF.Copy), [Tb], [TgaT])
                    bank, Tb = fm_proj(OKR, 32)
                    V(lambda e: e.tensor_tensor(out=kr1[:, 0:T], in0=bank[0:32, 0:T], in1=tbC[:, 0:T], op=ALU.mult), [Tb, TtbC], [Tkr1])
                    bank, Tb = fm_proj(OKRP, 32)
                    V(lambda e: e.tensor_tensor(out=kr2[:, 0:T], in0=bank[0:32, 0:T], in1=tbS[:, 0:T], op=ALU.mult), [Tb, TtbS], [Tkr2])
                    V(lambda e: e.tensor_tensor(out=kr1[:, 0:T], in0=kr1[:, 0:T], in1=kr2[:, 0:T], op=ALU.add), [Tkr1, Tkr2], [Tkr1])
                    V(lambda e: e.tensor_copy(out=krb[:, 0:T], in_=kr1[:, 0:T]), [Tkr1], [Tkrb])
                    if is_s:
                        stq(latS[512:544, 0:T], krb[:, 0:T], "a8", [Tkrb], [Tdram])
                    else:
                        stq(latL[row0 // 512][512:544, 0:T], krb[:, 0:T], "a8", [Tkrb], [TlatLa[row0 // 512]])
                    bank, Tb = stage()
                    for s in range(nsub):
                        P(lambda e: e.transpose(out=bank[0:PS, s * 32:(s + 1) * 32], in_=kr1[0:32, s * PS:(s + 1) * PS], identity=idf[0:32, 0:32]), [Tkr1, Tidf], [Tb])
                    V(lambda e: e.tensor_copy(out=kro[0:PS, 0:nsub, :], in_=bank[0:PS, 0:nsub * 32].rearrange("p (s d) -> p s d", d=32)), [Tb], [Tkro])
                    stq(kr_out[l, row0:row0 + T, :].rearrange("(s p) d -> p s d", p=PS), kro[0:PS, 0:nsub, :], "a9", [Tkro])

                    b0, Tb0 = pb[0]
                    b1, Tb1 = pb[1]
                    b2, Tb2 = pb[2]
                    b3, Tb3 = pb[3]
                    b4, Tb4 = pb[4]
                    b5, Tb5 = pb[5]
                    b6, Tb6 = pb[6]
                    pbT3 = b6[0:64, 0:256].rearrange("p (h t) -> p h t", h=4)

                    def PGmm(c, g):
                        c0 = c * CL
                        bi, ncols, wc0 = [(0, 512, OCQ), (1, 256, OGK), (2, 512, OGV), (3, 512, OGG)][g]
                        bank, Tb = pb[bi]
                        for kc in range(8):
                            P(lambda e: e.matmul(bank[0:CL, 0:ncols], lhsT=hT[:, kc, c0:c0 + CL], rhs=win[:, kc, wc0:wc0 + ncols], start=(kc == 0), stop=(kc == 7)), [ThT, Twin], [Tb])
                        if g == 1:
                            P(lambda e: e.matmul(b1[0:CL, 256:512], lhsT=gaT[0:16, c0:c0 + CL], rhs=wa2[0:16, :], start=True, stop=True), [TgaT, Twa2], [Tb1])

                    def PGev(c, g):
                        par = c % 2
                        if g == 0:
                            A(lambda e: e.activation(out=zsb_b[par][0][0:CL, :], in_=b0[0:CL, :], func=AF.Copy), [Tb0], [zsb_b[par][1]])
                        elif g == 1:
                            V(lambda e: e.tensor_copy(out=ksb_b[par][0][0:CL, :], in_=b1[0:CL, 0:256]), [Tb1], [ksb_b[par][1]])
                            V(lambda e: e.tensor_tensor(out=lgt_b[par][0][0:CL, :], in0=b1[0:CL, 256:512], in1=ba2[0:CL, :], op=ALU.add), [Tb1, Tba2], [lgt_b[par][1]])
                        elif g == 2:
                            A(lambda e: e.activation(out=vb_b[par][0][0:CL, :], in_=b2[0:CL, :], func=AF.Copy), [Tb2], [vb_b[par][1]])
                        else:
                            A(lambda e: e.activation(out=gsb_b[par][0][0:CL, :], in_=b3[0:CL, :], func=AF.Copy), [Tb3], [gsb_b[par][1]])

                    def LATa(c):
                        c0 = c * CL
                        par = c % 2
                        zsb, Tzsb = zsb_b[par]
                        nfc, Tnfc = nf_b[par]
                        A(lambda e: e.activation(out=sq[0:CL, :], in_=zsb[0:CL, :], func=AF.Square), [Tzsb], [Tsq])
                        V(lambda e: e.tensor_reduce(out=ss2[0:CL, :], in_=sq[0:CL, :].rearrange("p (g d) -> p g d", g=2), axis=AX.X, op=ALU.add), [Tsq], [Tss2])
                        rstd_inplace(ss2[0:CL, :], Tss2, 1.0 / 256)
                        V(lambda e: e.tensor_tensor(out=nfc[0:CL, :].rearrange("p (g d) -> p g d", g=2), in0=zsb[0:CL, :].rearrange("p (g d) -> p g d", g=2), in1=ss2[0:CL, :].unsqueeze(2).broadcast_to([CL, 2, 256]), op=ALU.mult), [Tzsb, Tss2], [Tnfc])
                        V(lambda e: e.tensor_tensor(out=nfc[0:CL, :], in0=nfc[0:CL, :], in1=gct[0:CL, :], op=ALU.mult), [Tnfc, Tgct], [Tnfc])
                        A(lambda e: e.activation(out=nb[0:CL, :], in_=nfc[0:CL, :], func=AF.Copy), [Tnfc], [Tnb])
                        stq(ckv_out[l, row0 + c0:row0 + c0 + CL, :], nfc[0:CL, 256:512], "a12_%d" % par, [Tnfc])

                    def LATb(c):
                        c0 = c * CL
                        pTl = pT[:, 0:512].rearrange("p (b t) -> p b t", b=4)
                        for b_ in range(4):
                            P(lambda e: e.transpose(out=pTl[:, b_, 0:CL], in_=nb[0:CL, b_ * 128:(b_ + 1) * 128], identity=idb[0:CL, 0:CL]), [Tnb, Tidb], [TpT])
                        V(lambda e: e.tensor_copy(out=latT[:, :, c0:c0 + CL], in_=pTl[:, :, 0:CL]), [TpT], [TlatT])

                    def SILU(c):
                        c0 = c * CL
                        par = c % 2
                        gsb, Tgsb = gsb_b[par]
                        sgc_, Tsgc_ = sgo_b[par]
                        A(lambda e: e.activation(out=sgt[0:CL, :], in_=gsb[0:CL, :], func=AF.Exp, scale=-1.0), [Tgsb], [Tsgt])
                        A(lambda e: e.activation(out=sgt[0:CL, :], in_=sgt[0:CL, :], func=AF.Ln, bias=1.0), [Tsgt], [Tsgt])
                        A(lambda e: e.activation(out=sgt[0:CL, :], in_=sgt[0:CL, :], func=AF.Exp, scale=-1.0), [Tsgt], [Tsgt])
                        V(lambda e: e.tensor_tensor(out=sgc_[0:CL, :], in0=gsb[0:CL, :], in1=sgt[0:CL, :], op=ALU.mult), [Tgsb, Tsgt], [Tsgc_])
                        stq(sgg[row0 + c0:row0 + c0 + CL, :], sgc_[0:CL, :], "a14_%d" % par, [Tsgc_], [Tdram])

                    def G1a(c):
                        par = c % 2
                        lgt, Tlgt = lgt_b[par]
                        A(lambda e: e.activation(out=lgt[0:CL, :], in_=lgt[0:CL, :], func=AF.Exp, scale=-1.0), [Tlgt], [Tlgt])
                        A(lambda e: e.activation(out=lgt[0:CL, :], in_=lgt[0:CL, :], func=AF.Ln, bias=1.0), [Tlgt], [Tlgt])
                        V(lambda e: e.tensor_scalar(out=lgt[0:CL, :], in0=lgt[0:CL, :], scalar1=-1.0 / 16.0, scalar2=None, op0=ALU.mult), [Tlgt], [Tlgt])

                    def G1b(c):
                        par = c % 2
                        lgt, Tlgt = lgt_b[par]
                        if is_s:
                            ld(SstX[:].rearrange("k (h v) -> k h v", h=4), sgla[l, c].rearrange("h k v -> k h v"), "a10", [TSstX])
                            V(lambda e: e.tensor_copy(out=SbfX[:], in_=SstX[:]), [TSstX], [TSbfX])
                        for h in range(4):
                            P(lambda e: e.matmul(pbT3[:, h, 0:CL], lhsT=lgt[0:CL, h * 64:(h + 1) * 64], rhs=tlt[0:CL, 0:CL], start=True, stop=True), [Tlgt, Ttlt], [Tb6])
                        P(lambda e: e.matmul(b6[0:CL, 256:512], lhsT=tut[0:CL, 0:CL], rhs=lgt[0:CL, :], start=True, stop=True), [Tlgt, Ttut], [Tb6])

                    def G2(c):
                        c0 = c * CL
                        par = c % 2
                        ksb, Tksb = ksb_b[par]
                        A(lambda e: e.activation(out=eb[:, :, 0:CL], in_=pbT3[:, :, 0:CL], func=AF.Exp), [Tb6], [Teb])
                        A(lambda e: e.activation(out=enb[:, :, 0:CL], in_=pbT3[:, :, 0:CL], func=AF.Exp, scale=-1.0), [Tb6], [Tenb])
                        A(lambda e: e.activation(out=ec[0:CL, :], in_=b6[0:CL, 256:512], func=AF.Exp), [Tb6], [Tec])
                        V(lambda e: e.tensor_tensor(out=qt[:, :, 0:CL], in0=gqT[:, :, c0:c0 + CL], in1=eb[:, :, 0:CL], op=ALU.mult), [TgqT, Teb], [Tqt])
                        V(lambda e: e.tensor_tensor(out=kt[:, :, 0:CL], in0=gkT[:, :, c0:c0 + CL], in1=enb[:, :, 0:CL], op=ALU.mult), [TgkT, Tenb], [Tkt])
                        if not is_s:
                            V(lambda e: e.tensor_tensor(out=Bc[:, :], in0=Bc[:, :], in1=pbT3[:, :, CL - 1], op=ALU.add), [TBc, Tb6], [TBc])
                        for h in range(4):
                            P(lambda e: e.matmul(b5[0:CL, h * 128:(h + 1) * 128], lhsT=qt[:, h, 0:CL], rhs=SbfX[:, h * 128:(h + 1) * 128], start=(h == 0), stop=False, skip_group_check=True), [Tqt, TSbfX], [Tb5])
                        pA3 = b4[0:64, 0:256].rearrange("p (h t) -> p h t", h=4)
                        for h in range(4):
                            P(lambda e: e.matmul(pA3[0:CL, h, 0:CL], lhsT=kt[:, h, 0:CL], rhs=qt[:, h, 0:CL], start=True, stop=True), [Tkt, Tqt], [Tb4])
                        V(lambda e: e.tensor_tensor(out=kh[0:CL, :], in0=ksb[0:CL, :], in1=ec[0:CL, :], op=ALU.mult), [Tksb, Tec], [Tkh])
                        if not is_s:
                            V(lambda e: e.tensor_tensor(out=qhat[:, :, c0:c0 + CL], in0=qt[:, :, 0:CL], in1=eBc[:, :].unsqueeze(2).broadcast_to([64, 4, CL]), op=ALU.mult), [Tqt, TeBc], [Tqhat])
                            A(lambda e: e.activation(out=eBc[:, :], in_=Bc[:, :], func=AF.Exp), [TBc], [TeBc])

                    def G3(c):
                        c0 = c * CL
                        par = c % 2
                        vb, Tvb = vb_b[par]
                        olc_, Tolc_ = ol_b[par]
                        pA3 = b4[0:64, 0:256].rearrange("p (h t) -> p h t", h=4)
                        V(lambda e: e.tensor_tensor(out=At[0:CL, :, 0:CL], in0=pA3[0:CL, :, 0:CL], in1=tlt[0:CL, 0:CL].unsqueeze(1).broadcast_to([CL, 4, CL]), op=ALU.mult), [Tb4, Ttlt], [TAt])
                        for h in range(4):
                            P(lambda e: e.matmul(b5[0:CL, h * 128:(h + 1) * 128], lhsT=At[0:CL, h, 0:CL], rhs=vb[0:CL, h * 128:(h + 1) * 128], start=False, stop=(h == 3), skip_group_check=True), [TAt, Tvb], [Tb5])
                        for h in range(4):
                            P(lambda e: e.matmul(b6[0:64, h * 128:(h + 1) * 128], lhsT=kh[0:CL, h * 64:(h + 1) * 64], rhs=vb[0:CL, h * 128:(h + 1) * 128], start=True, stop=True), [Tkh, Tvb], [Tb6])
                        A(lambda e: e.activation(out=olc_[0:CL, :], in_=b5[0:CL, :], func=AF.Copy), [Tb5], [Tolc_])
                        stq(oloc[row0 + c0:row0 + c0 + CL, :], olc_[0:CL, :], "a13_%d" % par, [Tolc_], [Tdram])

                    def G4(c):
                        for h in range(4):
                            V(lambda e: e.scalar_tensor_tensor(out=SstX[:, h * 128:(h + 1) * 128], in0=SstX[:, h * 128:(h + 1) * 128], scalar=eb[:, h, CL - 1:CL], in1=b6[0:64, h * 128:(h + 1) * 128], op0=ALU.mult, op1=ALU.add), [TSstX, Teb, Tb6], [TSstX])
                        if is_s:
                            stq(gla_out[l, 1 + c], SstX[:], "a11", [TSstX])
                        else:
                            V(lambda e: e.tensor_copy(out=SbfX[:], in_=SstX[:]), [TSstX], [TSbfX])

                    for g in range(4):
                        PGmm(0, g)
                        PGev(0, g)
                    G1a(0)
                    for c in range(nch):
                        nxt = c + 1 < nch
                        G1b(c)
                        if nxt:
                            PGmm(c + 1, 0)
                            PGmm(c + 1, 1)
                        if c >= 1:
                            LATb(c - 1)
                        G2(c)
                        if nxt:
                            PGev(c + 1, 0)
                            PGev(c + 1, 1)
                            G1a(c + 1)
                            PGmm(c + 1, 2)
                        G3(c)
                        if nxt:
                            PGev(c + 1, 2)
                            PGmm(c + 1, 3)
                        G4(c)
                        LATa(c)
                        SILU(c)
                        if nxt:
                            PGev(c + 1, 3)
                        if l == 0 and bg_budget[0] > 0:
                            bg_run(2)
                            bg_budget[0] -= 2
                        for fn_ in nhooks.pop(c, []):
                            fn_()
                    LATb(nch - 1)
                    for k_ in sorted(nhooks):
                        for fn_ in nhooks[k_]:
                            fn_()

                    if is_s:
                        stq(latS[0:512, 0:T].rearrange("(b p) t -> p b t", p=128), latT[:, :, 0:T], "a15", [TlatT], [Tdram])
                    else:
                        stq(latL[row0 // 512][0:512, 0:T].rearrange("(b p) t -> p b t", p=128), latT[:, :, 0:T], "a15", [TlatT], [TlatLb[row0 // 512]])
                        stq(qhT[:, :, row0:row0 + T], qhat[:, :, 0:T], "a16", [Tqhat], [Tdram])
                        ti_ = row0 // 512
                        deferred_ag.append((latL[ti_], latG[ti_], [TlatLa[ti_], TlatLb[ti_]], [TlatG[ti_]]))
                    if not is_s and row0 + T == NTOK:
                        V(lambda e: e.tensor_copy(out=sout[:, 0:512], in_=Sst[:]), [TSst], [Tsout])
                        V(lambda e: e.tensor_copy(out=sout[:, 512:516], in_=eBc[:, :]), [TeBc], [Tsout])
                        stq(glaLa[:, :], sout[:], "a17", [Tsout], [TglaL])
                        deferred_ag.append((glaLa, glaGa, [TglaL], [TglaG]))
                fw.barrier_all()

            fw.barrier_all()
            for args_ in deferred_ag:
                allgather(*args_)
            if l == 0:
                bg_run(11 * STEPS_PER_CHUNK)
            if l == 0:
                bg_run(max(0, bg_budget[0]))
            fw.barrier_all()
            if KSTOP == "A":
                return nc

            if KSTOP == "AG":
                return nc
            with ExitStack() as ph:
                KT, TKT = mk(ph, "sb", "KT", [96, SEQ], BF16)
                Vg, TVg = mk(ph, "sb", "Vg", [128, 128, 128], BF16)
                wuq, Twuq = mk(ph, "sb", "wuq", [128, 2, 2 * 96], BF16)
                wuk, Twuk = mk(ph, "sb", "wuk", [128, 2, 96], BF16)
                wuv, Twuv = mk(ph, "sb", "wuv", [128, 2, 64], BF16)
                wuq2, Twuq2 = mk(ph, "sb", "wuq2", [128, 2, 2 * 96], BF16)
                wuk2, Twuk2 = mk(ph, "sb", "wuk2", [128, 2, 96], BF16)
                wuv2, Twuv2 = mk(ph, "sb", "wuv2", [128, 2, 64], BF16)
                lt_bufs = [mk(ph, "sb", "ltb%d" % i, [128, 2, 512], BF16) for i in range(4)]
                cq_bufs = [mk(ph, "sb", "cqb%d" % i, [128, 2, 512], BF16) for i in range(2)]
                tC_bufs = [mk(ph, "sb", "tC%d" % i, [96, 512], F32) for i in range(2)]
                tS_bufs = [mk(ph, "sb", "tS%d" % i, [96, 512], F32) for i in range(2)]
                q1, Tq1 = mk(ph, "sb", "q1", [96, 512], F32)
                q2, Tq2 = mk(ph, "sb", "q2", [96, 512], F32)
                qb_bufs = [mk(ph, "sb", "qbb%d" % i, [96, 512], BF16) for i in range(2)]
                pt_bufs = [mk(ph, "sb", "ptb%d" % i, [128, 512], BF16) for i in range(4)]
                of_b = [mk(ph, "sb", "of%d" % i, [128, 512], F32) for i in range(2)]
                orc, Torc = mk(ph, "sb", "orc", [64, 512], F32)
                ob_b = [mk(ph, "sb", "ob%d" % i, [64, 512], BF16) for i in range(2)]
                ckt, Tckt = mk(ph, "sb", "ckt", [128, 16, 256], BF16)
                krp, Tkrp = mk(ph, "sb", "krp", [128, 17, 96], BF16)
                cT, TcT = mk(ph, "sb", "cT", [128, 2, 2080], BF16)
                cqS, TcqS = mk(ph, "sb", "cqS", [128, 2, NS], BF16)
                krS, TkrS = mk(ph, "sb", "krS", [32, NS], BF16)
                KTs, TKTs = mk(ph, "sb", "KTs", [96, 2080], BF16)
                Vs, TVs = mk(ph, "sb", "Vs", [128, 17, 128], BF16)
                pall_bufs = [mk(ph, "sb", "pall%d" % i, [128, 544], BF16) for i in range(2)]
                tCs, TtCs = mk(ph, "sb", "tCs", [96, NS], F32)
                tSs, TtSs = mk(ph, "sb", "tSs", [96, NS], F32)

                G(lambda e: e.memset(Vg[:, :, 64:128], 1.0), [], [TVg])
                G(lambda e: e.memset(Vs[:, :, 64:128], 1.0), [], [TVs])
                G(lambda e: e.memset(wuk[:], 0.0), [], [Twuk])
                G(lambda e: e.memset(wuk2[:], 0.0), [], [Twuk2])
                G(lambda e: e.memset(krp[:], 0.0), [], [Tkrp])
                for of_, Tof_ in of_b:
                    G(lambda e: e.memset(of_[:], 0.0), [], [Tof_])

                def make_q(wq_tile, Twq, wcol, cq_ap, Tcq, tCa, tSa, Ttabs, N, qb, Tqb):
                    br, Tbr = pb[4]
                    for kc in range(2):
                        P(lambda e: e.matmul(br[0:96, 0:N], lhsT=wq_tile[:, kc, wcol:wcol + 96], rhs=cq_ap(kc), start=(kc == 0), stop=(kc == 1)), [Twq, Tcq], [Tbr])
                    V(lambda e: e.tensor_copy(out=qb[0:64, 0:N], in_=br[0:64, 0:N]), [Tbr], [Tqb])
                    V(lambda e: e.tensor_tensor(out=q1[64:96, 0:N], in0=br[64:96, 0:N], in1=tCa, op=ALU.mult), [Tbr] + Ttabs, [Tq1])
                    for kc in range(2):
                        P(lambda e: e.matmul(br[0:96, 0:N], lhsT=wq_tile[:, kc, wcol + 96:wcol + 192], rhs=cq_ap(kc), start=(kc == 0), stop=(kc == 1)), [Twq, Tcq], [Tbr])
                    V(lambda e: e.tensor_tensor(out=q2[64:96, 0:N], in0=br[64:96, 0:N], in1=tSa, op=ALU.mult), [Tbr] + Ttabs, [Tq2])
                    V(lambda e: e.tensor_tensor(out=qb[64:96, 0:N], in0=q1[64:96, 0:N], in1=q2[64:96, 0:N], op=ALU.add), [Tq1, Tq2], [Tqb])

                def finish_o1(obank, Tobank, N, par):
                    of, Tof = of_b[par]
                    V(lambda e: e.tensor_copy(out=of[0:64, 0:N], in_=obank[0:64, 0:N]), [Tobank], [Tof])
                    V(lambda e: e.reciprocal(out=of[64:128, 0:N], in_=obank[64:128, 0:N]), [Tobank], [Tof])

                def finish_o2(N, par, dst_dram, Tdst=None):
                    of, Tof = of_b[par]
                    ob, Tob = ob_b[par]
                    bs, Tbs = pb[5]
                    P(lambda e: e.matmul(bs[0:64, 0:N], lhsT=smat[64:128, 0:64], rhs=of[64:128, 0:N], start=True, stop=True), [Tsmat, Tof], [Tbs])
                    V(lambda e: e.tensor_copy(out=orc[:, 0:N], in_=bs[0:64, 0:N]), [Tbs], [Torc])
                    V(lambda e: e.tensor_tensor(out=ob[:, 0:N], in0=of[0:64, 0:N], in1=orc[:, 0:N], op=ALU.mult), [Tof, Torc], [Tob])
                    stq(dst_dram, ob[:, 0:N], "b9_%d" % par, [Tob], [Tdst if Tdst is not None else Tdram])

                def finish_o(obank, Tobank, N, dst_dram, Tdst=None, par=0):
                    finish_o1(obank, Tobank, N, par)
                    finish_o2(N, par, dst_dram, Tdst)

                for hh in range(2):
                    ldc(wuq[:], w_uq_loc[l, :, hh].rearrange("(kc p) v d -> p kc (v d)", p=128), "b0", [Twuq])
                    ldc(wuk[:, :, 0:64], w_uk_loc[l, :, hh, :].rearrange("(kc p) d -> p kc d", p=128), "b1", [Twuk])
                    ldc(wuv[:], w_uv_loc[l, :, hh, :].rearrange("(kc p) d -> p kc d", p=128), "b2", [Twuv])
                    for t in range(SEQ // 512):
                        rk, ti = t // 8, t % 8
                        ld(KT[64:96, t * 512:(t + 1) * 512], latG[ti][rk * 544 + 512:rk * 544 + 544, :], "b3", [TKT], [TlatG[ti]])
                    for t in range(SEQ // 512):
                        rk, ti = t // 8, t % 8
                        ltb, Tltb = lt_bufs[t % 4]
                        ld(ltb[:], latG[ti][rk * 544 + 256:rk * 544 + 512, :].rearrange("(kc p) t -> p kc t", p=128), "b4_%d" % (t % 4), [Tltb], [TlatG[ti]])
                        bk, Tbk = pb[t % 2]
                        for kc in range(2):
                            P(lambda e: e.matmul(bk[0:64, :], lhsT=wuk[:, kc, 0:64], rhs=ltb[:, kc, :], start=(kc == 0), stop=(kc == 1)), [Twuk, Tltb], [Tbk])
                        A(lambda e: e.activation(out=KT[0:64, t * 512:(t + 1) * 512], in_=bk[0:64, :], func=AF.Copy), [Tbk], [TKT])
                        bv, Tbv = pb[2 + t % 2]
                        for j in range(4):
                            for kc in range(2):
                                P(lambda e: e.matmul(bv[:, j * 64:(j + 1) * 64], lhsT=ltb[:, kc, j * 128:(j + 1) * 128], rhs=wuv[:, kc, :], start=(kc == 0), stop=(kc == 1)), [Tltb, Twuv], [Tbv])
                        V(lambda e: e.tensor_copy(out=Vg[:, t * 4:(t + 1) * 4, 0:64], in_=bv[:, 0:256].rearrange("p (j d) -> p j d", j=4)), [Tbv], [TVg])
                    NQ = SEQ // 512
                    units = [(qi, kti) for qi in range(NQ) for kti in range(4 * qi + 4)]
                    LOOK = 2

                    def q_loads(qi):
                        rk, ti = qi // 8, qi % 8
                        cqb, Tcqb = cq_bufs[qi % 2]
                        tCb, TtCb = tC_bufs[qi % 2]
                        tSb, TtSb = tS_bufs[qi % 2]
                        ld(cqb[:], latG[ti][rk * 544:rk * 544 + 256, :].rearrange("(kc p) t -> p kc t", p=128), "b5_%d" % (qi % 2), [Tcqb], [TlatG[ti]])
                        ld(tCb[64:96, :], tabGC[64:96, qi * 512:(qi + 1) * 512], "b6_%d" % (qi % 2), [TtCb], [TtabG[qi // 2]])
                        ld(tSb[64:96, :], tabGS[64:96, qi * 512:(qi + 1) * 512], "b7_%d" % (qi % 2), [TtSb], [TtabG[qi // 2]])

                    def q_make(qi):
                        cqb, Tcqb = cq_bufs[qi % 2]
                        tCb, TtCb = tC_bufs[qi % 2]
                        tSb, TtSb = tS_bufs[qi % 2]
                        qb, Tqb = qb_bufs[qi % 2]
                        make_q(wuq, Twuq, 0, lambda kc: cqb[:, kc, :], Tcqb, tCb[64:96, :], tSb[64:96, :], [TtCb, TtSb], 512, qb, Tqb)

                    s_banks = [pb[2], pb[3], pb[6]]

                    def emit_qk(ui):
                        qi, kti = units[ui]
                        d = kti - 4 * qi
                        cs = 0 if d < 0 else d * 128
                        qb, Tqb = qb_bufs[qi % 2]
                        sbank, Tsbank = s_banks[ui % 3]
                        P(lambda e: e.matmul(sbank[:, cs:512], lhsT=KT[0:96, kti * 128:(kti + 1) * 128], rhs=qb[0:96, cs:512], start=True, stop=True), [TKT, Tqb], [Tsbank])

                    def emit_exp_pv(ui):
                        qi, kti = units[ui]
                        nkt = 4 * qi + 4
                        d = kti - 4 * qi
                        cs = 0 if d < 0 else d * 128
                        sbank, Tsbank = s_banks[ui % 3]
                        ptb, Tptb = pt_bufs[ui % 4]
                        obank, Tobank = pb[qi % 2]
                        A(lambda e: e.activation(out=ptb[:, cs:512], in_=sbank[:, cs:512], func=AF.Exp, scale=ATTN_SCALE), [Tsbank], [Tptb])
                        if d >= 0:
                            V(lambda e: e.memset(ptb[64:128, cs:cs + 64], 0.0), [], [Tptb])
                        P(lambda e: e.matmul(obank[:, cs:512], lhsT=Vg[:, kti, :], rhs=ptb[:, cs:512], start=(kti == 0), stop=(kti == nkt - 1), skip_group_check=True), [TVg, Tptb], [Tobank])

                    def do_finish1(fq):
                        finish_o1(pb[fq % 2][0], pb[fq % 2][1], 512, fq % 2)

                    def do_finish2(fq):
                        finish_o2(512, fq % 2, oL[fq // 8][hh * 64:(hh + 1) * 64, (fq % 8) * 512:(fq % 8 + 1) * 512], ToL[fq // 8])
                        if hh == 1 and fq % 8 == 7:
                            allgather(oL[fq // 8], oG[fq // 8], [ToL[fq // 8]], [ToG[fq // 8]])

                    if l == 0 and hh == 1:
                        bg_run(len(bg_steps))
                    q_loads(0)
                    q_loads(1)
                    q_make(0)
                    for ui in range(min(LOOK, len(units))):
                        emit_qk(ui)
                    pending = []
                    for ui, (qi, kti) in enumerate(units):
                        if kti == 0:
                            if qi + 2 < NQ:
                                q_loads(qi + 2)
                            if qi + 1 < NQ:
                                q_make(qi + 1)
                        if ui + LOOK < len(units):
                            emit_qk(ui + LOOK)
                        emit_exp_pv(ui)
                        if l == 0 and hh == 0 and ui % 2 == 0:
                            bg_run(1)
                        if kti == 4 * qi + 3:
                            pending.append((ui + 2, 0, qi))
                            pending.append((ui + 12, 1, qi))
                            pending.sort()
                        while pending and pending[0][0] <= ui:
                            _, kind, fq = pending.pop(0)
                            (do_finish1 if kind == 0 else do_finish2)(fq)
                    for _, kind, fq in sorted(pending):
                        (do_finish1 if kind == 0 else do_finish2)(fq)

                fw._need(fw.pool, ("cc", fw.dma_sems["cc"][1]))
                ld(cqS[:], latS[0:256, :].rearrange("(kc p) t -> p kc t", p=128), "s0", [TcqS], [Tdram])
                ld(tCs[64:96, :], tabLC[64:96, NTOK:NROW], "s1", [TtCs], [Tdram])
                ld(tSs[64:96, :], tabLS[64:96, NTOK:NROW], "s2", [TtSs], [Tdram])
                for s in range(2):
                    ldc(ckt[:], cckv[l, s].rearrange("(t p) d -> p t d", p=128), "s3", [Tckt])
                    ldc(krp[:, 0:16, 64:96], ckr[l, s].rearrange("(t p) d -> p t d", p=128), "s4", [Tkrp])
                    ld(cT[:, :, PAST:PAST + 32], latS[256:512, s * 32:(s + 1) * 32].rearrange("(kc p) t -> p kc t", p=128), "s5", [TcT], [Tdram])
                    ld(krS[:, 0:32], latS[512:544, s * 32:(s + 1) * 32], "s6", [TkrS], [Tdram])
                    pTc = pT[:].rearrange("p (b t) -> p b t", b=8)
                    for t4 in range(4):
                        for tt_ in range(4):
                            for kc in range(2):
                                P(lambda e: e.transpose(out=pTc[:, tt_ * 2 + kc, :], in_=ckt[:, t4 * 4 + tt_, kc * 128:(kc + 1) * 128], identity=idb[:]), [Tckt, Tidb], [TpT])
                        for kc in range(2):
                            V(lambda e: e.tensor_copy(out=cT[:, kc, t4 * 512:(t4 + 1) * 512].rearrange("p (t c) -> p t c", t=4), in_=pTc[:, kc::2, :]), [TpT], [TcT])
                    for h in range(8):
                        (swq, Tswq, swk, Tswk, swv, Tswv) = (wuq, Twuq, wuk, Twuk, wuv, Twuv) if h % 2 == 0 else (wuq2, Twuq2, wuk2, Twuk2, wuv2, Twuv2)
                        ldc(swq[:], w_uq_all[l, :, h].rearrange("(kc p) v d -> p kc (v d)", p=128), "b0_%d" % (h % 2), [Tswq])
                        ldc(swk[:, :, 0:64], w_uk_all[l, :, h, :].rearrange("(kc p) d -> p kc d", p=128), "b1_%d" % (h % 2), [Tswk])
                        ldc(swv[:], w_uv_all[l, :, h, :].rearrange("(kc p) d -> p kc d", p=128), "b2_%d" % (h % 2), [Tswv])
                        for t4 in range(4):
                            bk, Tbk = pb[t4 % 2]
                            for j in range(4):
                                tix = t4 * 4 + j
                                for kc in range(2):
                                    P(lambda e: e.matmul(bk[0:96, j * 128:(j + 1) * 128], lhsT=swk[:, kc, 0:96], rhs=cT[:, kc, tix * 128:(tix + 1) * 128], start=(kc == 0), stop=False, skip_group_check=True), [Tswk, TcT], [Tbk])
                                P(lambda e: e.matmul(bk[0:96, j * 128:(j + 1) * 128], lhsT=krp[:, tix, :], rhs=idb[:], start=False, stop=True, skip_group_check=True), [Tkrp, Tidb], [Tbk])
                            A(lambda e: e.activation(out=KTs[:, t4 * 512:(t4 + 1) * 512], in_=bk[0:96, :], func=AF.Copy), [Tbk], [TKTs])
                        bk, Tbk = pb[0]
                        for kc in range(2):
                            P(lambda e: e.matmul(bk[0:64, 0:32], lhsT=swk[:, kc, 0:64], rhs=cT[:, kc, PAST:PAST + 32], start=(kc == 0), stop=(kc == 1)), [Tswk, TcT], [Tbk])
                        A(lambda e: e.activation(out=KTs[0:64, PAST:PAST + 32], in_=bk[0:64, 0:32], func=AF.Copy), [Tbk], [TKTs])
                        fw.dma(sp, KTs[64:96, PAST:PAST + 32], latS[512:544, s * 32:(s + 1) * 32], "s7", reads=[Tdram], writes=[TKTs])
                        for t4 in range(5):
                            bv, Tbv = pb[2 + t4 % 2]
                            nt = 4 if t4 < 4 else 1
                            for j in range(nt):
                                tix = t4 * 4 + j
                                kp = 128 if tix < 16 else 32
                                for kc in range(2):
                                    P(lambda e: e.matmul(bv[0:kp, j * 64:(j + 1) * 64], lhsT=cT[:, kc, tix * 128:tix * 128 + kp], rhs=swv[:, kc, :], start=(kc == 0), stop=(kc == 1)), [TcT, Tswv], [Tbv])
                            if t4 < 4:
                                V(lambda e: e.tensor_copy(out=Vs[:, t4 * 4:(t4 + 1) * 4, 0:64], in_=bv[:, 0:256].rearrange("p (j d) -> p j d", j=4)), [Tbv], [TVs])
                            else:
                                V(lambda e: e.tensor_copy(out=Vs[0:32, 16, 0:64], in_=bv[0:32, 0:64]), [Tbv], [TVs])
                        qb, Tqb = qb_bufs[h % 2]
                        make_q(swq, Tswq, 0, lambda kc: cqS[:, kc, s * 32:(s + 1) * 32], TcqS, tCs[64:96, s * 32:(s + 1) * 32], tSs[64:96, s * 32:(s + 1) * 32], [TtCs, TtSs], 32, qb, Tqb)
                        obank, Tobank = pb[h % 2]
                        sA, TsA = pb[2]
                        sB, TsB = pb[3]
                        pall, Tpall = pall_bufs[h % 2]
                        for kti in range(16):
                            P(lambda e: e.matmul(sA[:, kti * 32:(kti + 1) * 32], lhsT=KTs[0:96, kti * 128:(kti + 1) * 128], rhs=qb[0:96, 0:32], start=True, stop=True, skip_group_check=True), [TKTs, Tqb], [TsA])
                        P(lambda e: e.matmul(sB[0:32, 0:32], lhsT=KTs[0:96, PAST:PAST + 32], rhs=qb[0:96, 0:32], start=True, stop=True), [TKTs, Tqb], [TsB])
                        A(lambda e: e.activation(out=pall[:, 0:512], in_=sA[:, :], func=AF.Exp, scale=ATTN_SCALE), [TsA], [Tpall])
                        A(lambda e: e.activation(out=pall[0:32, 512:544], in_=sB[0:32, 0:32], func=AF.Exp, scale=ATTN_SCALE), [TsB], [Tpall])
                        for kti in range(17):
                            kp = 128 if kti < 16 else 32
                            P(lambda e: e.matmul(obank[:, 0:32], lhsT=Vs[0:kp, kti, :], rhs=pall[0:kp, kti * 32:(kti + 1) * 32], start=(kti == 0), stop=(kti == 16)), [TVs, Tpall], [Tobank])
                        finish_o(obank, Tobank, 32, oS[h * 64:(h + 1) * 64, s * 32:(s + 1) * 32])
                fw.barrier_all()
            if l == 0:
                tg.close()

            if KSTOP == "B":
                return nc

            with ExitStack() as ph0:
                gg_, Tgg = mk(ph0, "sb", "ggl", [64, 4, 516], F32)
                Rs, TRs = mk(ph0, "sb", "Rs", [64, 512], F32)
                cf, Tcf = mk(ph0, "sb", "cf", [64, 4], F32)
                se, Tse = mk(ph0, "sb", "se", [64, 512], F32)
                ld(gg_[:], glaGa.rearrange("(j k) c -> k j c", j=4), "c14", [Tgg], [TglaG])
                V(lambda e: e.memset(Rs[:], 0.0), [], [TRs])
                for j in range(4):
                    V(lambda e: e.tensor_scalar(out=cf[:], in0=gg_[:, j, 512:516], scalar1=-1.0, scalar2=None, op0=ALU.add), [Tgg], [Tcf])
                    V(lambda e: e.tensor_scalar(out=cf[:], in0=cf[:], scalar1=sel[0:64, 4 + j:5 + j], scalar2=None, op0=ALU.mult), [Tcf, Tsel], [Tcf])
                    V(lambda e: e.tensor_scalar(out=cf[:], in0=cf[:], scalar1=1.0, scalar2=None, op0=ALU.add), [Tcf], [Tcf])
                    V(lambda e: e.tensor_tensor(out=Rs[:].rearrange("k (h v) -> k h v", h=4), in0=Rs[:].rearrange("k (h v) -> k h v", h=4), in1=cf[:, :].unsqueeze(2).broadcast_to([64, 4, 128]), op=ALU.mult), [TRs, Tcf], [TRs])
                    V(lambda e: e.scalar_tensor_tensor(out=Rs[:], in0=gg_[:, j, 0:512], scalar=sel[0:64, 4 + j:5 + j], in1=Rs[:], op0=ALU.mult, op1=ALU.add), [Tgg, Tsel, TRs], [TRs])
                V(lambda e: e.tensor_copy(out=Rb[:], in_=Rs[:]), [TRs], [TRb])
                V(lambda e: e.tensor_tensor(out=se[:].rearrange("k (h v) -> k h v", h=4), in0=Rs[:].rearrange("k (h v) -> k h v", h=4), in1=eBc[:, :].unsqueeze(2).broadcast_to([64, 4, 128]), op=ALU.mult), [TRs, TeBc], [Tse])
                V(lambda e: e.tensor_tensor(out=se[:], in0=se[:], in1=Sst[:], op=ALU.add), [Tse, TSst], [Tse])
                stq(gla_out[l, 0], se[:], "c15", [Tse])
                fw.barrier_all()

            with ExitStack() as ph:
                wo, Two = mk(ph, "sb", "wo", [128, 8, D], BF16)
                wd, Twd = mk(ph, "sb", "wd", [128, 22, D], BF16)
                NWG = 4
                wg_bufs = [mk(ph, "sb", "wgb%d" % i, [128, 8, 256], BF16) for i in range(NWG)]
                gft, Tgft = mk(ph, "sb", "gft", [128, D], F32)
                gon, Tgon = mk(ph, "sb", "gon", [128, 128], F32)
                xt_b = [mk(ph, "sb", "xtc%d" % i, [128, 4, D], F32)[0] for i in range(2)]
                Txt_b = [[Trk("xt%d_%d" % (i, s_)) for s_ in range(4)] for i in range(2)]
                cand_bufs = [mk(ph, "sb", "cand%d" % i, [128, 4, 512], BF16) for i in range(2)]
                cat_b = [mk(ph, "sb", "catT%d" % i, [128, 8, 512], BF16)[0] for i in range(2)]
                Tcat_b = [[Trk("cat%d_%d" % (i, s_)) for s_ in range(4)] for i in range(2)]
                hid, Thid = mk(ph, "sb", "hid", [128, 22, 512], BF16)
                olc, Tolc = mk(ph, "sb", "olc", [128, 512], F32)
                sgc, Tsgc = mk(ph, "sb", "sgc", [128, 512], F32)
                qhc_b = [mk(ph, "sb", "qhc%d" % i, [64, 4, 512], BF16) for i in range(2)]
                sq, Tsq = mk(ph, "sb", "sqc", [128, 512], F32)
                ss4, Tss4 = mk(ph, "sb", "ss4", [128, 4], F32)
                ogb_b = [mk(ph, "sb", "ogb%d" % i, [128, 512], BF16) for i in range(2)]
                junk, Tjunk = mk(ph, "sb", "junkc", [128, D], BF16)
                ss, Tss = mk(ph, "sb", "ssc", [128, 1], F32)
                hb_b = [mk(ph, "sb", "hbc%d" % i, [128, D], BF16) for i in range(2)]
                sa_bufs = [mk(ph, "sb", "sa%d" % i, [128, 512], F32) for i in range(2)]
                yt_b = [mk(ph, "sb", "yt%d" % i, [128, D], F32) for i in range(2)]

                ld(wo[:], woB[l].rearrange("(kc p) n -> p kc n", p=128), "c10", [Two], [Twc])
                ld(wd[:], wdB[l].rearrange("(kc p) n -> p kc n", p=128), "c11", [Twd], [Twc])
                ld(gft[:], g_ffn[l, 0:1, :].partition_broadcast(128), "c12", [Tgft])
                ld(gon[:], g_on[l, 0:1, :].partition_broadcast(128), "c13", [Tgon])

                wg_ctr = [0]
                NT = len(TILES)

                def P_load(i):
                    row0, T, PS, CL, is_s = TILES[i]
                    nsub = T // PS
                    src = x_in if l == 0 else xs
                    xt = xt_b[i % 2]
                    ld(xt[0:PS, 0:nsub, :], src[row0:row0 + T, :].rearrange("(s p) d -> p s d", p=PS), "c16_%d" % (i % 2), Txt_b[i % 2][0:nsub], [Tdram])
                    if not is_s:
                        qhc, Tqhc = qhc_b[i % 2]
                        ld(qhc[:, :, 0:T], qhT[:, :, row0:row0 + T], "c19_%d" % (i % 2), [Tqhc], [Tdram])

                def P_select(i):
                    row0, T, PS, CL, is_s = TILES[i]
                    nsub = T // PS
                    catT = cat_b[i % 2]
                    Tc = Tcat_b[i % 2][0:nsub]
                    if is_s:
                        ld(catT[:, 0:4, 0:T], oS.rearrange("(kc p) t -> p kc t", p=128), "c17", Tc, [Tdram])
                    else:
                        for j in range(4):
                            cand, Tcand = cand_bufs[j % 2]
                            ld(cand[:], oG[j][:, row0:row0 + T].rearrange("(kc p) t -> p kc t", p=128), "c18_%d" % (j % 2), [Tcand], [ToG[j]])
                            if j == 0:
                                V(lambda e: e.tensor_scalar(out=catT[:, 0:4, :], in0=cand[:], scalar1=sel[:, 0:1], scalar2=None, op0=ALU.mult), [Tcand, Tsel], Tc)
                            else:
                                V(lambda e: e.scalar_tensor_tensor(out=catT[:, 0:4, :], in0=cand[:], scalar=sel[:, j:j + 1], in1=catT[:, 0:4, :], op0=ALU.mult, op1=ALU.add), [Tcand, Tsel] + Tc, Tc)

                def P1(i, s):
                    row0, T, PS, CL, is_s = TILES[i]
                    r0 = row0 + s * PS
                    ogb, Togb = ogb_b[s % 2]
                    ld(olc[0:PS, :], oloc[r0:r0 + PS, :], "c20", [Tolc], [Tdram])
                    ld(sgc[0:PS, :], sgg[r0:r0 + PS, :], "c21", [Tsgc], [Tdram])
                    if not is_s:
                        qhc, Tqhc = qhc_b[i % 2]
                        bc_, Tbc_ = pb[6]
                        for h in range(4):
                            P(lambda e: e.matmul(bc_[0:PS, h * 128:(h + 1) * 128], lhsT=qhc[:, h, s * PS:(s + 1) * PS], rhs=Rb[:, h * 128:(h + 1) * 128], start=True, stop=True), [Tqhc, TRb], [Tbc_])
                        V(lambda e: e.tensor_tensor(out=olc[0:PS, :], in0=olc[0:PS, :], in1=bc_[0:PS, :], op=ALU.add), [Tolc, Tbc_], [Tolc])
                    V(lambda e: e.tensor_tensor(out=sq[0:PS, :], in0=olc[0:PS, :], in1=olc[0:PS, :], op=ALU.mult), [Tolc], [Tsq])
                    V(lambda e: e.tensor_reduce(out=ss4[0:PS, :], in_=sq[0:PS, :].rearrange("p (h d) -> p h d", h=4), axis=AX.X, op=ALU.add), [Tsq], [Tss4])
                    rstd_inplace(ss4[0:PS, :], Tss4, 1.0 / 128)
                    V(lambda e: e.tensor_tensor(out=olc[0:PS, :].rearrange("p (h d) -> p h d", h=4), in0=olc[0:PS, :].rearrange("p (h d) -> p h d", h=4), in1=ss4[0:PS, :].unsqueeze(2).broadcast_to([PS, 4, 128]), op=ALU.mult), [Tolc, Tss4], [Tolc])
                    V(lambda e: e.tensor_tensor(out=sgc[0:PS, :].rearrange("p (h d) -> p h d", h=4), in0=sgc[0:PS, :].rearrange("p (h d) -> p h d", h=4), in1=gon[0:PS, :].unsqueeze(1).broadcast_to([PS, 4, 128]), op=ALU.mult), [Tsgc, Tgon], [Tsgc])
                    V(lambda e: e.tensor_tensor(out=ogb[0:PS, :], in0=olc[0:PS, :], in1=sgc[0:PS, :], op=ALU.mult), [Tolc, Tsgc], [Togb])

                def P2(i, s):
                    row0, T, PS, CL, is_s = TILES[i]
                    ogb, Togb = ogb_b[s % 2]
                    catT = cat_b[i % 2]
                    pTl = pT[:, 0:512].rearrange("p (b t) -> p b t", b=4)
                    for b_ in range(4):
                        P(lambda e: e.transpose(out=pTl[:, b_, 0:PS], in_=ogb[0:PS, b_ * 128:(b_ + 1) * 128], identity=idb[0:PS, 0:PS]), [Togb, Tidb], [TpT])
                    V(lambda e: e.tensor_copy(out=catT[:, 4:8, s * PS:(s + 1) * PS], in_=pTl[:, :, 0:PS]), [TpT], [Tcat_b[i % 2][s]])

                def M(i, hooks):
                    row0, T, PS, CL, is_s = TILES[i]
                    nsub = T // PS
                    xt = xt_b[i % 2]
                    Txs = Txt_b[i % 2]
                    catT = cat_b[i % 2]
                    Tcs = Tcat_b[i % 2]
                    h2T = catT
                    pTv = pT[:].rearrange("p (c t) -> p c t", c=8)

                    def WO(s):
                        for n in range(2):
                            bo, Tbo = pb[(2 * s + n) % 4]
                            for kc in range(8):
                                P(lambda e: e.matmul(bo[0:PS, :], lhsT=catT[:, kc, s * PS:(s + 1) * PS], rhs=wo[:, kc, n * 512:(n + 1) * 512], start=(kc == 0), stop=(kc == 7)), [Tcs[s], Two], [Tbo])
                            V(lambda e: e.tensor_tensor(out=xt[0:PS, s, n * 512:(n + 1) * 512], in0=xt[0:PS, s, n * 512:(n + 1) * 512], in1=bo[0:PS, :], op=ALU.add), [Txs[s], Tbo], [Txs[s]])

                    def N_(s):
                        hb, Thb = hb_b[s % 2]
                        A(lambda e: e.activation(out=junk[0:PS, :], in_=xt[0:PS, s, :], func=AF.Square, accum_out=ss[0:PS, 0:1]), [Txs[s]], [Tjunk, Tss])
                        rstd_inplace(ss[0:PS, 0:1], Tss, 1.0 / D)
                        V(lambda e: e.scalar_tensor_tensor(out=hb[0:PS, :], in0=xt[0:PS, s, :], scalar=ss[0:PS, 0:1], in1=gft[0:PS, :], op0=ALU.mult, op1=ALU.mult), [Txs[s], Tss, Tgft], [Thb])

                    def T_(s):
                        hb, Thb = hb_b[s % 2]
                        for c in range(8):
                            P(lambda e: e.transpose(out=pTv[:, c, 0:PS], in_=hb[0:PS, c * 128:(c + 1) * 128], identity=idb[0:PS, 0:PS]), [Thb, Tidb], [TpT])
                        A(lambda e: e.activation(out=h2T[:, :, s * PS:(s + 1) * PS], in_=pTv[:, :, 0:PS], func=AF.Copy), [TpT], [Tcs[s]])

                    for s in range(nsub):
                        WO(s)
                        if s >= 1:
                            N_(s - 1)
                        if s >= 2:
                            T_(s - 2)
                    N_(nsub - 1)
                    if nsub >= 2:
                        T_(nsub - 2)
                    T_(nsub - 1)
                    for m in range(22):
                        wgb, Twgb = wg_bufs[wg_ctr[0] % NWG]
                        fw.dma(pool, wgb[:], wguB[l][m].rearrange("p (kc c) -> p kc c", kc=8), "c22_%d" % (wg_ctr[0] % NWG), reads=[Twc], writes=[Twgb])
                        wg_ctr[0] += 1
                        ba, Tba = pb[4 + (m % 2)]
                        bu, Tbu = pb[2 * (m % 2)]
                        for kc in range(8):
                            P(lambda e: e.matmul(ba[:, 0:T], lhsT=wgb[:, kc, 0:128], rhs=h2T[:, kc, 0:T], start=(kc == 0), stop=(kc == 7)), [Twgb] + Tcs[0:nsub], [Tba])
                        for kc in range(8):
                            P(lambda e: e.matmul(bu[:, 0:T], lhsT=wgb[:, kc, 128:256], rhs=h2T[:, kc, 0:T], start=(kc == 0), stop=(kc == 7)), [Twgb] + Tcs[0:nsub], [Tbu])
                        sa, Tsa = sa_bufs[m % 2]
                        A(lambda e: e.activation(out=sa[:, 0:T], in_=ba[:, 0:T], func=AF.Silu), [Tba], [Tsa])
                        V(lambda e: e.tensor_tensor(out=hid[:, m, 0:T], in0=sa[:, 0:T], in1=bu[:, 0:T], op=ALU.mult), [Tsa, Tbu], [Thid])
                        for fn in hooks.get(m, []):
                            fn()
                    for s in range(nsub):
                        for n in range(2):
                            bo, Tbo = pb[1 + 2 * ((2 * s + n) % 2)]
                            for m in range(22):
                                P(lambda e: e.matmul(bo[0:PS, :], lhsT=hid[:, m, s * PS:(s + 1) * PS], rhs=wd[:, m, n * 512:(n + 1) * 512], start=(m == 0), stop=(m == 21)), [Thid, Twd], [Tbo])
                            V(lambda e: e.tensor_tensor(out=xt[0:PS, s, n * 512:(n + 1) * 512], in0=xt[0:PS, s, n * 512:(n + 1) * 512], in1=bo[0:PS, :], op=ALU.add), [Txs[s], Tbo], [Txs[s]])
                        if s == 0:
                            for fn in hooks.get(22, []):
                                fn()
                    if l == 0:
                        stq(xs[row0:row0 + T, :].rearrange("(s p) d -> p s d", p=PS), xt[0:PS, 0:nsub, :], "c23_%d" % (i % 2), Txs[0:nsub], [Tdram])
                    else:
                        for s in range(nsub):
                            yt, Tyt = yt_b[s % 2]
                            A(lambda e: e.activation(out=junk[0:PS, :], in_=xt[0:PS, s, :], func=AF.Square, accum_out=ss[0:PS, 0:1]), [Txs[s]], [Tjunk, Tss])
                            rstd_inplace(ss[0:PS, 0:1], Tss, 1.0 / D)
                            V(lambda e: e.scalar_tensor_tensor(out=yt[0:PS, :], in0=xt[0:PS, s, :], scalar=ss[0:PS, 0:1], in1=gfin[0:PS, :], op0=ALU.mult, op1=ALU.mult), [Txs[s], Tss, Tgfin], [Tyt])
                            stq(y_out[row0 + s * PS:row0 + (s + 1) * PS, :], yt[0:PS, :], "c24_%d" % (s % 2), [Tyt])

                P_load(0)
                P_select(0)
                for s in range(TILES[0][1] // TILES[0][2]):
                    P1(0, s)
                    P2(0, s)
                for i in range(NT):
                    hooks = {}
                    if i + 1 < NT:
                        nsub_n = TILES[i + 1][1] // TILES[i + 1][2]
                        hooks.setdefault(0, []).append(lambda i=i: P_load(i + 1))
                        hooks.setdefault(1, []).append(lambda i=i: P_select(i + 1))
                        for s in range(nsub_n):
                            hooks.setdefault(3 + 5 * s, []).append(lambda i=i, s=s: P1(i + 1, s))
                            hooks.setdefault(3 + 5 * s + 4, []).append(lambda i=i, s=s: P2(i + 1, s))
                    M(i, hooks)
                fw.barrier_all()

        fw.barrier_all()
        print("[kernel] instructions:", fw.n_instr, "semaphores:", fw.nsem)
    return nc


_NC_CACHE = {}


def _f32(a):
    return np.ascontiguousarray(np.asarray(a, dtype=np.float32))


def kernel(x_prompt, x_sample, cache_ckv, cache_krope, state_gla,
           g_attn, w_in, g_qn, w_uq, g_kvn, w_ukv, w_a2, b_a2, g_gla_on, w_o,
           g_ffn, w_gu, w_down, g_final):
    x_prompt = _f32(x_prompt); x_sample = _f32(x_sample)
    cache_ckv = _f32(cache_ckv); cache_krope = _f32(cache_krope); state_gla = _f32(state_gla)
    w_in = _f32(w_in); w_uq = _f32(w_uq); w_ukv = _f32(w_ukv); w_gu = _f32(w_gu)

    perm = np.concatenate([np.arange(16, 32), np.arange(0, 16)])
    w_in_x = np.concatenate([w_in, w_in[:, :, OKR:OKR + 32][:, :, perm]], axis=2)
    wq = w_uq.reshape(2, 256, 8, 96)
    wq_raw = wq
    wq_perm = np.concatenate([wq[..., :64], wq[..., 64:][..., perm]], axis=-1)
    w_uq_all = np.ascontiguousarray(np.stack([wq_raw, wq_perm], axis=3))
    wkv = w_ukv.reshape(2, 256, 8, 128)
    w_uk_all = np.ascontiguousarray(wkv[..., :64])
    w_uv_all = np.ascontiguousarray(wkv[..., 64:])
    wa = w_gu[:, :, :DFF].reshape(2, 8, 128, 22, 128)
    wu = w_gu[:, :, DFF:].reshape(2, 8, 128, 22, 128)
    w_gu_t = np.ascontiguousarray(np.concatenate([wa, wu], axis=-1).transpose(0, 3, 2, 1, 4))
    ident = np.eye(128, dtype=np.float32)
    jj, ii = np.meshgrid(np.arange(64), np.arange(64), indexing="ij")
    trilt = (jj <= ii).astype(np.float32)
    triut = (jj > ii).astype(np.float32)
    selmat = np.zeros((128, 64), np.float32)
    selmat[64 + np.arange(64), np.arange(64)] = 1.0
    half = 16
    inv = (10000.0 ** (-np.arange(half, dtype=np.float32) / half)).astype(np.float32)
    ropec = np.zeros((96, 2), np.float32)
    for r_ in range(96):
        ropec[r_, 0] = inv[r_ % 16]
        ropec[r_, 1] = -1.0 if (r_ % 32) < 16 else 1.0
    pos_g = np.arange(SEQ, dtype=np.float32)[None, :]
    g_cat = np.concatenate([_f32(g_qn), _f32(g_kvn)], axis=1)[:, None, :]

    shared = {
        "w_in": w_in_x, "w_uq_all": w_uq_all, "w_uk_all": w_uk_all, "w_uv_all": w_uv_all,
        "w_a2": _f32(w_a2), "b_a2": _f32(b_a2)[:, None, :], "g_attn": _f32(g_attn)[:, None, :],
        "g_ffn": _f32(g_ffn)[:, None, :], "g_final": _f32(g_final)[None, :], "g_cat": _f32(g_cat),
        "g_on": _f32(g_gla_on)[:, None, :], "w_o": _f32(w_o), "w_gu": w_gu_t, "w_dn": _f32(w_down),
        "ident": ident, "trilt": trilt, "triut": triut, "pos_g": pos_g, "ropec": ropec, "selmat": selmat,
    }
    in_maps = []
    for c in range(8):
        g, r = c // 4, c % 4
        xs_ = np.concatenate([x_prompt[g, r * NTOK:(r + 1) * NTOK], x_sample[2 * c:2 * c + 2].reshape(NS, D)], axis=0)
        selm = np.zeros((128, 8), np.float32)
        selm[:, r] = 1.0
        for j in range(4):
            selm[:, 4 + j] = 1.0 if j < r else 0.0
        pos_l = np.concatenate([np.arange(r * NTOK, (r + 1) * NTOK), PAST + np.arange(32), PAST + np.arange(32)]).astype(np.float32)[None, :]
        m = dict(shared)
        m.update({
            "x": np.ascontiguousarray(xs_),
            "cckv": np.ascontiguousarray(cache_ckv[:, 2 * c:2 * c + 2]),
            "ckr": np.ascontiguousarray(cache_krope[:, 2 * c:2 * c + 2]),
            "sgla": np.ascontiguousarray(state_gla[:, 2 * c:2 * c + 2]),
            "w_uq_loc": np.ascontiguousarray(w_uq_all[:, :, 2 * r:2 * r + 2]),
            "w_uk_loc": np.ascontiguousarray(w_uk_all[:, :, 2 * r:2 * r + 2]),
            "w_uv_loc": np.ascontiguousarray(w_uv_all[:, :, 2 * r:2 * r + 2]),
            "selm": selm, "pos_l": pos_l,
        })
        in_maps.append(m)

    if "nc" not in _NC_CACHE:
        _NC_CACHE["nc"] = build_program()
    res = run_bass_kernel_spmd(_NC_CACHE["nc"], in_maps, core_ids=list(range(8)))
    R = res.results

    y_p = np.zeros((2, SEQ, D), np.float32)
    y_s = np.zeros((16, 32, D), np.float32)
    ckv_p = np.zeros((2, 2, SEQ, 256), np.float32)
    kr_p = np.zeros((2, 2, SEQ, 32), np.float32)
    gla_p = np.zeros((2, 2, 4, 64, 128), np.float32)
    ckv_s = np.zeros((2, 16, 32, 256), np.float32)
    kr_s = np.zeros((2, 16, 32, 32), np.float32)
    gla_s = np.zeros((2, 16, 4, 64, 128), np.float32)
    for c in range(8):
        g, r = c // 4, c % 4
        o = R[c]
        y_p[g, r * NTOK:(r + 1) * NTOK] = o["y"][:NTOK]
        y_s[2 * c:2 * c + 2] = o["y"][NTOK:].reshape(2, 32, D)
        ckv_p[:, g, r * NTOK:(r + 1) * NTOK] = o["ckv_o"][:, :NTOK]
        kr_p[:, g, r * NTOK:(r + 1) * NTOK] = o["kr_o"][:, :NTOK]
        ckv_s[:, 2 * c:2 * c + 2] = o["ckv_o"][:, NTOK:].reshape(2, 2, 32, 256)
        kr_s[:, 2 * c:2 * c + 2] = o["kr_o"][:, NTOK:].reshape(2, 2, 32, 32)
        gl = o["gla_o"].reshape(2, 3, 64, 4, 128).transpose(0, 1, 3, 2, 4)
        if r == 3:
            gla_p[:, g] = gl[:, 0]
        gla_s[:, 2 * c] = gl[:, 1]
        gla_s[:, 2 * c + 1] = gl[:, 2]
    return (y_p, y_s, ckv_p, kr_p, gla_p, ckv_s, kr_s, gla_s)
```

```python
import math
from contextlib import ExitStack

import numpy as np
import concourse.bass as bass
import concourse.mybir as mybir
from concourse.bass_utils import run_bass_kernel_spmd

F32 = mybir.dt.float32
BF16 = mybir.dt.bfloat16
I32 = mybir.dt.int32
AF = mybir.ActivationFunctionType
ALU = mybir.AluOpType
AX = mybir.AxisListType

D = 1024
SEQ = 16384
NTOK = 4096
NS = 64
NROW = NTOK + NS
PAST = 2048
DFF = 2816
EPS = 1e-6
ATTN_SCALE = 96 ** -0.5
OCQ, OCKV, OKR, OGQ, OGK, OGV, OGG, OGA, OKRP = 0, 256, 512, 544, 800, 1056, 1568, 2080, 2096
WIN = 2128
TWO_PI = 2.0 * math.pi
C1 = 6.28125
C2 = TWO_PI - C1


class Trk:
    __slots__ = ("name", "w", "r")

    def __init__(self, name):
        self.name = name
        self.w = None
        self.r = []


class Eng:
    def __init__(self, fw, name, eng):
        self.name = name
        self.eng = eng
        self.sem = fw.new_sem("e_" + name)
        self.count = 0
        self.waited = {}


class FW:
    def __init__(self, nc, stack):
        self.nc = nc
        self.stack = stack
        self.nsem = 0
        self.pe = Eng(self, "pe", nc.tensor)
        self.dve = Eng(self, "dve", nc.vector)
        self.act = Eng(self, "act", nc.scalar)
        self.pool = Eng(self, "pool", nc.gpsimd)
        self.sp = Eng(self, "sp", nc.sync)
        self.engs = [self.pe, self.dve, self.act, self.pool, self.sp]
        self.dma_sems = {}
        self.n_instr = 0

    def new_sem(self, name):
        self.nsem += 1
        return self.stack.enter_context(self.nc.semaphore(name))

    def _need(self, E, dep):
        if dep is None:
            return
        key, val = dep
        if E.waited.get(key, 0) >= val:
            return
        sem = key.sem if isinstance(key, Eng) else self.dma_sems[key][0]
        E.eng.wait_ge(sem, val)
        E.waited[key] = val
        self.n_instr += 1

    def _deps(self, E, reads, writes, is_dma=False):
        reads = [t for t in reads if t.name != "dram"]
        writes = [t for t in writes if t.name != "dram"]
        for t in reads:
            if t.w is not None and (is_dma or not (t.w[0] is E and E is self.pe)):
                self._need(E, t.w)
        for t in writes:
            if t.w is not None and (is_dma or t.w[0] is not E):
                self._need(E, t.w)
            for r in t.r:
                if is_dma or r[0] is not E:
                    self._need(E, r)

    def _mark(self, stamp, reads, writes):
        reads = [t for t in reads if t.name != "dram"]
        writes = [t for t in writes if t.name != "dram"]
        for t in reads:
            t.r.append(stamp)
            if len(t.r) > 8:
                last = {}
                for k, v in t.r:
                    if last.get(k, 0) < v:
                        last[k] = v
                t.r = list(last.items())
        for t in writes:
            t.w = stamp
            t.r = []

    def op(self, E, fn, reads=(), writes=()):
        self._deps(E, reads, writes)
        ins = fn(E.eng)
        E.count += 1
        ins.then_inc(E.sem, 1)
        self._mark((E, E.count), reads, writes)
        self.n_instr += 1
        return ins

    def dma(self, Q, out, in_, key, reads=(), writes=()):
        if key not in self.dma_sems:
            self.dma_sems[key] = [self.new_sem("d_" + key), 0, 16]
        self._deps(Q, reads, writes, is_dma=True)
        ent = self.dma_sems[key]
        ins = Q.eng.dma_start(out=out, in_=in_)
        ent[1] += 16
        ins.then_inc(ent[0], 16)
        self._mark((key, ent[1]), reads, writes)
        self.n_instr += 1
        return ins

    def coll(self, fn, key, reads=(), writes=()):
        if key not in self.dma_sems:
            self.dma_sems[key] = [self.new_sem("c_" + key), 0, 1]
        Q = self.pool
        self._deps(Q, reads, writes, is_dma=True)
        ent = self.dma_sems[key]
        ins = fn(Q.eng)
        ent[1] += 1
        ins.then_inc(ent[0], 1)
        self._mark((key, ent[1]), reads, writes)
        self.n_instr += 1
        return ins

    def barrier_all(self):
        for E in self.engs:
            for P in self.engs:
                if P.count > 0 and P is not E:
                    self._need(E, (P, P.count))
            for key, ent in self.dma_sems.items():
                if ent[1] > 0:
                    self._need(E, (key, ent[1]))


import os
KSTOP = os.environ.get("KSTOP", "")


class _Stop(Exception):
    pass


def build_program():
    nc = bass.Bass("TRN2", target_bir_lowering=False)

    def din(name, shape, dt=F32):
        return nc.dram_tensor(name, list(shape), dt, kind="ExternalInput").ap()

    def dout(name, shape, dt=F32):
        return nc.dram_tensor(name, list(shape), dt, kind="ExternalOutput").ap()

    def dscr(name, shape, dt):
        return nc.dram_tensor(name, list(shape), dt)

    x_in = din("x", [NROW, D])
    cckv = din("cckv", [2, 2, PAST, 256])
    ckr = din("ckr", [2, 2, PAST, 32])
    sgla = din("sgla", [2, 2, 4, 64, 128])
    w_in = din("w_in", [2, D, WIN])
    w_uq_loc = din("w_uq_loc", [2, 256, 2, 2, 96])
    w_uq_all = din("w_uq_all", [2, 256, 8, 2, 96])
    w_uk_loc = din("w_uk_loc", [2, 256, 2, 64])
    w_uv_loc = din("w_uv_loc", [2, 256, 2, 64])
    w_uk_all = din("w_uk_all", [2, 256, 8, 64])
    w_uv_all = din("w_uv_all", [2, 256, 8, 64])
    w_a2 = din("w_a2", [2, 16, 256])
    b_a2 = din("b_a2", [2, 1, 256])
    g_attn = din("g_attn", [2, 1, D])
    g_ffn = din("g_ffn", [2, 1, D])
    g_final = din("g_final", [1, D])
    g_cat = din("g_cat", [2, 1, 512])
    g_on = din("g_on", [2, 1, 128])
    w_o = din("w_o", [2, D, D])
    w_gu = din("w_gu", [2, 22, 128, 8, 256])
    w_dn = din("w_dn", [2, DFF, D])
    ident = din("ident", [128, 128])
    trilt = din("trilt", [64, 64])
    triut = din("triut", [64, 64])
    selm = din("selm", [128, 8])
    pos_g = din("pos_g", [1, SEQ])
    pos_l = din("pos_l", [1, NROW])
    ropec = din("ropec", [96, 2])
    selmat = din("selmat", [128, 64])

    y_out = dout("y", [NROW, D])
    ckv_out = dout("ckv_o", [2, NROW, 256])
    kr_out = dout("kr_o", [2, NROW, 32])
    gla_out = dout("gla_o", [2, 3, 64, 512])

    xs = dscr("xs", [NROW, D], F32).ap()
    latL = [dscr("latL%d" % i, [544, 512], BF16).ap() for i in range(8)]
    latG = [dscr("latG%d" % i, [4 * 544, 512], BF16).ap() for i in range(8)]
    latS = dscr("latS", [544, NS], BF16).ap()
    glaL = dscr("glaL", [64, 516], F32)
    glaG = dscr("glaG", [256, 516], F32)
    oloc = dscr("oloc", [NROW, 512], F32).ap()
    qhT = dscr("qhT", [64, 4, NROW], BF16).ap()
    sgg = dscr("sgg", [NROW, 512], F32).ap()
    oL = [dscr("oL%d" % i, [128, NTOK], BF16).ap() for i in range(4)]
    oG = [dscr("oG%d" % i, [512, NTOK], BF16).ap() for i in range(4)]
    oS = dscr("oS", [512, NS], BF16).ap()
    tabGC = dscr("tabGC", [96, SEQ], F32).ap()
    tabGS = dscr("tabGS", [96, SEQ], F32).ap()
    tabLC = dscr("tabLC", [96, NROW], F32).ap()
    tabLS = dscr("tabLS", [96, NROW], F32).ap()
    glaLa, glaGa = glaL.ap(), glaG.ap()
    wguB = [dscr("wguB%d" % i, [22, 128, 2048], BF16).ap() for i in range(2)]
    woB = [dscr("woB%d" % i, [D, D], BF16).ap() for i in range(2)]
    wdB = [dscr("wdB%d" % i, [DFF, D], BF16).ap() for i in range(2)]

    GROUPS = [[0, 1, 2, 3], [4, 5, 6, 7]]

    with ExitStack() as top:
        fw = FW(nc, top)
        pe, dve, act, pool, sp = fw.pe, fw.dve, fw.act, fw.pool, fw.sp

        uniq = [0]

        def mk(stack, kind, name, shape, dt):
            f = nc.sbuf_tensor if kind == "sb" else nc.psum_tensor
            uniq[0] += 1
            return stack.enter_context(f("%s_%d" % (name, uniq[0]), list(shape), dt)), Trk(name)

        def V(fn, r=(), w=()):
            return fw.op(dve, fn, r, w)

        def A(fn, r=(), w=()):
            return fw.op(act, fn, r, w)

        def G(fn, r=(), w=()):
            return fw.op(pool, fn, r, w)

        def P(fn, r=(), w=()):
            return fw.op(pe, fn, r, w)

        def ld(out, in_, key, w, r=()):
            return fw.dma(sp, out, in_, key, reads=r, writes=w)

        def ldc(out, in_, key, w, r=()):
            return fw.dma(pool, out, in_, key, reads=r, writes=w)

        def stq(out, in_, key, r, w=()):
            return fw.dma(sp, out, in_, key, reads=r, writes=w)

        pb = []
        for i in range(7):
            pb.append(mk(top, "ps", "pb%d" % i, [128, 512], F32))
        pT, TpT = mk(top, "ps", "pbT", [128, 1024], BF16)

        idb, Tidb = mk(top, "sb", "idb", [128, 128], BF16)
        idf, Tidf = mk(top, "sb", "idf", [128, 128], F32)
        tlt, Ttlt = mk(top, "sb", "tlt", [64, 64], F32)
        tut, Ttut = mk(top, "sb", "tut", [64, 64], F32)
        sel, Tsel = mk(top, "sb", "sel", [128, 8], F32)
        smat, Tsmat = mk(top, "sb", "smat", [128, 64], F32)
        gfin, Tgfin = mk(top, "sb", "gfin", [128, D], F32)
        Sst, TSst = mk(top, "sb", "Sst", [64, 512], F32)
        Sbf, TSbf = mk(top, "sb", "Sbf", [64, 512], BF16)
        Bc, TBc = mk(top, "sb", "Bc", [64, 4], F32)
        eBc, TeBc = mk(top, "sb", "eBc", [64, 4], F32)
        Twc = Trk("wconv")
        TlatLa = [Trk("latLa%d" % i) for i in range(8)]
        TlatLb = [Trk("latLb%d" % i) for i in range(8)]
        TlatG = [Trk("latG%d" % i) for i in range(8)]
        ToL = [Trk("oL%d" % i) for i in range(4)]
        ToG = [Trk("oG%d" % i) for i in range(4)]
        TglaL, TglaG = Trk("glaL"), Trk("glaG")

        def allgather(src, dst, r, w):
            r = list(r) + [Twc]
            fw.coll(lambda e: e.collective_compute("AllGather", ALU.bypass, replica_groups=GROUPS, ins=[src.opt()], outs=[dst.opt()]), "cc", r, w)

        Rb, TRb = mk(top, "sb", "Rb", [64, 512], BF16)
        Tdram = Trk("dram")

        ld(idf[:], ident[:, :], "c0", [Tidf])
        ldc(idb[:], ident[:, :], "c1", [Tidb])
        ld(tlt[:], trilt[:, :], "c2", [Ttlt])
        ld(tut[:], triut[:, :], "c3", [Ttut])
        ld(sel[:], selm[:, :], "c4", [Tsel])
        ld(smat[:], selmat[:, :], "c5", [Tsmat])
        ld(gfin[:], g_final[0:1, :].partition_broadcast(128), "c6", [Tgfin])

        def rstd_inplace(ap, T_, inv_n):
            V(lambda e: e.tensor_scalar(out=ap, in0=ap, scalar1=inv_n, scalar2=EPS, op0=ALU.mult, op1=ALU.add), [T_], [T_])
            A(lambda e: e.activation(out=ap, in_=ap, func=AF.Ln), [T_], [T_])
            A(lambda e: e.activation(out=ap, in_=ap, func=AF.Exp, scale=-0.5), [T_], [T_])

        tg = ExitStack()
        TGW = 1024
        rc, Trc = mk(tg, "sb", "rc", [96, 2], F32)
        ptl, Tptl = mk(tg, "sb", "ptl", [96, TGW], F32)
        ang, Tang = mk(tg, "sb", "ang", [96, TGW], F32)
        aa, Taa = mk(tg, "sb", "aa", [96, TGW], F32)
        tt, Ttt = mk(tg, "sb", "tt", [96, TGW], F32)
        ki, Tki = mk(tg, "sb", "ki", [96, TGW], I32)
        mm, Tmm = mk(tg, "sb", "mm", [96, TGW], F32)
        res, Tres = mk(tg, "sb", "res", [96, TGW], F32)
        ld(rc[:], ropec[:, :], "p0", [Trc])

        TtabG = [Trk("tabG%d" % i) for i in range(SEQ // 1024)]

        def table_steps(pos_ap, ntot, dC, dS, chunk_trks=None):
            steps = []

            def wrap_and_sin(dst_dram, n, use_sign, wtrk=None):
                steps.append(lambda: V(lambda e: e.tensor_scalar(out=tt[:, 0:n], in0=mm[:, 0:n], scalar1=math.pi, scalar2=None, op0=ALU.is_gt), [Tmm], [Ttt]))
                steps.append(lambda: V(lambda e: e.scalar_tensor_tensor(out=mm[:, 0:n], in0=tt[:, 0:n], scalar=-TWO_PI, in1=mm[:, 0:n], op0=ALU.mult, op1=ALU.add), [Ttt, Tmm], [Tmm]))
                steps.append(lambda: V(lambda e: e.tensor_scalar(out=tt[:, 0:n], in0=mm[:, 0:n], scalar1=-math.pi, scalar2=None, op0=ALU.is_lt), [Tmm], [Ttt]))
                steps.append(lambda: V(lambda e: e.scalar_tensor_tensor(out=mm[:, 0:n], in0=tt[:, 0:n], scalar=TWO_PI, in1=mm[:, 0:n], op0=ALU.mult, op1=ALU.add), [Ttt, Tmm], [Tmm]))
                steps.append(lambda: V(lambda e: e.tensor_scalar(out=aa[:, 0:n], in0=mm[:, 0:n], scalar1=3.1415925, scalar2=-3.1415925, op0=ALU.min, op1=ALU.max), [Tmm], [Taa]))
                steps.append(lambda: A(lambda e: e.activation(out=res[:, 0:n], in_=aa[:, 0:n], func=AF.Sin), [Taa], [Tres]))
                if use_sign:
                    steps.append(lambda: V(lambda e: e.tensor_scalar(out=res[:, 0:n], in0=res[:, 0:n], scalar1=rc[:, 1:2], scalar2=None, op0=ALU.mult), [Tres, Trc], [Tres]))
                steps.append(lambda: stq(dst_dram, res[:, 0:n], "p2", [Tres], [wtrk if wtrk is not None else Tdram]))

            for c0 in range(0, ntot, TGW):
                n = min(TGW, ntot - c0)
                steps.append(lambda c0=c0, n=n: ld(ptl[:, 0:n], pos_ap[0:1, c0:c0 + n].partition_broadcast(96), "p1", [Tptl]))
                steps.append(lambda n=n: V(lambda e: e.tensor_scalar(out=ang[:, 0:n], in0=ptl[:, 0:n], scalar1=rc[:, 0:1], scalar2=None, op0=ALU.mult), [Tptl, Trc], [Tang]))
                steps.append(lambda n=n: V(lambda e: e.tensor_scalar(out=tt[:, 0:n], in0=ang[:, 0:n], scalar1=1.0 / TWO_PI, scalar2=None, op0=ALU.mult), [Tang], [Ttt]))
                steps.append(lambda n=n: V(lambda e: e.tensor_copy(out=ki[:, 0:n], in_=tt[:, 0:n]), [Ttt], [Tki]))
                steps.append(lambda n=n: V(lambda e: e.tensor_copy(out=tt[:, 0:n], in_=ki[:, 0:n]), [Tki], [Ttt]))
                steps.append(lambda n=n: V(lambda e: e.scalar_tensor_tensor(out=mm[:, 0:n], in0=tt[:, 0:n], scalar=-C1, in1=ang[:, 0:n], op0=ALU.mult, op1=ALU.add), [Ttt, Tang], [Tmm]))
                steps.append(lambda n=n: V(lambda e: e.scalar_tensor_tensor(out=mm[:, 0:n], in0=tt[:, 0:n], scalar=-C2, in1=mm[:, 0:n], op0=ALU.mult, op1=ALU.add), [Ttt, Tmm], [Tmm]))
                wtrk_ = chunk_trks[c0 // TGW] if chunk_trks is not None else None
                wrap_and_sin(dS[:, c0:c0 + n], n, True, wtrk_)
                steps.append(lambda n=n: V(lambda e: e.tensor_scalar(out=mm[:, 0:n], in0=mm[:, 0:n], scalar1=math.pi / 2, scalar2=None, op0=ALU.add), [Tmm], [Tmm]))
                wrap_and_sin(dC[:, c0:c0 + n], n, False, wtrk_)
            return steps

        for st_ in table_steps(pos_l, NROW, tabLC, tabLS):
            st_()
        fw.barrier_all()
        bg_steps = table_steps(pos_g, SEQ, tabGC, tabGS, TtabG)
        STEPS_PER_CHUNK = len(bg_steps) // (SEQ // TGW)

        def bg_run(k):
            for _ in range(k):
                if bg_steps:
                    bg_steps.pop(0)()

        if KSTOP == "pro":
            return nc

        TILES = [(t * 512, 512, 128, 64, False) for t in range(NTOK // 512)] + [(NTOK, NS, 64, 32, True)]

        for l in range(2):
            with ExitStack() as ph:
                win, Twin = mk(ph, "sb", "win", [128, 8, WIN], BF16)
                wa2, Twa2 = mk(ph, "sb", "wa2", [16, 256], BF16)
                ba2, Tba2 = mk(ph, "sb", "ba2", [64, 256], F32)
                gat, Tgat = mk(ph, "sb", "gat", [128, D], F32)
                gct, Tgct = mk(ph, "sb", "gct", [64, 512], F32)
                xt, Txt = mk(ph, "sb", "xt", [128, 4, D], F32)
                junk, Tjunk = mk(ph, "sb", "junk", [128, D], F32)
                ss, Tss = mk(ph, "sb", "ss", [128, 1], F32)
                hb_b = [mk(ph, "sb", "hb%d" % i, [128, D], BF16) for i in range(2)]
                hT_b = [mk(ph, "sb", "hT%d" % i, [128, 8, 512], BF16) for i in range(2)]
                gqT, TgqT = mk(ph, "sb", "gqT", [64, 4, 512], F32)
                gkT, TgkT = mk(ph, "sb", "gkT", [64, 4, 512], F32)
                gaT, TgaT = mk(ph, "sb", "gaT", [16, 512], BF16)
                tbC, TtbC = mk(ph, "sb", "tbC", [32, 512], F32)
                tbS, TtbS = mk(ph, "sb", "tbS", [32, 512], F32)
                kr1, Tkr1 = mk(ph, "sb", "kr1", [32, 512], F32)
                kr2, Tkr2 = mk(ph, "sb", "kr2", [32, 512], F32)
                krb, Tkrb = mk(ph, "sb", "krb", [32, 512], BF16)
                kro, Tkro = mk(ph, "sb", "kro", [128, 4, 32], F32)
                sq, Tsq = mk(ph, "sb", "sq", [64, 512], F32)
                ss2, Tss2 = mk(ph, "sb", "ss2", [64, 2], F32)
                nf_b = [mk(ph, "sb", "nf%d" % i, [64, 512], F32) for i in range(2)]
                zsb_b = [mk(ph, "sb", "zsb%d" % i, [64, 512], F32) for i in range(2)]
                ksb_b = [mk(ph, "sb", "ksb%d" % i, [64, 256], F32) for i in range(2)]
                gsb_b = [mk(ph, "sb", "gsb%d" % i, [64, 512], F32) for i in range(2)]
                lgt_b = [mk(ph, "sb", "lgt%d" % i, [64, 256], F32) for i in range(2)]
                vb_b = [mk(ph, "sb", "vb%d" % i, [64, 512], BF16) for i in range(2)]
                sgo_b = [mk(ph, "sb", "sgo%d" % i, [64, 512], F32) for i in range(2)]
                ol_b = [mk(ph, "sb", "ol%d" % i, [64, 512], F32) for i in range(2)]
                nb, Tnb = mk(ph, "sb", "nb", [64, 512], BF16)
                latT, TlatT = mk(ph, "sb", "latT", [128, 4, 512], BF16)
                eb, Teb = mk(ph, "sb", "eb", [64, 4, 64], F32)
                enb, Tenb = mk(ph, "sb", "enb", [64, 4, 64], F32)
                ec, Tec = mk(ph, "sb", "ec", [64, 256], F32)
                qt, Tqt = mk(ph, "sb", "qt", [64, 4, 64], BF16)
                kt, Tkt = mk(ph, "sb", "kt", [64, 4, 64], BF16)
                kh, Tkh = mk(ph, "sb", "kh", [64, 256], BF16)
                sgt, Tsgt = mk(ph, "sb", "sgt", [64, 512], F32)
                qhat, Tqhat = mk(ph, "sb", "qhat", [64, 4, 512], BF16)
                At, TAt = mk(ph, "sb", "At", [64, 4, 64], BF16)
                sout, Tsout = mk(ph, "sb", "sout", [64, 516], F32)
                SsS, TSsS = mk(ph, "sb", "SsS", [64, 512], F32)
                SbS, TSbS = mk(ph, "sb", "SbS", [64, 512], BF16)

                ldc(win[:], w_in[l].rearrange("(kc p) n -> p kc n", p=128), "a0", [Twin])
                ldc(wa2[:], w_a2[l], "a1", [Twa2])
                if l == 0:
                    for l_ in range(2):
                        fw.dma(pool, woB[l_][:, :], w_o[l_], "wc", writes=[Twc])
                        fw.dma(pool, wdB[l_][:, :], w_dn[l_], "wc", writes=[Twc])
                        for hm in range(2):
                            fw.dma(pool, wguB[l_][hm * 11:(hm + 1) * 11], w_gu[l_, hm * 11:(hm + 1) * 11].rearrange("m p kc c -> m p (kc c)"), "wc", writes=[Twc])
                ld(ba2[:], b_a2[l, 0:1, :].partition_broadcast(64), "a2", [Tba2])
                ld(gat[:], g_attn[l, 0:1, :].partition_broadcast(128), "a3", [Tgat])
                ld(gct[:], g_cat[l, 0:1, :].partition_broadcast(64), "a4", [Tgct])
                V(lambda e: e.memset(Sst[:], 0.0), [], [TSst])
                V(lambda e: e.memset(Sbf[:], 0.0), [], [TSbf])
                V(lambda e: e.memset(Bc[:], 0.0), [], [TBc])
                V(lambda e: e.memset(eBc[:], 1.0), [], [TeBc])

                deferred_ag = []
                bg_budget = [0]
                stage_banks = [4, 5, 6, 3]
                stage_i = [0]

                def stage():
                    b = stage_banks[stage_i[0] % len(stage_banks)]
                    stage_i[0] += 1
                    return pb[b]

                def load_x(tidx):
                    row0_, T_, PS_, CL_, is_s_ = TILES[tidx]
                    src_ = x_in if l == 0 else xs
                    ld(xt[0:PS_, 0:T_ // PS_, :], src_[row0_:row0_ + T_, :].rearrange("(s p) d -> p s d", p=PS_), "a5", [Txt], [Tdram])

                pTv = pT[:].rearrange("p (c t) -> p c t", c=8)

                def norm_a(ti, s_):
                    PS_ = TILES[ti][2]
                    hb, Thb = hb_b[s_ % 2]
                    A(lambda e: e.activation(out=junk[0:PS_, :], in_=xt[0:PS_, s_, :], func=AF.Square, accum_out=ss[0:PS_, 0:1]), [Txt], [Tjunk, Tss])
                    rstd_inplace(ss[0:PS_, 0:1], Tss, 1.0 / D)
                    V(lambda e: e.scalar_tensor_tensor(out=hb[0:PS_, :], in0=xt[0:PS_, s_, :], scalar=ss[0:PS_, 0:1], in1=gat[0:PS_, :], op0=ALU.mult, op1=ALU.mult), [Txt, Tss, Tgat], [Thb])

                def norm_b(ti, s_):
                    PS_ = TILES[ti][2]
                    hb, Thb = hb_b[s_ % 2]
                    hTn, ThTn = hT_b[ti % 2]
                    for c_ in range(8):
                        P(lambda e: e.transpose(out=pTv[:, c_, 0:PS_], in_=hb[0:PS_, c_ * 128:(c_ + 1) * 128], identity=idb[0:PS_, 0:PS_]), [Thb, Tidb], [TpT])
                    A(lambda e: e.activation(out=hTn[:, :, s_ * PS_:(s_ + 1) * PS_], in_=pTv[:, :, 0:PS_], func=AF.Copy), [TpT], [ThTn])

                load_x(0)
                for s_ in range(TILES[0][1] // TILES[0][2]):
                    norm_a(0, s_)
                    norm_b(0, s_)
                load_x(1)
                for tidx, (row0, T, PS, CL, is_s) in enumerate(TILES):
                    hT, ThT = hT_b[tidx % 2]
                    nhooks = {}
                    if tidx + 1 < len(TILES):
                        nsub_n = TILES[tidx + 1][1] // TILES[tidx + 1][2]
                        for s_ in range(nsub_n):
                            nhooks.setdefault(1 + s_, []).append(lambda s_=s_, ti=tidx + 1: norm_a(ti, s_))
                            nhooks.setdefault(2 + s_, []).append(lambda s_=s_, ti=tidx + 1: norm_b(ti, s_))
                        if tidx + 2 < len(TILES):
                            nhooks.setdefault(nsub_n, []).append(lambda ti=tidx + 2: load_x(ti))
                    nsub = T // PS
                    nch = T // CL
                    SstX, TSstX, SbfX, TSbfX = (SsS, TSsS, SbS, TSbS) if is_s else (Sst, TSst, Sbf, TSbf)
                    ld(tbC[:, 0:T], tabLC[0:32, row0:row0 + T], "a6", [TtbC], [Tdram])
                    ld(tbS[:, 0:T], tabLS[0:32, row0:row0 + T], "a7", [TtbS], [Tdram])
                    def fm_proj(col0, M):
                        bank, Tb = stage()
                        for kc in range(8):
                            P(lambda e: e.matmul(bank[0:M, 0:T], lhsT=win[:, kc, col0:col0 + M], rhs=hT[:, kc, 0:T], start=(kc == 0), stop=(kc == 7)), [Twin, ThT], [Tb])
                        return bank, Tb

                    for h in range(4):
                        bank, Tb = fm_proj(OGQ + h * 64, 64)
                        V(lambda e: e.tensor_scalar(out=gqT[:, h, 0:T], in0=bank[0:64, 0:T], scalar1=0.125, scalar2=None, op0=ALU.mult), [Tb], [TgqT])
                        bank, Tb = fm_proj(OGK + h * 64, 64)
                        V(lambda e: e.tensor_copy(out=gkT[:, h, 0:T], in_=bank[0:64, 0:T]), [Tb], [TgkT])
                    bank, Tb = fm_proj(OGA, 16)
                    A(lambda e: e.activation(out=gaT[:, 0:T], in_=bank[0:16, 0:T], func=AF.Copy), [Tb], [TgaT])
                    bank, Tb = fm_proj(OKR, 32)
                    V(lambda e: e.tensor_tensor(out=kr1[:, 0:T], in0=bank[0:32, 0:T], in1=tbC[:, 0:T], op=ALU.mult), [Tb, TtbC], [Tkr1])
                    bank, Tb = fm_proj(OKRP, 32)
                    V(lambda e: e.tensor_tensor(out=kr2[:, 0:T], in0=bank[0:32, 0:T], in1=tbS[:, 0:T], op=ALU.mult), [Tb, TtbS], [Tkr2])
                    V(lambda e: e.tensor_tensor(out=kr1[:, 0:T], in0=kr1[:, 0:T], in1=kr2[:, 0:T], op=ALU.add), [Tkr1, Tkr2], [Tkr1])
                    V(lambda e: e.tensor_copy(out=krb[:, 0:T], in_=kr1[:, 0:T]), [Tkr1], [Tkrb])
                    if is_s:
                        stq(latS[512:544, 0:T], krb[:, 0:T], "a8", [Tkrb], [Tdram])
                    else:
                        stq(latL[row0 // 512][512:544, 0:T], krb[:, 0:T], "a8", [Tkrb], [TlatLa[row0 // 512]])
                    bank, Tb = stage()
                    for s in range(nsub):
                        P(lambda e: e.transpose(out=bank[0:PS, s * 32:(s + 1) * 32], in_=kr1[0:32, s * PS:(s + 1) * PS], identity=idf[0:32, 0:32]), [Tkr1, Tidf], [Tb])
                    V(lambda e: e.tensor_copy(out=kro[0:PS, 0:nsub, :], in_=bank[0:PS, 0:nsub * 32].rearrange("p (s d) -> p s d", d=32)), [Tb], [Tkro])
                    stq(kr_out[l, row0:row0 + T, :].rearrange("(s p) d -> p s d", p=PS), kro[0:PS, 0:nsub, :], "a9", [Tkro])

                    b0, Tb0 = pb[0]
                    b1, Tb1 = pb[1]
                    b2, Tb2 = pb[2]
                    b3, Tb3 = pb[3]
                    b4, Tb4 = pb[4]
                    b5, Tb5 = pb[5]
                    b6, Tb6 = pb[6]
                    pbT3 = b6[0:64, 0:256].rearrange("p (h t) -> p h t", h=4)

                    def PGmm(c, g):
                        c0 = c * CL
                        bi, ncols, wc0 = [(0, 512, OCQ), (1, 256, OGK), (2, 512, OGV), (3, 512, OGG)][g]
                        bank, Tb = pb[bi]
                        for kc in range(8):
                            P(lambda e: e.matmul(bank[0:CL, 0:ncols], lhsT=hT[:, kc, c0:c0 + CL], rhs=win[:, kc, wc0:wc0 + ncols], start=(kc == 0), stop=(kc == 7)), [ThT, Twin], [Tb])
                        if g == 1:
                            P(lambda e: e.matmul(b1[0:CL, 256:512], lhsT=gaT[0:16, c0:c0 + CL], rhs=wa2[0:16, :], start=True, stop=True), [TgaT, Twa2], [Tb1])

                    def PGev(c, g):
                        par = c % 2
                        if g == 0:
                            A(lambda e: e.activation(out=zsb_b[par][0][0:CL, :], in_=b0[0:CL, :], func=AF.Copy), [Tb0], [zsb_b[par][1]])
                        elif g == 1:
                            V(lambda e: e.tensor_copy(out=ksb_b[par][0][0:CL, :], in_=b1[0:CL, 0:256]), [Tb1], [ksb_b[par][1]])
                            V(lambda e: e.tensor_tensor(out=lgt_b[par][0][0:CL, :], in0=b1[0:CL, 256:512], in1=ba2[0:CL, :], op=ALU.add), [Tb1, Tba2], [lgt_b[par][1]])
                        elif g == 2:
                            A(lambda e: e.activation(out=vb_b[par][0][0:CL, :], in_=b2[0:CL, :], func=AF.Copy), [Tb2], [vb_b[par][1]])
                        else:
                            A(lambda e: e.activation(out=gsb_b[par][0][0:CL, :], in_=b3[0:CL, :], func=AF.Copy), [Tb3], [gsb_b[par][1]])

                    def LATa(c):
                        c0 = c * CL
                        par = c % 2
                        zsb, Tzsb = zsb_b[par]
                        nfc, Tnfc = nf_b[par]
                        A(lambda e: e.activation(out=sq[0:CL, :], in_=zsb[0:CL, :], func=AF.Square), [Tzsb], [Tsq])
                        V(lambda e: e.tensor_reduce(out=ss2[0:CL, :], in_=sq[0:CL, :].rearrange("p (g d) -> p g d", g=2), axis=AX.X, op=ALU.add), [Tsq], [Tss2])
                        rstd_inplace(ss2[0:CL, :], Tss2, 1.0 / 256)
                        V(lambda e: e.tensor_tensor(out=nfc[0:CL, :].rearrange("p (g d) -> p g d", g=2), in0=zsb[0:CL, :].rearrange("p (g d) -> p g d", g=2), in1=ss2[0:CL, :].unsqueeze(2).broadcast_to([CL, 2, 256]), op=ALU.mult), [Tzsb, Tss2], [Tnfc])
                        V(lambda e: e.tensor_tensor(out=nfc[0:CL, :], in0=nfc[0:CL, :], in1=gct[0:CL, :], op=ALU.mult), [Tnfc, Tgct], [Tnfc])
                        A(lambda e: e.activation(out=nb[0:CL, :], in_=nfc[0:CL, :], func=AF.Copy), [Tnfc], [Tnb])
                        stq(ckv_out[l, row0 + c0:row0 + c0 + CL, :], nfc[0:CL, 256:512], "a12_%d" % par, [Tnfc])

                    def LATb(c):
                        c0 = c * CL
                        pTl = pT[:, 0:512].rearrange("p (b t) -> p b t", b=4)
                        for b_ in range(4):
                            P(lambda e: e.transpose(out=pTl[:, b_, 0:CL], in_=nb[0:CL, b_ * 128:(b_ + 1) * 128], identity=idb[0:CL, 0:CL]), [Tnb, Tidb], [TpT])
                        V(lambda e: e.tensor_copy(out=latT[:, :, c0:c0 + CL], in_=pTl[:, :, 0:CL]), [TpT], [TlatT])

                    def SILU(c):
                        c0 = c * CL
                        par = c % 2
                        gsb, Tgsb = gsb_b[par]
                        sgc_, Tsgc_ = sgo_b[par]
                        A(lambda e: e.activation(out=sgt[0:CL, :], in_=gsb[0:CL, :], func=AF.Exp, scale=-1.0), [Tgsb], [Tsgt])
                        A(lambda e: e.activation(out=sgt[0:CL, :], in_=sgt[0:CL, :], func=AF.Ln, bias=1.0), [Tsgt], [Tsgt])
                        A(lambda e: e.activation(out=sgt[0:CL, :], in_=sgt[0:CL, :], func=AF.Exp, scale=-1.0), [Tsgt], [Tsgt])
                        V(lambda e: e.tensor_tensor(out=sgc_[0:CL, :], in0=gsb[0:CL, :], in1=sgt[0:CL, :], op=ALU.mult), [Tgsb, Tsgt], [Tsgc_])
                        stq(sgg[row0 + c0:row0 + c0 + CL, :], sgc_[0:CL, :], "a14_%d" % par, [Tsgc_], [Tdram])

                    def G1a(c):
                        par = c % 2
                        lgt, Tlgt = lgt_b[par]
                        A(lambda e: e.activation(out=lgt[0:CL, :], in_=lgt[0:CL, :], func=AF.Exp, scale=-1.0), [Tlgt], [Tlgt])
                        A(lambda e: e.activation(out=lgt[0:CL, :], in_=lgt[0:CL, :], func=AF.Ln, bias=1.0), [Tlgt], [Tlgt])
                        V(lambda e: e.tensor_scalar(out=lgt[0:CL, :], in0=lgt[0:CL, :], scalar1=-1.0 / 16.0, scalar2=None, op0=ALU.mult), [Tlgt], [Tlgt])

                    def G1b(c):
                        par = c % 2
                        lgt, Tlgt = lgt_b[par]
                        if is_s:
                            ld(SstX[:].rearrange("k (h v) -> k h v", h=4), sgla[l, c].rearrange("h k v -> k h v"), "a10", [TSstX])
                            V(lambda e: e.tensor_copy(out=SbfX[:], in_=SstX[:]), [TSstX], [TSbfX])
                        for h in range(4):
                            P(lambda e: e.matmul(pbT3[:, h, 0:CL], lhsT=lgt[0:CL, h * 64:(h + 1) * 64], rhs=tlt[0:CL, 0:CL], start=True, stop=True), [Tlgt, Ttlt], [Tb6])
                        P(lambda e: e.matmul(b6[0:CL, 256:512], lhsT=tut[0:CL, 0:CL], rhs=lgt[0:CL, :], start=True, stop=True), [Tlgt, Ttut], [Tb6])

                    def G2(c):
                        c0 = c * CL
                        par = c % 2
                        ksb, Tksb = ksb_b[par]
                        A(lambda e: e.activation(out=eb[:, :, 0:CL], in_=pbT3[:, :, 0:CL], func=AF.Exp), [Tb6], [Teb])
                        A(lambda e: e.activation(out=enb[:, :, 0:CL], in_=pbT3[:, :, 0:CL], func=AF.Exp, scale=-1.0), [Tb6], [Tenb])
                        A(lambda e: e.activation(out=ec[0:CL, :], in_=b6[0:CL, 256:512], func=AF.Exp), [Tb6], [Tec])
                        V(lambda e: e.tensor_tensor(out=qt[:, :, 0:CL], in0=gqT[:, :, c0:c0 + CL], in1=eb[:, :, 0:CL], op=ALU.mult), [TgqT, Teb], [Tqt])
                        V(lambda e: e.tensor_tensor(out=kt[:, :, 0:CL], in0=gkT[:, :, c0:c0 + CL], in1=enb[:, :, 0:CL], op=ALU.mult), [TgkT, Tenb], [Tkt])
                        if not is_s:
                            V(lambda e: e.tensor_tensor(out=Bc[:, :], in0=Bc[:, :], in1=pbT3[:, :, CL - 1], op=ALU.add), [TBc, Tb6], [TBc])
                        for h in range(4):
                            P(lambda e: e.matmul(b5[0:CL, h * 128:(h + 1) * 128], lhsT=qt[:, h, 0:CL], rhs=SbfX[:, h * 128:(h + 1) * 128], start=(h == 0), stop=False, skip_group_check=True), [Tqt, TSbfX], [Tb5])
                        pA3 = b4[0:64, 0:256].rearrange("p (h t) -> p h t", h=4)
                        for h in range(4):
                            P(lambda e: e.matmul(pA3[0:CL, h, 0:CL], lhsT=kt[:, h, 0:CL], rhs=qt[:, h, 0:CL], start=True, stop=True), [Tkt, Tqt], [Tb4])
                        V(lambda e: e.tensor_tensor(out=kh[0:CL, :], in0=ksb[0:CL, :], in1=ec[0:CL, :], op=ALU.mult), [Tksb, Tec], [Tkh])
                        if not is_s:
                            V(lambda e: e.tensor_tensor(out=qhat[:, :, c0:c0 + CL], in0=qt[:, :, 0:CL], in1=eBc[:, :].unsqueeze(2).broadcast_to([64, 4, CL]), op=ALU.mult), [Tqt, TeBc], [Tqhat])
                            A(lambda e: e.activation(out=eBc[:, :], in_=Bc[:, :], func=AF.Exp), [TBc], [TeBc])

                    def G3(c):
                        c0 = c * CL
                        par = c % 2
                        vb, Tvb = vb_b[par]
                        olc_, Tolc_ = ol_b[par]
                        pA3 = b4[0:64, 0:256].rearrange("p (h t) -> p h t", h=4)
                        V(lambda e: e.tensor_tensor(out=At[0:CL, :, 0:CL], in0=pA3[0:CL, :, 0:CL], in1=tlt[0:CL, 0:CL].unsqueeze(1).broadcast_to([CL, 4, CL]), op=ALU.mult), [Tb4, Ttlt], [TAt])
                        for h in range(4):
                            P(lambda e: e.matmul(b5[0:CL, h * 128:(h + 1) * 128], lhsT=At[0:CL, h, 0:CL], rhs=vb[0:CL, h * 128:(h + 1) * 128], start=False, stop=(h == 3), skip_group_check=True), [TAt, Tvb], [Tb5])
                        for h in range(4):
                            P(lambda e: e.matmul(b6[0:64, h * 128:(h + 1) * 128], lhsT=kh[0:CL, h * 64:(h + 1) * 64], rhs=vb[0:CL, h * 128:(h + 1) * 128], start=True, stop=True), [Tkh, Tvb], [Tb6])
                        A(lambda e: e.activation(out=olc_[0:CL, :], in_=b5[0:CL, :], func=AF.Copy), [Tb5], [Tolc_])
                        stq(oloc[row0 + c0:row0 + c0 + CL, :], olc_[0:CL, :], "a13_%d" % par, [Tolc_], [Tdram])

                    def G4(c):
                        for h in range(4):
                            V(lambda e: e.scalar_tensor_tensor(out=SstX[:, h * 128:(h + 1) * 128], in0=SstX[:, h * 128:(h + 1) * 128], scalar=eb[:, h, CL - 1:CL], in1=b6[0:64, h * 128:(h + 1) * 128], op0=ALU.mult, op1=ALU.add), [TSstX, Teb, Tb6], [TSstX])
                        if is_s:
                            stq(gla_out[l, 1 + c], SstX[:], "a11", [TSstX])
                        else:
                            V(lambda e: e.tensor_copy(out=SbfX[:], in_=SstX[:]), [TSstX], [TSbfX])

                    for g in range(4):
                        PGmm(0, g)
                        PGev(0, g)
                    G1a(0)
                    for c in range(nch):
                        nxt = c + 1 < nch
                        G1b(c)
                        if nxt:
                            PGmm(c + 1, 0)
                            PGmm(c + 1, 1)
                        if c >= 1:
                            LATb(c - 1)
                        G2(c)
                        if nxt:
                            PGev(c + 1, 0)
                            PGev(c + 1, 1)
                            G1a(c + 1)
                            PGmm(c + 1, 2)
                        G3(c)
                        if nxt:
                            PGev(c + 1, 2)
                            PGmm(c + 1, 3)
                        G4(c)
                        LATa(c)
                        SILU(c)
                        if nxt:
                            PGev(c + 1, 3)
                        if l == 0 and bg_budget[0] > 0:
                            bg_run(2)
                            bg_budget[0] -= 2
                        for fn_ in nhooks.pop(c, []):
                            fn_()
                    LATb(nch - 1)
                    for k_ in sorted(nhooks):
                        for fn_ in nhooks[k_]:
                            fn_()

                    if is_s:
                        stq(latS[0:512, 0:T].rearrange("(b p) t -> p b t", p=128), latT[:, :, 0:T], "a15", [TlatT], [Tdram])
                    else:
                        stq(latL[row0 // 512][0:512, 0:T].rearrange("(b p) t -> p b t", p=128), latT[:, :, 0:T], "a15", [TlatT], [TlatLb[row0 // 512]])
                        stq(qhT[:, :, row0:row0 + T], qhat[:, :, 0:T], "a16", [Tqhat], [Tdram])
                        ti_ = row0 // 512
                        deferred_ag.append((latL[ti_], latG[ti_], [TlatLa[ti_], TlatLb[ti_]], [TlatG[ti_]]))
                    if not is_s and row0 + T == NTOK:
                        V(lambda e: e.tensor_copy(out=sout[:, 0:512], in_=Sst[:]), [TSst], [Tsout])
                        V(lambda e: e.tensor_copy(out=sout[:, 512:516], in_=eBc[:, :]), [TeBc], [Tsout])
                        stq(glaLa[:, :], sout[:], "a17", [Tsout], [TglaL])
                        deferred_ag.append((glaLa, glaGa, [TglaL], [TglaG]))
                fw.barrier_all()

            fw.barrier_all()
            for args_ in deferred_ag:
                allgather(*args_)
            if l == 0:
                bg_run(12 * STEPS_PER_CHUNK)
            if l == 0:
                bg_run(max(0, bg_budget[0]))
            fw.barrier_all()
            if KSTOP == "A":
                return nc

            if KSTOP == "AG":
                return nc
            with ExitStack() as ph:
                KT, TKT = mk(ph, "sb", "KT", [96, SEQ], BF16)
                Vg, TVg = mk(ph, "sb", "Vg", [128, 128, 128], BF16)
                wuq, Twuq = mk(ph, "sb", "wuq", [128, 2, 2 * 96], BF16)
                wuk, Twuk = mk(ph, "sb", "wuk", [128, 2, 96], BF16)
                wuv, Twuv = mk(ph, "sb", "wuv", [128, 2, 64], BF16)
                wuq2, Twuq2 = mk(ph, "sb", "wuq2", [128, 2, 2 * 96], BF16)
                wuk2, Twuk2 = mk(ph, "sb", "wuk2", [128, 2, 96], BF16)
                wuv2, Twuv2 = mk(ph, "sb", "wuv2", [128, 2, 64], BF16)
                lt_bufs = [mk(ph, "sb", "ltb%d" % i, [128, 2, 512], BF16) for i in range(4)]
                cq_bufs = [mk(ph, "sb", "cqb%d" % i, [128, 2, 512], BF16) for i in range(2)]
                tC_bufs = [mk(ph, "sb", "tC%d" % i, [96, 512], F32) for i in range(2)]
                tS_bufs = [mk(ph, "sb", "tS%d" % i, [96, 512], F32) for i in range(2)]
                q1, Tq1 = mk(ph, "sb", "q1", [96, 512], F32)
                q2, Tq2 = mk(ph, "sb", "q2", [96, 512], F32)
                qb_bufs = [mk(ph, "sb", "qbb%d" % i, [96, 512], BF16) for i in range(2)]
                pt_bufs = [mk(ph, "sb", "ptb%d" % i, [128, 512], BF16) for i in range(4)]
                of_b = [mk(ph, "sb", "of%d" % i, [128, 512], F32) for i in range(2)]
                orc, Torc = mk(ph, "sb", "orc", [64, 512], F32)
                ob_b = [mk(ph, "sb", "ob%d" % i, [64, 512], BF16) for i in range(2)]
                ckt, Tckt = mk(ph, "sb", "ckt", [128, 16, 256], BF16)
                krp, Tkrp = mk(ph, "sb", "krp", [128, 17, 96], BF16)
                cT, TcT = mk(ph, "sb", "cT", [128, 2, 2080], BF16)
                cqS, TcqS = mk(ph, "sb", "cqS", [128, 2, NS], BF16)
                krS, TkrS = mk(ph, "sb", "krS", [32, NS], BF16)
                KTs, TKTs = mk(ph, "sb", "KTs", [96, 2080], BF16)
                Vs, TVs = mk(ph, "sb", "Vs", [128, 17, 128], BF16)
                pall_bufs = [mk(ph, "sb", "pall%d" % i, [128, 544], BF16) for i in range(2)]
                tCs, TtCs = mk(ph, "sb", "tCs", [96, NS], F32)
                tSs, TtSs = mk(ph, "sb", "tSs", [96, NS], F32)

                G(lambda e: e.memset(Vg[:, :, 64:128], 1.0), [], [TVg])
                G(lambda e: e.memset(Vs[:, :, 64:128], 1.0), [], [TVs])
                G(lambda e: e.memset(wuk[:], 0.0), [], [Twuk])
                G(lambda e: e.memset(wuk2[:], 0.0), [], [Twuk2])
                G(lambda e: e.memset(krp[:], 0.0), [], [Tkrp])
                for of_, Tof_ in of_b:
                    G(lambda e: e.memset(of_[:], 0.0), [], [Tof_])

                def make_q(wq_tile, Twq, wcol, cq_ap, Tcq, tCa, tSa, Ttabs, N, qb, Tqb):
                    br, Tbr = pb[4]
                    for kc in range(2):
                        P(lambda e: e.matmul(br[0:96, 0:N], lhsT=wq_tile[:, kc, wcol:wcol + 96], rhs=cq_ap(kc), start=(kc == 0), stop=(kc == 1)), [Twq, Tcq], [Tbr])
                    V(lambda e: e.tensor_copy(out=qb[0:64, 0:N], in_=br[0:64, 0:N]), [Tbr], [Tqb])
                    V(lambda e: e.tensor_tensor(out=q1[64:96, 0:N], in0=br[64:96, 0:N], in1=tCa, op=ALU.mult), [Tbr] + Ttabs, [Tq1])
                    for kc in range(2):
                        P(lambda e: e.matmul(br[0:96, 0:N], lhsT=wq_tile[:, kc, wcol + 96:wcol + 192], rhs=cq_ap(kc), start=(kc == 0), stop=(kc == 1)), [Twq, Tcq], [Tbr])
                    V(lambda e: e.tensor_tensor(out=q2[64:96, 0:N], in0=br[64:96, 0:N], in1=tSa, op=ALU.mult), [Tbr] + Ttabs, [Tq2])
                    V(lambda e: e.tensor_tensor(out=qb[64:96, 0:N], in0=q1[64:96, 0:N], in1=q2[64:96, 0:N], op=ALU.add), [Tq1, Tq2], [Tqb])

                def finish_o1(obank, Tobank, N, par):
                    of, Tof = of_b[par]
                    V(lambda e: e.tensor_copy(out=of[0:64, 0:N], in_=obank[0:64, 0:N]), [Tobank], [Tof])
                    V(lambda e: e.reciprocal(out=of[64:128, 0:N], in_=obank[64:128, 0:N]), [Tobank], [Tof])

                def finish_o2(N, par, dst_dram, Tdst=None):
                    of, Tof = of_b[par]
                    ob, Tob = ob_b[par]
                    bs, Tbs = pb[5]
                    P(lambda e: e.matmul(bs[0:64, 0:N], lhsT=smat[64:128, 0:64], rhs=of[64:128, 0:N], start=True, stop=True), [Tsmat, Tof], [Tbs])
                    V(lambda e: e.tensor_copy(out=orc[:, 0:N], in_=bs[0:64, 0:N]), [Tbs], [Torc])
                    V(lambda e: e.tensor_tensor(out=ob[:, 0:N], in0=of[0:64, 0:N], in1=orc[:, 0:N], op=ALU.mult), [Tof, Torc], [Tob])
                    stq(dst_dram, ob[:, 0:N], "b9_%d" % par, [Tob], [Tdst if Tdst is not None else Tdram])

                def finish_o(obank, Tobank, N, dst_dram, Tdst=None, par=0):
                    finish_o1(obank, Tobank, N, par)
                    finish_o2(N, par, dst_dram, Tdst)

                for hh in range(2):
                    ldc(wuq[:], w_uq_loc[l, :, hh].rearrange("(kc p) v d -> p kc (v d)", p=128), "b0", [Twuq])
                    ldc(wuk[:, :, 0:64], w_uk_loc[l, :, hh, :].rearrange("(kc p) d -> p kc d", p=128), "b1", [Twuk])
                    ldc(wuv[:], w_uv_loc[l, :, hh, :].rearrange("(kc p) d -> p kc d", p=128), "b2", [Twuv])
                    for t in range(SEQ // 512):
                        rk, ti = t // 8, t % 8
                        ld(KT[64:96, t * 512:(t + 1) * 512], latG[ti][rk * 544 + 512:rk * 544 + 544, :], "b3", [TKT], [TlatG[ti]])
                    for t in range(SEQ // 512):
                        rk, ti = t // 8, t % 8
                        ltb, Tltb = lt_bufs[t % 4]
                        ld(ltb[:], latG[ti][rk * 544 + 256:rk * 544 + 512, :].rearrange("(kc p) t -> p kc t", p=128), "b4_%d" % (t % 4), [Tltb], [TlatG[ti]])
                        bk, Tbk = pb[t % 2]
                        for kc in range(2):
                            P(lambda e: e.matmul(bk[0:64, :], lhsT=wuk[:, kc, 0:64], rhs=ltb[:, kc, :], start=(kc == 0), stop=(kc == 1)), [Twuk, Tltb], [Tbk])
                        A(lambda e: e.activation(out=KT[0:64, t * 512:(t + 1) * 512], in_=bk[0:64, :], func=AF.Copy), [Tbk], [TKT])
                        bv, Tbv = pb[2 + t % 2]
                        for j in range(4):
                            for kc in range(2):
                                P(lambda e: e.matmul(bv[:, j * 64:(j + 1) * 64], lhsT=ltb[:, kc, j * 128:(j + 1) * 128], rhs=wuv[:, kc, :], start=(kc == 0), stop=(kc == 1)), [Tltb, Twuv], [Tbv])
                        V(lambda e: e.tensor_copy(out=Vg[:, t * 4:(t + 1) * 4, 0:64], in_=bv[:, 0:256].rearrange("p (j d) -> p j d", j=4)), [Tbv], [TVg])
                    NQ = SEQ // 512
                    units = [(qi, kti) for qi in range(NQ) for kti in range(4 * qi + 4)]
                    LOOK = 2

                    def q_loads(qi):
                        rk, ti = qi // 8, qi % 8
                        cqb, Tcqb = cq_bufs[qi % 2]
                        tCb, TtCb = tC_bufs[qi % 2]
                        tSb, TtSb = tS_bufs[qi % 2]
                        ld(cqb[:], latG[ti][rk * 544:rk * 544 + 256, :].rearrange("(kc p) t -> p kc t", p=128), "b5_%d" % (qi % 2), [Tcqb], [TlatG[ti]])
                        ld(tCb[64:96, :], tabGC[64:96, qi * 512:(qi + 1) * 512], "b6_%d" % (qi % 2), [TtCb], [TtabG[qi // 2]])
                        ld(tSb[64:96, :], tabGS[64:96, qi * 512:(qi + 1) * 512], "b7_%d" % (qi % 2), [TtSb], [TtabG[qi // 2]])

                    def q_make(qi):
                        cqb, Tcqb = cq_bufs[qi % 2]
                        tCb, TtCb = tC_bufs[qi % 2]
                        tSb, TtSb = tS_bufs[qi % 2]
                        qb, Tqb = qb_bufs[qi % 2]
                        make_q(wuq, Twuq, 0, lambda kc: cqb[:, kc, :], Tcqb, tCb[64:96, :], tSb[64:96, :], [TtCb, TtSb], 512, qb, Tqb)

                    s_banks = [pb[2], pb[3], pb[6]]

                    def emit_qk(ui):
                        qi, kti = units[ui]
                        d = kti - 4 * qi
                        cs = 0 if d < 0 else d * 128
                        qb, Tqb = qb_bufs[qi % 2]
                        sbank, Tsbank = s_banks[ui % 3]
                        P(lambda e: e.matmul(sbank[:, cs:512], lhsT=KT[0:96, kti * 128:(kti + 1) * 128], rhs=qb[0:96, cs:512], start=True, stop=True), [TKT, Tqb], [Tsbank])

                    def emit_exp_pv(ui):
                        qi, kti = units[ui]
                        nkt = 4 * qi + 4
                        d = kti - 4 * qi
                        cs = 0 if d < 0 else d * 128
                        sbank, Tsbank = s_banks[ui % 3]
                        ptb, Tptb = pt_bufs[ui % 4]
                        obank, Tobank = pb[qi % 2]
                        A(lambda e: e.activation(out=ptb[:, cs:512], in_=sbank[:, cs:512], func=AF.Exp, scale=ATTN_SCALE), [Tsbank], [Tptb])
                        if d >= 0:
                            V(lambda e: e.memset(ptb[64:128, cs:cs + 64], 0.0), [], [Tptb])
                        P(lambda e: e.matmul(obank[:, cs:512], lhsT=Vg[:, kti, :], rhs=ptb[:, cs:512], start=(kti == 0), stop=(kti == nkt - 1), skip_group_check=True), [TVg, Tptb], [Tobank])

                    def do_finish1(fq):
                        finish_o1(pb[fq % 2][0], pb[fq % 2][1], 512, fq % 2)

                    def do_finish2(fq):
                        finish_o2(512, fq % 2, oL[fq // 8][hh * 64:(hh + 1) * 64, (fq % 8) * 512:(fq % 8 + 1) * 512], ToL[fq // 8])
                        if hh == 1 and fq % 8 == 7:
                            allgather(oL[fq // 8], oG[fq // 8], [ToL[fq // 8]], [ToG[fq // 8]])

                    if l == 0 and hh == 1:
                        bg_run(len(bg_steps))
                    q_loads(0)
                    q_loads(1)
                    q_make(0)
                    for ui in range(min(LOOK, len(units))):
                        emit_qk(ui)
                    pending = []
                    for ui, (qi, kti) in enumerate(units):
                        if kti == 0:
                            if qi + 2 < NQ:
                                q_loads(qi + 2)
                            if qi + 1 < NQ:
                                q_make(qi + 1)
                        if ui + LOOK < len(units):
                            emit_qk(ui + LOOK)
                        emit_exp_pv(ui)
                        if l == 0 and hh == 0 and ui >= 256 and ui % 2 == 0:
                            bg_run(1)
                        if kti == 4 * qi + 3:
                            pending.append((ui + 2, 0, qi))
                            pending.append((ui + 12, 1, qi))
                            pending.sort()
                        while pending and pending[0][0] <= ui:
                            _, kind, fq = pending.pop(0)
                            (do_finish1 if kind == 0 else do_finish2)(fq)
                    for _, kind, fq in sorted(pending):
                        (do_finish1 if kind == 0 else do_finish2)(fq)

                fw._need(fw.pool, ("cc", fw.dma_sems["cc"][1]))
                ld(cqS[:], latS[0:256, :].rearrange("(kc p) t -> p kc t", p=128), "s0", [TcqS], [Tdram])
                ld(tCs[64:96, :], tabLC[64:96, NTOK:NROW], "s1", [TtCs], [Tdram])
                ld(tSs[64:96, :], tabLS[64:96, NTOK:NROW], "s2", [TtSs], [Tdram])
                for s in range(2):
                    ldc(ckt[:], cckv[l, s].rearrange("(t p) d -> p t d", p=128), "s3", [Tckt])
                    ldc(krp[:, 0:16, 64:96], ckr[l, s].rearrange("(t p) d -> p t d", p=128), "s4", [Tkrp])
                    ld(cT[:, :, PAST:PAST + 32], latS[256:512, s * 32:(s + 1) * 32].rearrange("(kc p) t -> p kc t", p=128), "s5", [TcT], [Tdram])
                    ld(krS[:, 0:32], latS[512:544, s * 32:(s + 1) * 32], "s6", [TkrS], [Tdram])
                    pTc = pT[:].rearrange("p (b t) -> p b t", b=8)
                    for t4 in range(4):
                        for tt_ in range(4):
                            for kc in range(2):
                                P(lambda e: e.transpose(out=pTc[:, tt_ * 2 + kc, :], in_=ckt[:, t4 * 4 + tt_, kc * 128:(kc + 1) * 128], identity=idb[:]), [Tckt, Tidb], [TpT])
                        for kc in range(2):
                            V(lambda e: e.tensor_copy(out=cT[:, kc, t4 * 512:(t4 + 1) * 512].rearrange("p (t c) -> p t c", t=4), in_=pTc[:, kc::2, :]), [TpT], [TcT])
                    for h in range(8):
                        (swq, Tswq, swk, Tswk, swv, Tswv) = (wuq, Twuq, wuk, Twuk, wuv, Twuv) if h % 2 == 0 else (wuq2, Twuq2, wuk2, Twuk2, wuv2, Twuv2)
                        ldc(swq[:], w_uq_all[l, :, h].rearrange("(kc p) v d -> p kc (v d)", p=128), "b0_%d" % (h % 2), [Tswq])
                        ldc(swk[:, :, 0:64], w_uk_all[l, :, h, :].rearrange("(kc p) d -> p kc d", p=128), "b1_%d" % (h % 2), [Tswk])
                        ldc(swv[:], w_uv_all[l, :, h, :].rearrange("(kc p) d -> p kc d", p=128), "b2_%d" % (h % 2), [Tswv])
                        for t4 in range(4):
                            bk, Tbk = pb[t4 % 2]
                            for j in range(4):
                                tix = t4 * 4 + j
                                for kc in range(2):
                                    P(lambda e: e.matmul(bk[0:96, j * 128:(j + 1) * 128], lhsT=swk[:, kc, 0:96], rhs=cT[:, kc, tix * 128:(tix + 1) * 128], start=(kc == 0), stop=False, skip_group_check=True), [Tswk, TcT], [Tbk])
                                P(lambda e: e.matmul(bk[0:96, j * 128:(j + 1) * 128], lhsT=krp[:, tix, :], rhs=idb[:], start=False, stop=True, skip_group_check=True), [Tkrp, Tidb], [Tbk])
                            A(lambda e: e.activation(out=KTs[:, t4 * 512:(t4 + 1) * 512], in_=bk[0:96, :], func=AF.Copy), [Tbk], [TKTs])
                        bk, Tbk = pb[0]
                        for kc in range(2):
                            P(lambda e: e.matmul(bk[0:64, 0:32], lhsT=swk[:, kc, 0:64], rhs=cT[:, kc, PAST:PAST + 32], start=(kc == 0), stop=(kc == 1)), [Tswk, TcT], [Tbk])
                        A(lambda e: e.activation(out=KTs[0:64, PAST:PAST + 32], in_=bk[0:64, 0:32], func=AF.Copy), [Tbk], [TKTs])
                        fw.dma(sp, KTs[64:96, PAST:PAST + 32], latS[512:544, s * 32:(s + 1) * 32], "s7", reads=[Tdram], writes=[TKTs])
                        for t4 in range(5):
                            bv, Tbv = pb[2 + t4 % 2]
                            nt = 4 if t4 < 4 else 1
                            for j in range(nt):
                                tix = t4 * 4 + j
                                kp = 128 if tix < 16 else 32
                                for kc in range(2):
                                    P(lambda e: e.matmul(bv[0:kp, j * 64:(j + 1) * 64], lhsT=cT[:, kc, tix * 128:tix * 128 + kp], rhs=swv[:, kc, :], start=(kc == 0), stop=(kc == 1)), [TcT, Tswv], [Tbv])
                            if t4 < 4:
                                V(lambda e: e.tensor_copy(out=Vs[:, t4 * 4:(t4 + 1) * 4, 0:64], in_=bv[:, 0:256].rearrange("p (j d) -> p j d", j=4)), [Tbv], [TVs])
                            else:
                                V(lambda e: e.tensor_copy(out=Vs[0:32, 16, 0:64], in_=bv[0:32, 0:64]), [Tbv], [TVs])
                        qb, Tqb = qb_bufs[h % 2]
                        make_q(swq, Tswq, 0, lambda kc: cqS[:, kc, s * 32:(s + 1) * 32], TcqS, tCs[64:96, s * 32:(s + 1) * 32], tSs[64:96, s * 32:(s + 1) * 32], [TtCs, TtSs], 32, qb, Tqb)
                        obank, Tobank = pb[h % 2]
                        sA, TsA = pb[2]
                        sB, TsB = pb[3]
                        pall, Tpall = pall_bufs[h % 2]
                        for kti in range(16):
                            P(lambda e: e.matmul(sA[:, kti * 32:(kti + 1) * 32], lhsT=KTs[0:96, kti * 128:(kti + 1) * 128], rhs=qb[0:96, 0:32], start=True, stop=True, skip_group_check=True), [TKTs, Tqb], [TsA])
                        P(lambda e: e.matmul(sB[0:32, 0:32], lhsT=KTs[0:96, PAST:PAST + 32], rhs=qb[0:96, 0:32], start=True, stop=True), [TKTs, Tqb], [TsB])
                        A(lambda e: e.activation(out=pall[:, 0:512], in_=sA[:, :], func=AF.Exp, scale=ATTN_SCALE), [TsA], [Tpall])
                        A(lambda e: e.activation(out=pall[0:32, 512:544], in_=sB[0:32, 0:32], func=AF.Exp, scale=ATTN_SCALE), [TsB], [Tpall])
                        for kti in range(17):
                            kp = 128 if kti < 16 else 32
                            P(lambda e: e.matmul(obank[:, 0:32], lhsT=Vs[0:kp, kti, :], rhs=pall[0:kp, kti * 32:(kti + 1) * 32], start=(kti == 0), stop=(kti == 16)), [TVs, Tpall], [Tobank])
                        finish_o(obank, Tobank, 32, oS[h * 64:(h + 1) * 64, s * 32:(s + 1) * 32])
                fw.barrier_all()
            if l == 0:
                tg.close()

            if KSTOP == "B":
                return nc

            with ExitStack() as ph0:
                gg_, Tgg = mk(ph0, "sb", "ggl", [64, 4, 516], F32)
                Rs, TRs = mk(ph0, "sb", "Rs", [64, 512], F32)
                cf, Tcf = mk(ph0, "sb", "cf", [64, 4], F32)
                se, Tse = mk(ph0, "sb", "se", [64, 512], F32)
                ld(gg_[:], glaGa.rearrange("(j k) c -> k j c", j=4), "c14", [Tgg], [TglaG])
                V(lambda e: e.memset(Rs[:], 0.0), [], [TRs])
                for j in range(4):
                    V(lambda e: e.tensor_scalar(out=cf[:], in0=gg_[:, j, 512:516], scalar1=-1.0, scalar2=None, op0=ALU.add), [Tgg], [Tcf])
                    V(lambda e: e.tensor_scalar(out=cf[:], in0=cf[:], scalar1=sel[0:64, 4 + j:5 + j], scalar2=None, op0=ALU.mult), [Tcf, Tsel], [Tcf])
                    V(lambda e: e.tensor_scalar(out=cf[:], in0=cf[:], scalar1=1.0, scalar2=None, op0=ALU.add), [Tcf], [Tcf])
                    V(lambda e: e.tensor_tensor(out=Rs[:].rearrange("k (h v) -> k h v", h=4), in0=Rs[:].rearrange("k (h v) -> k h v", h=4), in1=cf[:, :].unsqueeze(2).broadcast_to([64, 4, 128]), op=ALU.mult), [TRs, Tcf], [TRs])
                    V(lambda e: e.scalar_tensor_tensor(out=Rs[:], in0=gg_[:, j, 0:512], scalar=sel[0:64, 4 + j:5 + j], in1=Rs[:], op0=ALU.mult, op1=ALU.add), [Tgg, Tsel, TRs], [TRs])
                V(lambda e: e.tensor_copy(out=Rb[:], in_=Rs[:]), [TRs], [TRb])
                V(lambda e: e.tensor_tensor(out=se[:].rearrange("k (h v) -> k h v", h=4), in0=Rs[:].rearrange("k (h v) -> k h v", h=4), in1=eBc[:, :].unsqueeze(2).broadcast_to([64, 4, 128]), op=ALU.mult), [TRs, TeBc], [Tse])
                V(lambda e: e.tensor_tensor(out=se[:], in0=se[:], in1=Sst[:], op=ALU.add), [Tse, TSst], [Tse])
                stq(gla_out[l, 0], se[:], "c15", [Tse])
                fw.barrier_all()

            with ExitStack() as ph:
                wo, Two = mk(ph, "sb", "wo", [128, 8, D], BF16)
                wd, Twd = mk(ph, "sb", "wd", [128, 22, D], BF16)
                NWG = 4
                wg_bufs = [mk(ph, "sb", "wgb%d" % i, [128, 8, 256], BF16) for i in range(NWG)]
                gft, Tgft = mk(ph, "sb", "gft", [128, D], F32)
                gon, Tgon = mk(ph, "sb", "gon", [128, 128], F32)
                xt_b = [mk(ph, "sb", "xtc%d" % i, [128, 4, D], F32)[0] for i in range(2)]
                Txt_b = [[Trk("xt%d_%d" % (i, s_)) for s_ in range(4)] for i in range(2)]
                cand_bufs = [mk(ph, "sb", "cand%d" % i, [128, 4, 512], BF16) for i in range(2)]
                cat_b = [mk(ph, "sb", "catT%d" % i, [128, 8, 512], BF16)[0] for i in range(2)]
                Tcat_b = [[Trk("cat%d_%d" % (i, s_)) for s_ in range(4)] for i in range(2)]
                hid, Thid = mk(ph, "sb", "hid", [128, 22, 512], BF16)
                olc, Tolc = mk(ph, "sb", "olc", [128, 512], F32)
                sgc, Tsgc = mk(ph, "sb", "sgc", [128, 512], F32)
                qhc_b = [mk(ph, "sb", "qhc%d" % i, [64, 4, 512], BF16) for i in range(2)]
                sq, Tsq = mk(ph, "sb", "sqc", [128, 512], F32)
                ss4, Tss4 = mk(ph, "sb", "ss4", [128, 4], F32)
                ogb_b = [mk(ph, "sb", "ogb%d" % i, [128, 512], BF16) for i in range(2)]
                junk, Tjunk = mk(ph, "sb", "junkc", [128, D], BF16)
                ss, Tss = mk(ph, "sb", "ssc", [128, 1], F32)
                hb_b = [mk(ph, "sb", "hbc%d" % i, [128, D], BF16) for i in range(2)]
                sa_bufs = [mk(ph, "sb", "sa%d" % i, [128, 512], F32) for i in range(2)]
                yt_b = [mk(ph, "sb", "yt%d" % i, [128, D], F32) for i in range(2)]

                ld(wo[:], woB[l].rearrange("(kc p) n -> p kc n", p=128), "c10", [Two], [Twc])
                ld(wd[:], wdB[l].rearrange("(kc p) n -> p kc n", p=128), "c11", [Twd], [Twc])
                ld(gft[:], g_ffn[l, 0:1, :].partition_broadcast(128), "c12", [Tgft])
                ld(gon[:], g_on[l, 0:1, :].partition_broadcast(128), "c13", [Tgon])

                wg_ctr = [0]
                NT = len(TILES)

                def P_load(i):
                    row0, T, PS, CL, is_s = TILES[i]
                    nsub = T // PS
                    src = x_in if l == 0 else xs
                    xt = xt_b[i % 2]
                    ld(xt[0:PS, 0:nsub, :], src[row0:row0 + T, :].rearrange("(s p) d -> p s d", p=PS), "c16_%d" % (i % 2), Txt_b[i % 2][0:nsub], [Tdram])
                    if not is_s:
                        qhc, Tqhc = qhc_b[i % 2]
                        ld(qhc[:, :, 0:T], qhT[:, :, row0:row0 + T], "c19_%d" % (i % 2), [Tqhc], [Tdram])

                def P_select(i):
                    row0, T, PS, CL, is_s = TILES[i]
                    nsub = T // PS
                    catT = cat_b[i % 2]
                    Tc = Tcat_b[i % 2][0:nsub]
                    if is_s:
                        ld(catT[:, 0:4, 0:T], oS.rearrange("(kc p) t -> p kc t", p=128), "c17", Tc, [Tdram])
                    else:
                        for j in range(4):
                            cand, Tcand = cand_bufs[j % 2]
                            ld(cand[:], oG[j][:, row0:row0 + T].rearrange("(kc p) t -> p kc t", p=128), "c18_%d" % (j % 2), [Tcand], [ToG[j]])
                            if j == 0:
                                V(lambda e: e.tensor_scalar(out=catT[:, 0:4, :], in0=cand[:], scalar1=sel[:, 0:1], scalar2=None, op0=ALU.mult), [Tcand, Tsel], Tc)
                            else:
                                V(lambda e: e.scalar_tensor_tensor(out=catT[:, 0:4, :], in0=cand[:], scalar=sel[:, j:j + 1], in1=catT[:, 0:4, :], op0=ALU.mult, op1=ALU.add), [Tcand, Tsel] + Tc, Tc)

                def P1(i, s):
                    row0, T, PS, CL, is_s = TILES[i]
                    r0 = row0 + s * PS
                    ogb, Togb = ogb_b[s % 2]
                    ld(olc[0:PS, :], oloc[r0:r0 + PS, :], "c20", [Tolc], [Tdram])
                    ld(sgc[0:PS, :], sgg[r0:r0 + PS, :], "c21", [Tsgc], [Tdram])
                    if not is_s:
                        qhc, Tqhc = qhc_b[i % 2]
                        bc_, Tbc_ = pb[6]
                        for h in range(4):
                            P(lambda e: e.matmul(bc_[0:PS, h * 128:(h + 1) * 128], lhsT=qhc[:, h, s * PS:(s + 1) * PS], rhs=Rb[:, h * 128:(h + 1) * 128], start=True, stop=True), [Tqhc, TRb], [Tbc_])
                        V(lambda e: e.tensor_tensor(out=olc[0:PS, :], in0=olc[0:PS, :], in1=bc_[0:PS, :], op=ALU.add), [Tolc, Tbc_], [Tolc])
                    V(lambda e: e.tensor_tensor(out=sq[0:PS, :], in0=olc[0:PS, :], in1=olc[0:PS, :], op=ALU.mult), [Tolc], [Tsq])
                    V(lambda e: e.tensor_reduce(out=ss4[0:PS, :], in_=sq[0:PS, :].rearrange("p (h d) -> p h d", h=4), axis=AX.X, op=ALU.add), [Tsq], [Tss4])
                    rstd_inplace(ss4[0:PS, :], Tss4, 1.0 / 128)
                    V(lambda e: e.tensor_tensor(out=olc[0:PS, :].rearrange("p (h d) -> p h d", h=4), in0=olc[0:PS, :].rearrange("p (h d) -> p h d", h=4), in1=ss4[0:PS, :].unsqueeze(2).broadcast_to([PS, 4, 128]), op=ALU.mult), [Tolc, Tss4], [Tolc])
                    V(lambda e: e.tensor_tensor(out=sgc[0:PS, :].rearrange("p (h d) -> p h d", h=4), in0=sgc[0:PS, :].rearrange("p (h d) -> p h d", h=4), in1=gon[0:PS, :].unsqueeze(1).broadcast_to([PS, 4, 128]), op=ALU.mult), [Tsgc, Tgon], [Tsgc])
                    V(lambda e: e.tensor_tensor(out=ogb[0:PS, :], in0=olc[0:PS, :], in1=sgc[0:PS, :], op=ALU.mult), [Tolc, Tsgc], [Togb])

                def P2(i, s):
                    row0, T, PS, CL, is_s = TILES[i]
                    ogb, Togb = ogb_b[s % 2]
                    catT = cat_b[i % 2]
                    pTl = pT[:, 0:512].rearrange("p (b t) -> p b t", b=4)
                    for b_ in range(4):
                        P(lambda e: e.transpose(out=pTl[:, b_, 0:PS], in_=ogb[0:PS, b_ * 128:(b_ + 1) * 128], identity=idb[0:PS, 0:PS]), [Togb, Tidb], [TpT])
                    V(lambda e: e.tensor_copy(out=catT[:, 4:8, s * PS:(s + 1) * PS], in_=pTl[:, :, 0:PS]), [TpT], [Tcat_b[i % 2][s]])

                def M(i, hooks):
                    row0, T, PS, CL, is_s = TILES[i]
                    nsub = T // PS
                    xt = xt_b[i % 2]
                    Txs = Txt_b[i % 2]
                    catT = cat_b[i % 2]
                    Tcs = Tcat_b[i % 2]
                    h2T = catT
                    pTv = pT[:].rearrange("p (c t) -> p c t", c=8)

                    def WO(s):
                        for n in range(2):
                            bo, Tbo = pb[(2 * s + n) % 4]
                            for kc in range(8):
                                P(lambda e: e.matmul(bo[0:PS, :], lhsT=catT[:, kc, s * PS:(s + 1) * PS], rhs=wo[:, kc, n * 512:(n + 1) * 512], start=(kc == 0), stop=(kc == 7)), [Tcs[s], Two], [Tbo])
                            V(lambda e: e.tensor_tensor(out=xt[0:PS, s, n * 512:(n + 1) * 512], in0=xt[0:PS, s, n * 512:(n + 1) * 512], in1=bo[0:PS, :], op=ALU.add), [Txs[s], Tbo], [Txs[s]])

                    def N_(s):
                        hb, Thb = hb_b[s % 2]
                        A(lambda e: e.activation(out=junk[0:PS, :], in_=xt[0:PS, s, :], func=AF.Square, accum_out=ss[0:PS, 0:1]), [Txs[s]], [Tjunk, Tss])
                        rstd_inplace(ss[0:PS, 0:1], Tss, 1.0 / D)
                        V(lambda e: e.scalar_tensor_tensor(out=hb[0:PS, :], in0=xt[0:PS, s, :], scalar=ss[0:PS, 0:1], in1=gft[0:PS, :], op0=ALU.mult, op1=ALU.mult), [Txs[s], Tss, Tgft], [Thb])

                    def T_(s):
                        hb, Thb = hb_b[s % 2]
                        for c in range(8):
                            P(lambda e: e.transpose(out=pTv[:, c, 0:PS], in_=hb[0:PS, c * 128:(c + 1) * 128], identity=idb[0:PS, 0:PS]), [Thb, Tidb], [TpT])
                        A(lambda e: e.activation(out=h2T[:, :, s * PS:(s + 1) * PS], in_=pTv[:, :, 0:PS], func=AF.Copy), [TpT], [Tcs[s]])

                    for s in range(nsub):
                        WO(s)
                        if s >= 1:
                            N_(s - 1)
                        if s >= 2:
                            T_(s - 2)
                    N_(nsub - 1)
                    if nsub >= 2:
                        T_(nsub - 2)
                    T_(nsub - 1)
                    for m in range(22):
                        wgb, Twgb = wg_bufs[wg_ctr[0] % NWG]
                        fw.dma(pool, wgb[:], wguB[l][m].rearrange("p (kc c) -> p kc c", kc=8), "c22_%d" % (wg_ctr[0] % NWG), reads=[Twc], writes=[Twgb])
                        wg_ctr[0] += 1
                        ba, Tba = pb[4 + (m % 2)]
                        bu, Tbu = pb[2 * (m % 2)]
                        for kc in range(8):
                            P(lambda e: e.matmul(ba[:, 0:T], lhsT=wgb[:, kc, 0:128], rhs=h2T[:, kc, 0:T], start=(kc == 0), stop=(kc == 7)), [Twgb] + Tcs[0:nsub], [Tba])
                        for kc in range(8):
                            P(lambda e: e.matmul(bu[:, 0:T], lhsT=wgb[:, kc, 128:256], rhs=h2T[:, kc, 0:T], start=(kc == 0), stop=(kc == 7)), [Twgb] + Tcs[0:nsub], [Tbu])
                        sa, Tsa = sa_bufs[m % 2]
                        A(lambda e: e.activation(out=sa[:, 0:T], in_=ba[:, 0:T], func=AF.Silu), [Tba], [Tsa])
                        V(lambda e: e.tensor_tensor(out=hid[:, m, 0:T], in0=sa[:, 0:T], in1=bu[:, 0:T], op=ALU.mult), [Tsa, Tbu], [Thid])
                        for fn in hooks.get(m, []):
                            fn()
                    for s in range(nsub):
                        for n in range(2):
                            bo, Tbo = pb[1 + 2 * ((2 * s + n) % 2)]
                            for m in range(22):
                                P(lambda e: e.matmul(bo[0:PS, :], lhsT=hid[:, m, s * PS:(s + 1) * PS], rhs=wd[:, m, n * 512:(n + 1) * 512], start=(m == 0), stop=(m == 21)), [Thid, Twd], [Tbo])
                            V(lambda e: e.tensor_tensor(out=xt[0:PS, s, n * 512:(n + 1) * 512], in0=xt[0:PS, s, n * 512:(n + 1) * 512], in1=bo[0:PS, :], op=ALU.add), [Txs[s], Tbo], [Txs[s]])
                        if s == 0:
                            for fn in hooks.get(22, []):
                                fn()
                    if l == 0:
                        stq(xs[row0:row0 + T, :].rearrange("(s p) d -> p s d", p=PS), xt[0:PS, 0:nsub, :], "c23_%d" % (i % 2), Txs[0:nsub], [Tdram])
                    else:
                        for s in range(nsub):
                            yt, Tyt = yt_b[s % 2]
                            A(lambda e: e.activation(out=junk[0:PS, :], in_=xt[0:PS, s, :], func=AF.Square, accum_out=ss[0:PS, 0:1]), [Txs[s]], [Tjunk, Tss])
                            rstd_inplace(ss[0:PS, 0:1], Tss, 1.0 / D)
                            V(lambda e: e.scalar_tensor_tensor(out=yt[0:PS, :], in0=xt[0:PS, s, :], scalar=ss[0:PS, 0:1], in1=gfin[0:PS, :], op0=ALU.mult, op1=ALU.mult), [Txs[s], Tss, Tgfin], [Tyt])
                            stq(y_out[row0 + s * PS:row0 + (s + 1) * PS, :], yt[0:PS, :], "c24_%d" % (s % 2), [Tyt])

                P_load(0)
                P_select(0)
                for s in range(TILES[0][1] // TILES[0][2]):
                    P1(0, s)
                    P2(0, s)
                for i in range(NT):
                    hooks = {}
                    if i + 1 < NT:
                        nsub_n = TILES[i + 1][1] // TILES[i + 1][2]
                        hooks.setdefault(0, []).append(lambda i=i: P_load(i + 1))
                        hooks.setdefault(1, []).append(lambda i=i: P_select(i + 1))
                        for s in range(nsub_n):
                            hooks.setdefault(3 + 5 * s, []).append(lambda i=i, s=s: P1(i + 1, s))
                            hooks.setdefault(3 + 5 * s + 4, []).append(lambda i=i, s=s: P2(i + 1, s))
                    M(i, hooks)
                fw.barrier_all()

        fw.barrier_all()
        print("[kernel] instructions:", fw.n_instr, "semaphores:", fw.nsem)
    return nc


_NC_CACHE = {}


def _f32(a):
    return np.ascontiguousarray(np.asarray(a, dtype=np.float32))


def kernel(x_prompt, x_sample, cache_ckv, cache_krope, state_gla,
           g_attn, w_in, g_qn, w_uq, g_kvn, w_ukv, w_a2, b_a2, g_gla_on, w_o,
           g_ffn, w_gu, w_down, g_final):
    x_prompt = _f32(x_prompt); x_sample = _f32(x_sample)
    cache_ckv = _f32(cache_ckv); cache_krope = _f32(cache_krope); state_gla = _f32(state_gla)
    w_in = _f32(w_in); w_uq = _f32(w_uq); w_ukv = _f32(w_ukv); w_gu = _f32(w_gu)

    perm = np.concatenate([np.arange(16, 32), np.arange(0, 16)])
    w_in_x = np.concatenate([w_in, w_in[:, :, OKR:OKR + 32][:, :, perm]], axis=2)
    wq = w_uq.reshape(2, 256, 8, 96)
    wq_raw = wq
    wq_perm = np.concatenate([wq[..., :64], wq[..., 64:][..., perm]], axis=-1)
    w_uq_all = np.ascontiguousarray(np.stack([wq_raw, wq_perm], axis=3))
    wkv = w_ukv.reshape(2, 256, 8, 128)
    w_uk_all = np.ascontiguousarray(wkv[..., :64])
    w_uv_all = np.ascontiguousarray(wkv[..., 64:])
    wa = w_gu[:, :, :DFF].reshape(2, 8, 128, 22, 128)
    wu = w_gu[:, :, DFF:].reshape(2, 8, 128, 22, 128)
    w_gu_t = np.ascontiguousarray(np.concatenate([wa, wu], axis=-1).transpose(0, 3, 2, 1, 4))
    ident = np.eye(128, dtype=np.float32)
    jj, ii = np.meshgrid(np.arange(64), np.arange(64), indexing="ij")
    trilt = (jj <= ii).astype(np.float32)
    triut = (jj > ii).astype(np.float32)
    selmat = np.zeros((128, 64), np.float32)
    selmat[64 + np.arange(64), np.arange(64)] = 1.0
    half = 16
    inv = (10000.0 ** (-np.arange(half, dtype=np.float32) / half)).astype(np.float32)
    ropec = np.zeros((96, 2), np.float32)
    for r_ in range(96):
        ropec[r_, 0] = inv[r_ % 16]
        ropec[r_, 1] = -1.0 if (r_ % 32) < 16 else 1.0
    pos_g = np.arange(SEQ, dtype=np.float32)[None, :]
    g_cat = np.concatenate([_f32(g_qn), _f32(g_kvn)], axis=1)[:, None, :]

    shared = {
        "w_in": w_in_x, "w_uq_all": w_uq_all, "w_uk_all": w_uk_all, "w_uv_all": w_uv_all,
        "w_a2": _f32(w_a2), "b_a2": _f32(b_a2)[:, None, :], "g_attn": _f32(g_attn)[:, None, :],
        "g_ffn": _f32(g_ffn)[:, None, :], "g_final": _f32(g_final)[None, :], "g_cat": _f32(g_cat),
        "g_on": _f32(g_gla_on)[:, None, :], "w_o": _f32(w_o), "w_gu": w_gu_t, "w_dn": _f32(w_down),
        "ident": ident, "trilt": trilt, "triut": triut, "pos_g": pos_g, "ropec": ropec, "selmat": selmat,
    }
    in_maps = []
    for c in range(8):
        g, r = c // 4, c % 4
        xs_ = np.concatenate([x_prompt[g, r * NTOK:(r + 1) * NTOK], x_sample[2 * c:2 * c + 2].reshape(NS, D)], axis=0)
        selm = np.zeros((128, 8), np.float32)
        selm[:, r] = 1.0
        for j in range(4):
            selm[:, 4 + j] = 1.0 if j < r else 0.0
        pos_l = np.concatenate([np.arange(r * NTOK, (r + 1) * NTOK), PAST + np.arange(32), PAST + np.arange(32)]).astype(np.float32)[None, :]
        m = dict(shared)
        m.update({
            "x": np.ascontiguousarray(xs_),
            "cckv": np.ascontiguousarray(cache_ckv[:, 2 * c:2 * c + 2]),
            "ckr": np.ascontiguousarray(cache_krope[:, 2 * c:2 * c + 2]),
            "sgla": np.ascontiguousarray(state_gla[:, 2 * c:2 * c + 2]),
            "w_uq_loc": np.ascontiguousarray(w_uq_all[:, :, 2 * r:2 * r + 2]),
            "w_uk_loc": np.ascontiguousarray(w_uk_all[:, :, 2 * r:2 * r + 2]),
            "w_uv_loc": np.ascontiguousarray(w_uv_all[:, :, 2 * r:2 * r + 2]),
            "selm": selm, "pos_l": pos_l,
        })
        in_maps.append(m)

    if "nc" not in _NC_CACHE:
        _NC_CACHE["nc"] = build_program()
    res = run_bass_kernel_spmd(_NC_CACHE["nc"], in_maps, core_ids=list(range(8)))
    R = res.results

    y_p = np.zeros((2, SEQ, D), np.float32)
    y_s = np.zeros((16, 32, D), np.float32)
    ckv_p = np.zeros((2, 2, SEQ, 256), np.float32)
    kr_p = np.zeros((2, 2, SEQ, 32), np.float32)
    gla_p = np.zeros((2, 2, 4, 64, 128), np.float32)
    ckv_s = np.zeros((2, 16, 32, 256), np.float32)
    kr_s = np.zeros((2, 16, 32, 32), np.float32)
    gla_s = np.zeros((2, 16, 4, 64, 128), np.float32)
    for c in range(8):
        g, r = c // 4, c % 4
        o = R[c]
        y_p[g, r * NTOK:(r + 1) * NTOK] = o["y"][:NTOK]
        y_s[2 * c:2 * c + 2] = o["y"][NTOK:].reshape(2, 32, D)
        ckv_p[:, g, r * NTOK:(r + 1) * NTOK] = o["ckv_o"][:, :NTOK]
        kr_p[:, g, r * NTOK:(r + 1) * NTOK] = o["kr_o"][:, :NTOK]
        ckv_s[:, 2 * c:2 * c + 2] = o["ckv_o"][:, NTOK:].reshape(2, 2, 32, 256)
        kr_s[:, 2 * c:2 * c + 2] = o["kr_o"][:, NTOK:].reshape(2, 2, 32, 32)
        gl = o["gla_o"].reshape(2, 3, 64, 4, 128).transpose(0, 1, 3, 2, 4)
        if r == 3:
            gla_p[:, g] = gl[:, 0]
        gla_s[:, 2 * c] = gl[:, 1]
        gla_s[:, 2 * c + 1] = gl[:, 2]
    return (y_p, y_s, ckv_p, kr_p, gla_p, ckv_s, kr_s, gla_s)
```

```python
import math
from contextlib import ExitStack

import numpy as np
import concourse.bass as bass
import concourse.mybir as mybir
from concourse.bass_utils import run_bass_kernel_spmd

F32 = mybir.dt.float32
BF16 = mybir.dt.bfloat16
I32 = mybir.dt.int32
AF = mybir.ActivationFunctionType
ALU = mybir.AluOpType
AX = mybir.AxisListType

D = 1024
SEQ = 16384
NTOK = 4096
NS = 64
NROW = NTOK + NS
PAST = 2048
DFF = 2816
EPS = 1e-6
ATTN_SCALE = 96 ** -0.5
OCQ, OCKV, OKR, OGQ, OGK, OGV, OGG, OGA, OKRP = 0, 256, 512, 544, 800, 1056, 1568, 2080, 2096
WIN = 2128
TWO_PI = 2.0 * math.pi
C1 = 6.28125
C2 = TWO_PI - C1


class Trk:
    __slots__ = ("name", "w", "r")

    def __init__(self, name):
        self.name = name
        self.w = None
        self.r = []


class Eng:
    def __init__(self, fw, name, eng):
        self.name = name
        self.eng = eng
        self.sem = fw.new_sem("e_" + name)
        self.count = 0
        self.waited = {}


class FW:
    def __init__(self, nc, stack):
        self.nc = nc
        self.stack = stack
        self.nsem = 0
        self.pe = Eng(self, "pe", nc.tensor)
        self.dve = Eng(self, "dve", nc.vector)
        self.act = Eng(self, "act", nc.scalar)
        self.pool = Eng(self, "pool", nc.gpsimd)
        self.sp = Eng(self, "sp", nc.sync)
        self.engs = [self.pe, self.dve, self.act, self.pool, self.sp]
        self.dma_sems = {}
        self.n_instr = 0

    def new_sem(self, name):
        self.nsem += 1
        return self.stack.enter_context(self.nc.semaphore(name))

    def _need(self, E, dep):
        if dep is None:
            return
        key, val = dep
        if E.waited.get(key, 0) >= val:
            return
        sem = key.sem if isinstance(key, Eng) else self.dma_sems[key][0]
        E.eng.wait_ge(sem, val)
        E.waited[key] = val
        self.n_instr += 1

    def _deps(self, E, reads, writes, is_dma=False):
        reads = [t for t in reads if t.name != "dram"]
        writes = [t for t in writes if t.name != "dram"]
        for t in reads:
            if t.w is not None and (is_dma or not (t.w[0] is E and E is self.pe)):
                self._need(E, t.w)
        for t in writes:
            if t.w is not None and (is_dma or t.w[0] is not E):
                self._need(E, t.w)
            for r in t.r:
                if is_dma or r[0] is not E:
                    self._need(E, r)

    def _mark(self, stamp, reads, writes):
        reads = [t for t in reads if t.name != "dram"]
        writes = [t for t in writes if t.name != "dram"]
        for t in reads:
            t.r.append(stamp)
            if len(t.r) > 8:
                last = {}
                for k, v in t.r:
                    if last.get(k, 0) < v:
                        last[k] = v
                t.r = list(last.items())
        for t in writes:
            t.w = stamp
            t.r = []

    def op(self, E, fn, reads=(), writes=()):
        self._deps(E, reads, writes)
        ins = fn(E.eng)
        E.count += 1
        ins.then_inc(E.sem, 1)
        self._mark((E, E.count), reads, writes)
        self.n_instr += 1
        return ins

    def dma(self, Q, out, in_, key, reads=(), writes=()):
        if key not in self.dma_sems:
            self.dma_sems[key] = [self.new_sem("d_" + key), 0, 16]
        self._deps(Q, reads, writes, is_dma=True)
        ent = self.dma_sems[key]
        ins = Q.eng.dma_start(out=out, in_=in_)
        ent[1] += 16
        ins.then_inc(ent[0], 16)
        self._mark((key, ent[1]), reads, writes)
        self.n_instr += 1
        return ins

    def coll(self, fn, key, reads=(), writes=()):
        if key not in self.dma_sems:
            self.dma_sems[key] = [self.new_sem("c_" + key), 0, 1]
        Q = self.pool
        self._deps(Q, reads, writes, is_dma=True)
        ent = self.dma_sems[key]
        ins = fn(Q.eng)
        ent[1] += 1
        ins.then_inc(ent[0], 1)
        self._mark((key, ent[1]), reads, writes)
        self.n_instr += 1
        return ins

    def barrier_all(self):
        for E in self.engs:
            for P in self.engs:
                if P.count > 0 and P is not E:
                    self._need(E, (P, P.count))
            for key, ent in self.dma_sems.items():
                if ent[1] > 0:
                    self._need(E, (key, ent[1]))


import os
KSTOP = os.environ.get("KSTOP", "")


class _Stop(Exception):
    pass


def build_program():
    nc = bass.Bass("TRN2", target_bir_lowering=False)

    def din(name, shape, dt=F32):
        return nc.dram_tensor(name, list(shape), dt, kind="ExternalInput").ap()

    def dout(name, shape, dt=F32):
        return nc.dram_tensor(name, list(shape), dt, kind="ExternalOutput").ap()

    def dscr(name, shape, dt):
        return nc.dram_tensor(name, list(shape), dt)

    x_in = din("x", [NROW, D])
    cckv = din("cckv", [2, 2, PAST, 256])
    ckr = din("ckr", [2, 2, PAST, 32])
    sgla = din("sgla", [2, 2, 4, 64, 128])
    w_in = din("w_in", [2, D, WIN])
    w_uq_loc = din("w_uq_loc", [2, 256, 2, 2, 96])
    w_uq_all = din("w_uq_all", [2, 256, 8, 2, 96])
    w_uk_loc = din("w_uk_loc", [2, 256, 2, 64])
    w_uv_loc = din("w_uv_loc", [2, 256, 2, 64])
    w_uk_all = din("w_uk_all", [2, 256, 8, 64])
    w_uv_all = din("w_uv_all", [2, 256, 8, 64])
    w_a2 = din("w_a2", [2, 16, 256])
    b_a2 = din("b_a2", [2, 1, 256])
    g_attn = din("g_attn", [2, 1, D])
    g_ffn = din("g_ffn", [2, 1, D])
    g_final = din("g_final", [1, D])
    g_cat = din("g_cat", [2, 1, 512])
    g_on = din("g_on", [2, 1, 128])
    w_o = din("w_o", [2, D, D])
    w_gu = din("w_gu", [2, 22, 128, 8, 256])
    w_dn = din("w_dn", [2, DFF, D])
    ident = din("ident", [128, 128])
    trilt = din("trilt", [64, 64])
    triut = din("triut", [64, 64])
    selm = din("selm", [128, 8])
    pos_g = din("pos_g", [1, SEQ])
    pos_l = din("pos_l", [1, NROW])
    ropec = din("ropec", [96, 2])
    selmat = din("selmat", [128, 64])

    y_out = dout("y", [NROW, D])
    ckv_out = dout("ckv_o", [2, NROW, 256])
    kr_out = dout("kr_o", [2, NROW, 32])
    gla_out = dout("gla_o", [2, 3, 64, 512])

    xs = dscr("xs", [NROW, D], F32).ap()
    latL = [dscr("latL%d" % i, [544, 512], BF16).ap() for i in range(8)]
    latG = [dscr("latG%d" % i, [4 * 544, 512], BF16).ap() for i in range(8)]
    latS = dscr("latS", [544, NS], BF16).ap()
    glaL = dscr("glaL", [64, 516], F32)
    glaG = dscr("glaG", [256, 516], F32)
    oloc = dscr("oloc", [NROW, 512], F32).ap()
    qhT = dscr("qhT", [64, 4, NROW], BF16).ap()
    sgg = dscr("sgg", [NROW, 512], F32).ap()
    oL = [dscr("oL%d" % i, [128, NTOK], BF16).ap() for i in range(4)]
    oG = [dscr("oG%d" % i, [512, NTOK], BF16).ap() for i in range(4)]
    oS = dscr("oS", [512, NS], BF16).ap()
    tabGC = dscr("tabGC", [96, SEQ], F32).ap()
    tabGS = dscr("tabGS", [96, SEQ], F32).ap()
    tabLC = dscr("tabLC", [96, NROW], F32).ap()
    tabLS = dscr("tabLS", [96, NROW], F32).ap()
    glaLa, glaGa = glaL.ap(), glaG.ap()
    wguB = [dscr("wguB%d" % i, [22, 128, 2048], BF16).ap() for i in range(2)]
    woB = [dscr("woB%d" % i, [D, D], BF16).ap() for i in range(2)]
    wdB = [dscr("wdB%d" % i, [DFF, D], BF16).ap() for i in range(2)]

    GROUPS = [[0, 1, 2, 3], [4, 5, 6, 7]]

    with ExitStack() as top:
        fw = FW(nc, top)
        pe, dve, act, pool, sp = fw.pe, fw.dve, fw.act, fw.pool, fw.sp

        uniq = [0]

        def mk(stack, kind, name, shape, dt):
            f = nc.sbuf_tensor if kind == "sb" else nc.psum_tensor
            uniq[0] += 1
            return stack.enter_context(f("%s_%d" % (name, uniq[0]), list(shape), dt)), Trk(name)

        def V(fn, r=(), w=()):
            return fw.op(dve, fn, r, w)

        def A(fn, r=(), w=()):
            return fw.op(act, fn, r, w)

        def G(fn, r=(), w=()):
            return fw.op(pool, fn, r, w)

        def P(fn, r=(), w=()):
            return fw.op(pe, fn, r, w)

        def ld(out, in_, key, w, r=()):
            return fw.dma(sp, out, in_, key, reads=r, writes=w)

        def ldc(out, in_, key, w, r=()):
            return fw.dma(pool, out, in_, key, reads=r, writes=w)

        def stq(out, in_, key, r, w=()):
            return fw.dma(sp, out, in_, key, reads=r, writes=w)

        pb = []
        for i in range(7):
            pb.append(mk(top, "ps", "pb%d" % i, [128, 512], F32))
        pT, TpT = mk(top, "ps", "pbT", [128, 1024], BF16)

        idb, Tidb = mk(top, "sb", "idb", [128, 128], BF16)
        idf, Tidf = mk(top, "sb", "idf", [128, 128], F32)
        tlt, Ttlt = mk(top, "sb", "tlt", [64, 64], F32)
        tut, Ttut = mk(top, "sb", "tut", [64, 64], F32)
        sel, Tsel = mk(top, "sb", "sel", [128, 8], F32)
        smat, Tsmat = mk(top, "sb", "smat", [128, 64], F32)
        gfin, Tgfin = mk(top, "sb", "gfin", [128, D], F32)
        Sst, TSst = mk(top, "sb", "Sst", [64, 512], F32)
        Sbf, TSbf = mk(top, "sb", "Sbf", [64, 512], BF16)
        Bc, TBc = mk(top, "sb", "Bc", [64, 4], F32)
        eBc, TeBc = mk(top, "sb", "eBc", [64, 4], F32)
        Twc = Trk("wconv")
        TlatLa = [Trk("latLa%d" % i) for i in range(8)]
        TlatLb = [Trk("latLb%d" % i) for i in range(8)]
        TlatG = [Trk("latG%d" % i) for i in range(8)]
        ToL = [Trk("oL%d" % i) for i in range(4)]
        ToG = [Trk("oG%d" % i) for i in range(4)]
        TglaL, TglaG = Trk("glaL"), Trk("glaG")

        def allgather(src, dst, r, w):
            r = list(r) + [Twc]
            fw.coll(lambda e: e.collective_compute("AllGather", ALU.bypass, replica_groups=GROUPS, ins=[src.opt()], outs=[dst.opt()]), "cc", r, w)

        Rb, TRb = mk(top, "sb", "Rb", [64, 512], BF16)
        Tdram = Trk("dram")

        ld(idf[:], ident[:, :], "c0", [Tidf])
        ldc(idb[:], ident[:, :], "c1", [Tidb])
        ld(tlt[:], trilt[:, :], "c2", [Ttlt])
        ld(tut[:], triut[:, :], "c3", [Ttut])
        ld(sel[:], selm[:, :], "c4", [Tsel])
        ld(smat[:], selmat[:, :], "c5", [Tsmat])
        ld(gfin[:], g_final[0:1, :].partition_broadcast(128), "c6", [Tgfin])

        def rstd_inplace(ap, T_, inv_n):
            V(lambda e: e.tensor_scalar(out=ap, in0=ap, scalar1=inv_n, scalar2=EPS, op0=ALU.mult, op1=ALU.add), [T_], [T_])
            A(lambda e: e.activation(out=ap, in_=ap, func=AF.Ln), [T_], [T_])
            A(lambda e: e.activation(out=ap, in_=ap, func=AF.Exp, scale=-0.5), [T_], [T_])

        tg = ExitStack()
        TGW = 1024
        rc, Trc = mk(tg, "sb", "rc", [96, 2], F32)
        ptl, Tptl = mk(tg, "sb", "ptl", [96, TGW], F32)
        ang, Tang = mk(tg, "sb", "ang", [96, TGW], F32)
        aa, Taa = mk(tg, "sb", "aa", [96, TGW], F32)
        tt, Ttt = mk(tg, "sb", "tt", [96, TGW], F32)
        ki, Tki = mk(tg, "sb", "ki", [96, TGW], I32)
        mm, Tmm = mk(tg, "sb", "mm", [96, TGW], F32)
        res, Tres = mk(tg, "sb", "res", [96, TGW], F32)
        ld(rc[:], ropec[:, :], "p0", [Trc])

        TtabG = [Trk("tabG%d" % i) for i in range(SEQ // 1024)]

        def table_steps(pos_ap, ntot, dC, dS, chunk_trks=None):
            steps = []

            def wrap_and_sin(dst_dram, n, use_sign, wtrk=None):
                steps.append(lambda: V(lambda e: e.tensor_scalar(out=tt[:, 0:n], in0=mm[:, 0:n], scalar1=math.pi, scalar2=None, op0=ALU.is_gt), [Tmm], [Ttt]))
                steps.append(lambda: V(lambda e: e.scalar_tensor_tensor(out=mm[:, 0:n], in0=tt[:, 0:n], scalar=-TWO_PI, in1=mm[:, 0:n], op0=ALU.mult, op1=ALU.add), [Ttt, Tmm], [Tmm]))
                steps.append(lambda: V(lambda e: e.tensor_scalar(out=tt[:, 0:n], in0=mm[:, 0:n], scalar1=-math.pi, scalar2=None, op0=ALU.is_lt), [Tmm], [Ttt]))
                steps.append(lambda: V(lambda e: e.scalar_tensor_tensor(out=mm[:, 0:n], in0=tt[:, 0:n], scalar=TWO_PI, in1=mm[:, 0:n], op0=ALU.mult, op1=ALU.add), [Ttt, Tmm], [Tmm]))
                steps.append(lambda: V(lambda e: e.tensor_scalar(out=aa[:, 0:n], in0=mm[:, 0:n], scalar1=3.1415925, scalar2=-3.1415925, op0=ALU.min, op1=ALU.max), [Tmm], [Taa]))
                steps.append(lambda: A(lambda e: e.activation(out=res[:, 0:n], in_=aa[:, 0:n], func=AF.Sin), [Taa], [Tres]))
                if use_sign:
                    steps.append(lambda: V(lambda e: e.tensor_scalar(out=res[:, 0:n], in0=res[:, 0:n], scalar1=rc[:, 1:2], scalar2=None, op0=ALU.mult), [Tres, Trc], [Tres]))
                steps.append(lambda: stq(dst_dram, res[:, 0:n], "p2", [Tres], [wtrk if wtrk is not None else Tdram]))

            for c0 in range(0, ntot, TGW):
                n = min(TGW, ntot - c0)
                steps.append(lambda c0=c0, n=n: ld(ptl[:, 0:n], pos_ap[0:1, c0:c0 + n].partition_broadcast(96), "p1", [Tptl]))
                steps.append(lambda n=n: V(lambda e: e.tensor_scalar(out=ang[:, 0:n], in0=ptl[:, 0:n], scalar1=rc[:, 0:1], scalar2=None, op0=ALU.mult), [Tptl, Trc], [Tang]))
                steps.append(lambda n=n: V(lambda e: e.tensor_scalar(out=tt[:, 0:n], in0=ang[:, 0:n], scalar1=1.0 / TWO_PI, scalar2=None, op0=ALU.mult), [Tang], [Ttt]))
                steps.append(lambda n=n: V(lambda e: e.tensor_copy(out=ki[:, 0:n], in_=tt[:, 0:n]), [Ttt], [Tki]))
                steps.append(lambda n=n: V(lambda e: e.tensor_copy(out=tt[:, 0:n], in_=ki[:, 0:n]), [Tki], [Ttt]))
                steps.append(lambda n=n: V(lambda e: e.scalar_tensor_tensor(out=mm[:, 0:n], in0=tt[:, 0:n], scalar=-C1, in1=ang[:, 0:n], op0=ALU.mult, op1=ALU.add), [Ttt, Tang], [Tmm]))
                steps.append(lambda n=n: V(lambda e: e.scalar_tensor_tensor(out=mm[:, 0:n], in0=tt[:, 0:n], scalar=-C2, in1=mm[:, 0:n], op0=ALU.mult, op1=ALU.add), [Ttt, Tmm], [Tmm]))
                wtrk_ = chunk_trks[c0 // TGW] if chunk_trks is not None else None
                wrap_and_sin(dS[:, c0:c0 + n], n, True, wtrk_)
                steps.append(lambda n=n: V(lambda e: e.tensor_scalar(out=mm[:, 0:n], in0=mm[:, 0:n], scalar1=math.pi / 2, scalar2=None, op0=ALU.add), [Tmm], [Tmm]))
                wrap_and_sin(dC[:, c0:c0 + n], n, False, wtrk_)
            return steps

        for st_ in table_steps(pos_l, NROW, tabLC, tabLS):
            st_()
        fw.barrier_all()
        bg_steps = table_steps(pos_g, SEQ, tabGC, tabGS, TtabG)
        STEPS_PER_CHUNK = len(bg_steps) // (SEQ // TGW)

        def bg_run(k):
            for _ in range(k):
                if bg_steps:
                    bg_steps.pop(0)()

        if KSTOP == "pro":
            return nc

        TILES = [(t * 512, 512, 128, 64, False) for t in range(NTOK // 512)] + [(NTOK, NS, 64, 32, True)]

        for l in range(2):
            with ExitStack() as ph:
                win, Twin = mk(ph, "sb", "win", [128, 8, WIN], BF16)
                wa2, Twa2 = mk(ph, "sb", "wa2", [16, 256], BF16)
                ba2, Tba2 = mk(ph, "sb", "ba2", [64, 256], F32)
                gat, Tgat = mk(ph, "sb", "gat", [128, D], F32)
                gct, Tgct = mk(ph, "sb", "gct", [64, 512], F32)
                xt, Txt = mk(ph, "sb", "xt", [128, 4, D], F32)
                junk, Tjunk = mk(ph, "sb", "junk", [128, D], F32)
                ss, Tss = mk(ph, "sb", "ss", [128, 1], F32)
                hb_b = [mk(ph, "sb", "hb%d" % i, [128, D], BF16) for i in range(2)]
                hT_b = [mk(ph, "sb", "hT%d" % i, [128, 8, 512], BF16) for i in range(2)]
                gqT, TgqT = mk(ph, "sb", "gqT", [64, 4, 512], F32)
                gkT, TgkT = mk(ph, "sb", "gkT", [64, 4, 512], F32)
                gaT, TgaT = mk(ph, "sb", "gaT", [16, 512], BF16)
                tbC, TtbC = mk(ph, "sb", "tbC", [32, 512], F32)
                tbS, TtbS = mk(ph, "sb", "tbS", [32, 512], F32)
                kr1, Tkr1 = mk(ph, "sb", "kr1", [32, 512], F32)
                kr2, Tkr2 = mk(ph, "sb", "kr2", [32, 512], F32)
                krb, Tkrb = mk(ph, "sb", "krb", [32, 512], BF16)
                kro, Tkro = mk(ph, "sb", "kro", [128, 4, 32], F32)
                sq, Tsq = mk(ph, "sb", "sq", [64, 512], F32)
                ss2, Tss2 = mk(ph, "sb", "ss2", [64, 2], F32)
                nf_b = [mk(ph, "sb", "nf%d" % i, [64, 512], F32) for i in range(2)]
                zsb_b = [mk(ph, "sb", "zsb%d" % i, [64, 512], F32) for i in range(2)]
                ksb_b = [mk(ph, "sb", "ksb%d" % i, [64, 256], F32) for i in range(2)]
                gsb_b = [mk(ph, "sb", "gsb%d" % i, [64, 512], F32) for i in range(2)]
                lgt_b = [mk(ph, "sb", "lgt%d" % i, [64, 256], F32) for i in range(2)]
                vb_b = [mk(ph, "sb", "vb%d" % i, [64, 512], BF16) for i in range(2)]
                sgo_b = [mk(ph, "sb", "sgo%d" % i, [64, 512], F32) for i in range(2)]
                ol_b = [mk(ph, "sb", "ol%d" % i, [64, 512], F32) for i in range(2)]
                nb, Tnb = mk(ph, "sb", "nb", [64, 512], BF16)
                latT, TlatT = mk(ph, "sb", "latT", [128, 4, 512], BF16)
                eb, Teb = mk(ph, "sb", "eb", [64, 4, 64], F32)
                enb, Tenb = mk(ph, "sb", "enb", [64, 4, 64], F32)
                ec, Tec = mk(ph, "sb", "ec", [64, 256], F32)
                qt, Tqt = mk(ph, "sb", "qt", [64, 4, 64], BF16)
                kt, Tkt = mk(ph, "sb", "kt", [64, 4, 64], BF16)
                kh, Tkh = mk(ph, "sb", "kh", [64, 256], BF16)
                sgt, Tsgt = mk(ph, "sb", "sgt", [64, 512], F32)
                qhat, Tqhat = mk(ph, "sb", "qhat", [64, 4, 512], BF16)
                At, TAt = mk(ph, "sb", "At", [64, 4, 64], BF16)
                sout, Tsout = mk(ph, "sb", "sout", [64, 516], F32)
                SsS, TSsS = mk(ph, "sb", "SsS", [64, 512], F32)
                SbS, TSbS = mk(ph, "sb", "SbS", [64, 512], BF16)

                ldc(win[:], w_in[l].rearrange("(kc p) n -> p kc n", p=128), "a0", [Twin])
                ldc(wa2[:], w_a2[l], "a1", [Twa2])
                if l == 0:
                    for l_ in range(2):
                        fw.dma(pool, woB[l_][:, :], w_o[l_], "wc", writes=[Twc])
                        fw.dma(pool, wdB[l_][:, :], w_dn[l_], "wc", writes=[Twc])
                        for hm in range(2):
                            fw.dma(pool, wguB[l_][hm * 11:(hm + 1) * 11], w_gu[l_, hm * 11:(hm + 1) * 11].rearrange("m p kc c -> m p (kc c)"), "wc", writes=[Twc])
                ld(ba2[:], b_a2[l, 0:1, :].partition_broadcast(64), "a2", [Tba2])
                ld(gat[:], g_attn[l, 0:1, :].partition_broadcast(128), "a3", [Tgat])
                ld(gct[:], g_cat[l, 0:1, :].partition_broadcast(64), "a4", [Tgct])
                V(lambda e: e.memset(Sst[:], 0.0), [], [TSst])
                V(lambda e: e.memset(Sbf[:], 0.0), [], [TSbf])
                V(lambda e: e.memset(Bc[:], 0.0), [], [TBc])
                V(lambda e: e.memset(eBc[:], 1.0), [], [TeBc])

                deferred_ag = []
                bg_budget = [0]
                stage_banks = [4, 5, 6, 3]
                stage_i = [0]

                def stage():
                    b = stage_banks[stage_i[0] % len(stage_banks)]
                    stage_i[0] += 1
                    return pb[b]

                def load_x(tidx):
                    row0_, T_, PS_, CL_, is_s_ = TILES[tidx]
                    src_ = x_in if l == 0 else xs
                    ld(xt[0:PS_, 0:T_ // PS_, :], src_[row0_:row0_ + T_, :].rearrange("(s p) d -> p s d", p=PS_), "a5", [Txt], [Tdram])

                pTv = pT[:].rearrange("p (c t) -> p c t", c=8)

                def norm_a(ti, s_):
                    PS_ = TILES[ti][2]
                    hb, Thb = hb_b[s_ % 2]
                    A(lambda e: e.activation(out=junk[0:PS_, :], in_=xt[0:PS_, s_, :], func=AF.Square, accum_out=ss[0:PS_, 0:1]), [Txt], [Tjunk, Tss])
                    rstd_inplace(ss[0:PS_, 0:1], Tss, 1.0 / D)
                    V(lambda e: e.scalar_tensor_tensor(out=hb[0:PS_, :], in0=xt[0:PS_, s_, :], scalar=ss[0:PS_, 0:1], in1=gat[0:PS_, :], op0=ALU.mult, op1=ALU.mult), [Txt, Tss, Tgat], [Thb])

                def norm_b(ti, s_):
                    PS_ = TILES[ti][2]
                    hb, Thb = hb_b[s_ % 2]
                    hTn, ThTn = hT_b[ti % 2]
                    for c_ in range(8):
                        P(lambda e: e.transpose(out=pTv[:, c_, 0:PS_], in_=hb[0:PS_, c_ * 128:(c_ + 1) * 128], identity=idb[0:PS_, 0:PS_]), [Thb, Tidb], [TpT])
                    A(lambda e: e.activation(out=hTn[:, :, s_ * PS_:(s_ + 1) * PS_], in_=pTv[:, :, 0:PS_], func=AF.Copy), [TpT], [ThTn])

                load_x(0)
                for s_ in range(TILES[0][1] // TILES[0][2]):
                    norm_a(0, s_)
                    norm_b(0, s_)
                load_x(1)
                for tidx, (row0, T, PS, CL, is_s) in enumerate(TILES):
                    hT, ThT = hT_b[tidx % 2]
                    nhooks = {}
                    if tidx + 1 < len(TILES):
                        nsub_n = TILES[tidx + 1][1] // TILES[tidx + 1][2]
                        for s_ in range(nsub_n):
                            nhooks.setdefault(1 + s_, []).append(lambda s_=s_, ti=tidx + 1: norm_a(ti, s_))
                            nhooks.setdefault(2 + s_, []).append(lambda s_=s_, ti=tidx + 1: norm_b(ti, s_))
                        if tidx + 2 < len(TILES):
                            nhooks.setdefault(nsub_n, []).append(lambda ti=tidx + 2: load_x(ti))
                    nsub = T // PS
                    nch = T // CL
                    SstX, TSstX, SbfX, TSbfX = (SsS, TSsS, SbS, TSbS) if is_s else (Sst, TSst, Sbf, TSbf)
                    ld(tbC[:, 0:T], tabLC[0:32, row0:row0 + T], "a6", [TtbC], [Tdram])
                    ld(tbS[:, 0:T], tabLS[0:32, row0:row0 + T], "a7", [TtbS], [Tdram])
                    def fm_proj(col0, M):
                        bank, Tb = stage()
                        for kc in range(8):
                            P(lambda e: e.matmul(bank[0:M, 0:T], lhsT=win[:, kc, col0:col0 + M], rhs=hT[:, kc, 0:T], start=(kc == 0), stop=(kc == 7)), [Twin, ThT], [Tb])
                        return bank, Tb

                    for h in range(4):
                        bank, Tb = fm_proj(OGQ + h * 64, 64)
                        V(lambda e: e.tensor_scalar(out=gqT[:, h, 0:T], in0=bank[0:64, 0:T], scalar1=0.125, scalar2=None, op0=ALU.mult), [Tb], [TgqT])
                        bank, Tb = fm_proj(OGK + h * 64, 64)
                        V(lambda e: e.tensor_copy(out=gkT[:, h, 0:T], in_=bank[0:64, 0:T]), [Tb], [TgkT])
                    bank, Tb = fm_proj(OGA, 16)
                    A(lambda e: e.activation(out=gaT[:, 0:T], in_=bank[0:16, 0:T], func=AF.Copy), [Tb], [TgaT])
                    bank, Tb = fm_proj(OKR, 32)
                    V(lambda e: e.tensor_tensor(out=kr1[:, 0:T], in0=bank[0:32, 0:T], in1=tbC[:, 0:T], op=ALU.mult), [Tb, TtbC], [Tkr1])
                    bank, Tb = fm_proj(OKRP, 32)
                    V(lambda e: e.tensor_tensor(out=kr2[:, 0:T], in0=bank[0:32, 0:T], in1=tbS[:, 0:T], op=ALU.mult), [Tb, TtbS], [Tkr2])
                    V(lambda e: e.tensor_tensor(out=kr1[:, 0:T], in0=kr1[:, 0:T], in1=kr2[:, 0:T], op=ALU.add), [Tkr1, Tkr2], [Tkr1])
                    V(lambda e: e.tensor_copy(out=krb[:, 0:T], in_=kr1[:, 0:T]), [Tkr1], [Tkrb])
                    if is_s:
                        stq(latS[512:544, 0:T], krb[:, 0:T], "a8", [Tkrb], [Tdram])
                    else:
                        stq(latL[row0 // 512][512:544, 0:T], krb[:, 0:T], "a8", [Tkrb], [TlatLa[row0 // 512]])
                    bank, Tb = stage()
                    for s in range(nsub):
                        P(lambda e: e.transpose(out=bank[0:PS, s * 32:(s + 1) * 32], in_=kr1[0:32, s * PS:(s + 1) * PS], identity=idf[0:32, 0:32]), [Tkr1, Tidf], [Tb])
                    V(lambda e: e.tensor_copy(out=kro[0:PS, 0:nsub, :], in_=bank[0:PS, 0:nsub * 32].rearrange("p (s d) -> p s d", d=32)), [Tb], [Tkro])
                    stq(kr_out[l, row0:row0 + T, :].rearrange("(s p) d -> p s d", p=PS), kro[0:PS, 0:nsub, :], "a9", [Tkro])

                    b0, Tb0 = pb[0]
                    b1, Tb1 = pb[1]
                    b2, Tb2 = pb[2]
                    b3, Tb3 = pb[3]
                    b4, Tb4 = pb[4]
                    b5, Tb5 = pb[5]
                    b6, Tb6 = pb[6]
                    pbT3 = b6[0:64, 0:256].rearrange("p (h t) -> p h t", h=4)

                    def PGmm(c, g):
                        c0 = c * CL
                        bi, ncols, wc0 = [(0, 512, OCQ), (1, 256, OGK), (2, 512, OGV), (3, 512, OGG)][g]
                        bank, Tb = pb[bi]
                        for kc in range(8):
                            P(lambda e: e.matmul(bank[0:CL, 0:ncols], lhsT=hT[:, kc, c0:c0 + CL], rhs=win[:, kc, wc0:wc0 + ncols], start=(kc == 0), stop=(kc == 7)), [ThT, Twin], [Tb])
                        if g == 1:
                            P(lambda e: e.matmul(b1[0:CL, 256:512], lhsT=gaT[0:16, c0:c0 + CL], rhs=wa2[0:16, :], start=True, stop=True), [TgaT, Twa2], [Tb1])

                    def PGev(c, g):
                        par = c % 2
                        if g == 0:
                            A(lambda e: e.activation(out=zsb_b[par][0][0:CL, :], in_=b0[0:CL, :], func=AF.Copy), [Tb0], [zsb_b[par][1]])
                        elif g == 1:
                            V(lambda e: e.tensor_copy(out=ksb_b[par][0][0:CL, :], in_=b1[0:CL, 0:256]), [Tb1], [ksb_b[par][1]])
                            V(lambda e: e.tensor_tensor(out=lgt_b[par][0][0:CL, :], in0=b1[0:CL, 256:512], in1=ba2[0:CL, :], op=ALU.add), [Tb1, Tba2], [lgt_b[par][1]])
                        elif g == 2:
                            A(lambda e: e.activation(out=vb_b[par][0][0:CL, :], in_=b2[0:CL, :], func=AF.Copy), [Tb2], [vb_b[par][1]])
                        else:
                            A(lambda e: e.activation(out=gsb_b[par][0][0:CL, :], in_=b3[0:CL, :], func=AF.Copy), [Tb3], [gsb_b[par][1]])

                    def LATa(c):
                        c0 = c * CL
                        par = c % 2
                        zsb, Tzsb = zsb_b[par]
                        nfc, Tnfc = nf_b[par]
                        A(lambda e: e.activation(out=sq[0:CL, :], in_=zsb[0:CL, :], func=AF.Square), [Tzsb], [Tsq])
                        V(lambda e: e.tensor_reduce(out=ss2[0:CL, :], in_=sq[0:CL, :].rearrange("p (g d) -> p g d", g=2), axis=AX.X, op=ALU.add), [Tsq], [Tss2])
                        rstd_inplace(ss2[0:CL, :], Tss2, 1.0 / 256)
                        V(lambda e: e.tensor_tensor(out=nfc[0:CL, :].rearrange("p (g d) -> p g d", g=2), in0=zsb[0:CL, :].rearrange("p (g d) -> p g d", g=2), in1=ss2[0:CL, :].unsqueeze(2).broadcast_to([CL, 2, 256]), op=ALU.mult), [Tzsb, Tss2], [Tnfc])
                        V(lambda e: e.tensor_tensor(out=nfc[0:CL, :], in0=nfc[0:CL, :], in1=gct[0:CL, :], op=ALU.mult), [Tnfc, Tgct], [Tnfc])
                        A(lambda e: e.activation(out=nb[0:CL, :], in_=nfc[0:CL, :], func=AF.Copy), [Tnfc], [Tnb])
                        stq(ckv_out[l, row0 + c0:row0 + c0 + CL, :], nfc[0:CL, 256:512], "a12_%d" % par, [Tnfc])

                    def LATb(c):
                        c0 = c * CL
                        pTl = pT[:, 0:512].rearrange("p (b t) -> p b t", b=4)
                        for b_ in range(4):
                            P(lambda e: e.transpose(out=pTl[:, b_, 0:CL], in_=nb[0:CL, b_ * 128:(b_ + 1) * 128], identity=idb[0:CL, 0:CL]), [Tnb, Tidb], [TpT])
                        V(lambda e: e.tensor_copy(out=latT[:, :, c0:c0 + CL], in_=pTl[:, :, 0:CL]), [TpT], [TlatT])

                    def SILU(c):
                        c0 = c * CL
                        par = c % 2
                        gsb, Tgsb = gsb_b[par]
                        sgc_, Tsgc_ = sgo_b[par]
                        A(lambda e: e.activation(out=sgt[0:CL, :], in_=gsb[0:CL, :], func=AF.Exp, scale=-1.0), [Tgsb], [Tsgt])
                        A(lambda e: e.activation(out=sgt[0:CL, :], in_=sgt[0:CL, :], func=AF.Ln, bias=1.0), [Tsgt], [Tsgt])
                        A(lambda e: e.activation(out=sgt[0:CL, :], in_=sgt[0:CL, :], func=AF.Exp, scale=-1.0), [Tsgt], [Tsgt])
                        V(lambda e: e.tensor_tensor(out=sgc_[0:CL, :], in0=gsb[0:CL, :], in1=sgt[0:CL, :], op=ALU.mult), [Tgsb, Tsgt], [Tsgc_])
                        stq(sgg[row0 + c0:row0 + c0 + CL, :], sgc_[0:CL, :], "a14_%d" % par, [Tsgc_], [Tdram])

                    def G1a(c):
                        par = c % 2
                        lgt, Tlgt = lgt_b[par]
                        A(lambda e: e.activation(out=lgt[0:CL, :], in_=lgt[0:CL, :], func=AF.Exp, scale=-1.0), [Tlgt], [Tlgt])
                        A(lambda e: e.activation(out=lgt[0:CL, :], in_=lgt[0:CL, :], func=AF.Ln, bias=1.0), [Tlgt], [Tlgt])
                        V(lambda e: e.tensor_scalar(out=lgt[0:CL, :], in0=lgt[0:CL, :], scalar1=-1.0 / 16.0, scalar2=None, op0=ALU.mult), [Tlgt], [Tlgt])

                    def G1b(c):
                        par = c % 2
                        lgt, Tlgt = lgt_b[par]
                        if is_s:
                            ld(SstX[:].rearrange("k (h v) -> k h v", h=4), sgla[l, c].rearrange("h k v -> k h v"), "a10", [TSstX])
                            V(lambda e: e.tensor_copy(out=SbfX[:], in_=SstX[:]), [TSstX], [TSbfX])
                        for h in range(4):
                            P(lambda e: e.matmul(pbT3[:, h, 0:CL], lhsT=lgt[0:CL, h * 64:(h + 1) * 64], rhs=tlt[0:CL, 0:CL], start=True, stop=True), [Tlgt, Ttlt], [Tb6])
                        P(lambda e: e.matmul(b6[0:CL, 256:512], lhsT=tut[0:CL, 0:CL], rhs=lgt[0:CL, :], start=True, stop=True), [Tlgt, Ttut], [Tb6])

                    def G2(c):
                        c0 = c * CL
                        par = c % 2
                        ksb, Tksb = ksb_b[par]
                        A(lambda e: e.activation(out=eb[:, :, 0:CL], in_=pbT3[:, :, 0:CL], func=AF.Exp), [Tb6], [Teb])
                        A(lambda e: e.activation(out=enb[:, :, 0:CL], in_=pbT3[:, :, 0:CL], func=AF.Exp, scale=-1.0), [Tb6], [Tenb])
                        A(lambda e: e.activation(out=ec[0:CL, :], in_=b6[0:CL, 256:512], func=AF.Exp), [Tb6], [Tec])
                        V(lambda e: e.tensor_tensor(out=qt[:, :, 0:CL], in0=gqT[:, :, c0:c0 + CL], in1=eb[:, :, 0:CL], op=ALU.mult), [TgqT, Teb], [Tqt])
                        V(lambda e: e.tensor_tensor(out=kt[:, :, 0:CL], in0=gkT[:, :, c0:c0 + CL], in1=enb[:, :, 0:CL], op=ALU.mult), [TgkT, Tenb], [Tkt])
                        if not is_s:
                            V(lambda e: e.tensor_tensor(out=Bc[:, :], in0=Bc[:, :], in1=pbT3[:, :, CL - 1], op=ALU.add), [TBc, Tb6], [TBc])
                        for h in range(4):
                            P(lambda e: e.matmul(b5[0:CL, h * 128:(h + 1) * 128], lhsT=qt[:, h, 0:CL], rhs=SbfX[:, h * 128:(h + 1) * 128], start=(h == 0), stop=False, skip_group_check=True), [Tqt, TSbfX], [Tb5])
                        pA3 = b4[0:64, 0:256].rearrange("p (h t) -> p h t", h=4)
                        for h in range(4):
                            P(lambda e: e.matmul(pA3[0:CL, h, 0:CL], lhsT=kt[:, h, 0:CL], rhs=qt[:, h, 0:CL], start=True, stop=True), [Tkt, Tqt], [Tb4])
                        V(lambda e: e.tensor_tensor(out=kh[0:CL, :], in0=ksb[0:CL, :], in1=ec[0:CL, :], op=ALU.mult), [Tksb, Tec], [Tkh])
                        if not is_s:
                            V(lambda e: e.tensor_tensor(out=qhat[:, :, c0:c0 + CL], in0=qt[:, :, 0:CL], in1=eBc[:, :].unsqueeze(2).broadcast_to([64, 4, CL]), op=ALU.mult), [Tqt, TeBc], [Tqhat])
                            A(lambda e: e.activation(out=eBc[:, :], in_=Bc[:, :], func=AF.Exp), [TBc], [TeBc])

                    def G3(c):
                        c0 = c * CL
                        par = c % 2
                        vb, Tvb = vb_b[par]
                        olc_, Tolc_ = ol_b[par]
                        pA3 = b4[0:64, 0:256].rearrange("p (h t) -> p h t", h=4)
                        V(lambda e: e.tensor_tensor(out=At[0:CL, :, 0:CL], in0=pA3[0:CL, :, 0:CL], in1=tlt[0:CL, 0:CL].unsqueeze(1).broadcast_to([CL, 4, CL]), op=ALU.mult), [Tb4, Ttlt], [TAt])
                        for h in range(4):
                            P(lambda e: e.matmul(b5[0:CL, h * 128:(h + 1) * 128], lhsT=At[0:CL, h, 0:CL], rhs=vb[0:CL, h * 128:(h + 1) * 128], start=False, stop=(h == 3), skip_group_check=True), [TAt, Tvb], [Tb5])
                        for h in range(4):
                            P(lambda e: e.matmul(b6[0:64, h * 128:(h + 1) * 128], lhsT=kh[0:CL, h * 64:(h + 1) * 64], rhs=vb[0:CL, h * 128:(h + 1) * 128], start=True, stop=True), [Tkh, Tvb], [Tb6])
                        A(lambda e: e.activation(out=olc_[0:CL, :], in_=b5[0:CL, :], func=AF.Copy), [Tb5], [Tolc_])
                        stq(oloc[row0 + c0:row0 + c0 + CL, :], olc_[0:CL, :], "a13_%d" % par, [Tolc_], [Tdram])

                    def G4(c):
                        for h in range(4):
                            V(lambda e: e.scalar_tensor_tensor(out=SstX[:, h * 128:(h + 1) * 128], in0=SstX[:, h * 128:(h + 1) * 128], scalar=eb[:, h, CL - 1:CL], in1=b6[0:64, h * 128:(h + 1) * 128], op0=ALU.mult, op1=ALU.add), [TSstX, Teb, Tb6], [TSstX])
                        if is_s:
                            stq(gla_out[l, 1 + c], SstX[:], "a11", [TSstX])
                        else:
                            V(lambda e: e.tensor_copy(out=SbfX[:], in_=SstX[:]), [TSstX], [TSbfX])

                    for g in range(4):
                        PGmm(0, g)
                        PGev(0, g)
                    G1a(0)
                    for c in range(nch):
                        nxt = c + 1 < nch
                        G1b(c)
                        if nxt:
                            PGmm(c + 1, 0)
                            PGmm(c + 1, 1)
                        if c >= 1:
                            LATb(c - 1)
                        G2(c)
                        if nxt:
                            PGev(c + 1, 0)
                            PGev(c + 1, 1)
                            G1a(c + 1)
                            PGmm(c + 1, 2)
                        G3(c)
                        if nxt:
                            PGev(c + 1, 2)
                            PGmm(c + 1, 3)
                        G4(c)
                        LATa(c)
                        SILU(c)
                        if nxt:
                            PGev(c + 1, 3)
                        if l == 0 and bg_budget[0] > 0:
                            bg_run(2)
                            bg_budget[0] -= 2
                        for fn_ in nhooks.pop(c, []):
                            fn_()
                    LATb(nch - 1)
                    for k_ in sorted(nhooks):
                        for fn_ in nhooks[k_]:
                            fn_()

                    if is_s:
                        stq(latS[0:512, 0:T].rearrange("(b p) t -> p b t", p=128), latT[:, :, 0:T], "a15", [TlatT], [Tdram])
                    else:
                        stq(latL[row0 // 512][0:512, 0:T].rearrange("(b p) t -> p b t", p=128), latT[:, :, 0:T], "a15", [TlatT], [TlatLb[row0 // 512]])
                        stq(qhT[:, :, row0:row0 + T], qhat[:, :, 0:T], "a16", [Tqhat], [Tdram])
                        ti_ = row0 // 512
                        deferred_ag.append((latL[ti_], latG[ti_], [TlatLa[ti_], TlatLb[ti_]], [TlatG[ti_]]))
                    if not is_s and row0 + T == NTOK:
                        V(lambda e: e.tensor_copy(out=sout[:, 0:512], in_=Sst[:]), [TSst], [Tsout])
                        V(lambda e: e.tensor_copy(out=sout[:, 512:516], in_=eBc[:, :]), [TeBc], [Tsout])
                        stq(glaLa[:, :], sout[:], "a17", [Tsout], [TglaL])
                        deferred_ag.append((glaLa, glaGa, [TglaL], [TglaG]))
                fw.barrier_all()

            fw.barrier_all()
            for args_ in deferred_ag:
                allgather(*args_)
            if l == 0:
                bg_run(9 * STEPS_PER_CHUNK)
            if l == 0:
                bg_run(max(0, bg_budget[0]))
            fw.barrier_all()
            if KSTOP == "A":
                return nc

            if KSTOP == "AG":
                return nc
            with ExitStack() as ph:
                KT, TKT = mk(ph, "sb", "KT", [96, SEQ], BF16)
                Vg, TVg = mk(ph, "sb", "Vg", [128, 128, 128], BF16)
                wuq, Twuq = mk(ph, "sb", "wuq", [128, 2, 2 * 96], BF16)
                wuk, Twuk = mk(ph, "sb", "wuk", [128, 2, 96], BF16)
                wuv, Twuv = mk(ph, "sb", "wuv", [128, 2, 64], BF16)
                wuq2, Twuq2 = mk(ph, "sb", "wuq2", [128, 2, 2 * 96], BF16)
                wuk2, Twuk2 = mk(ph, "sb", "wuk2", [128, 2, 96], BF16)
                wuv2, Twuv2 = mk(ph, "sb", "wuv2", [128, 2, 64], BF16)
                lt_bufs = [mk(ph, "sb", "ltb%d" % i, [128, 2, 512], BF16) for i in range(4)]
                cq_bufs = [mk(ph, "sb", "cqb%d" % i, [128, 2, 512], BF16) for i in range(2)]
                tC_bufs = [mk(ph, "sb", "tC%d" % i, [96, 512], F32) for i in range(2)]
                tS_bufs = [mk(ph, "sb", "tS%d" % i, [96, 512], F32) for i in range(2)]
                q1, Tq1 = mk(ph, "sb", "q1", [96, 512], F32)
                q2, Tq2 = mk(ph, "sb", "q2", [96, 512], F32)
                qb_bufs = [mk(ph, "sb", "qbb%d" % i, [96, 512], BF16) for i in range(2)]
                pt_bufs = [mk(ph, "sb", "ptb%d" % i, [128, 512], BF16) for i in range(4)]
                of_b = [mk(ph, "sb", "of%d" % i, [128, 512], F32) for i in range(2)]
                orc, Torc = mk(ph, "sb", "orc", [64, 512], F32)
                ob_b = [mk(ph, "sb", "ob%d" % i, [64, 512], BF16) for i in range(2)]
                ckt, Tckt = mk(ph, "sb", "ckt", [128, 16, 256], BF16)
                krp, Tkrp = mk(ph, "sb", "krp", [128, 17, 96], BF16)
                cT, TcT = mk(ph, "sb", "cT", [128, 2, 2080], BF16)
                cqS, TcqS = mk(ph, "sb", "cqS", [128, 2, NS], BF16)
                krS, TkrS = mk(ph, "sb", "krS", [32, NS], BF16)
                KTs, TKTs = mk(ph, "sb", "KTs", [96, 2080], BF16)
                Vs, TVs = mk(ph, "sb", "Vs", [128, 17, 128], BF16)
                pall_bufs = [mk(ph, "sb", "pall%d" % i, [128, 544], BF16) for i in range(2)]
                tCs, TtCs = mk(ph, "sb", "tCs", [96, NS], F32)
                tSs, TtSs = mk(ph, "sb", "tSs", [96, NS], F32)

                G(lambda e: e.memset(Vg[:, :, 64:128], 1.0), [], [TVg])
                G(lambda e: e.memset(Vs[:, :, 64:128], 1.0), [], [TVs])
                G(lambda e: e.memset(wuk[:], 0.0), [], [Twuk])
                G(lambda e: e.memset(wuk2[:], 0.0), [], [Twuk2])
                G(lambda e: e.memset(krp[:], 0.0), [], [Tkrp])
                for of_, Tof_ in of_b:
                    G(lambda e: e.memset(of_[:], 0.0), [], [Tof_])

                def make_q(wq_tile, Twq, wcol, cq_ap, Tcq, tCa, tSa, Ttabs, N, qb, Tqb):
                    br, Tbr = pb[4]
                    for kc in range(2):
                        P(lambda e: e.matmul(br[0:96, 0:N], lhsT=wq_tile[:, kc, wcol:wcol + 96], rhs=cq_ap(kc), start=(kc == 0), stop=(kc == 1)), [Twq, Tcq], [Tbr])
                    V(lambda e: e.tensor_copy(out=qb[0:64, 0:N], in_=br[0:64, 0:N]), [Tbr], [Tqb])
                    V(lambda e: e.tensor_tensor(out=q1[64:96, 0:N], in0=br[64:96, 0:N], in1=tCa, op=ALU.mult), [Tbr] + Ttabs, [Tq1])
                    for kc in range(2):
                        P(lambda e: e.matmul(br[0:96, 0:N], lhsT=wq_tile[:, kc, wcol + 96:wcol + 192], rhs=cq_ap(kc), start=(kc == 0), stop=(kc == 1)), [Twq, Tcq], [Tbr])
                    V(lambda e: e.tensor_tensor(out=q2[64:96, 0:N], in0=br[64:96, 0:N], in1=tSa, op=ALU.mult), [Tbr] + Ttabs, [Tq2])
                    V(lambda e: e.tensor_tensor(out=qb[64:96, 0:N], in0=q1[64:96, 0:N], in1=q2[64:96, 0:N], op=ALU.add), [Tq1, Tq2], [Tqb])

                def finish_o1(obank, Tobank, N, par):
                    of, Tof = of_b[par]
                    V(lambda e: e.tensor_copy(out=of[0:64, 0:N], in_=obank[0:64, 0:N]), [Tobank], [Tof])
                    V(lambda e: e.reciprocal(out=of[64:128, 0:N], in_=obank[64:128, 0:N]), [Tobank], [Tof])

                def finish_o2(N, par, dst_dram, Tdst=None):
                    of, Tof = of_b[par]
                    ob, Tob = ob_b[par]
                    bs, Tbs = pb[5]
                    P(lambda e: e.matmul(bs[0:64, 0:N], lhsT=smat[64:128, 0:64], rhs=of[64:128, 0:N], start=True, stop=True), [Tsmat, Tof], [Tbs])
                    V(lambda e: e.tensor_copy(out=orc[:, 0:N], in_=bs[0:64, 0:N]), [Tbs], [Torc])
                    V(lambda e: e.tensor_tensor(out=ob[:, 0:N], in0=of[0:64, 0:N], in1=orc[:, 0:N], op=ALU.mult), [Tof, Torc], [Tob])
                    stq(dst_dram, ob[:, 0:N], "b9_%d" % par, [Tob], [Tdst if Tdst is not None else Tdram])

                def finish_o(obank, Tobank, N, dst_dram, Tdst=None, par=0):
                    finish_o1(obank, Tobank, N, par)
                    finish_o2(N, par, dst_dram, Tdst)

                for hh in range(2):
                    ldc(wuq[:], w_uq_loc[l, :, hh].rearrange("(kc p) v d -> p kc (v d)", p=128), "b0", [Twuq])
                    ldc(wuk[:, :, 0:64], w_uk_loc[l, :, hh, :].rearrange("(kc p) d -> p kc d", p=128), "b1", [Twuk])
                    ldc(wuv[:], w_uv_loc[l, :, hh, :].rearrange("(kc p) d -> p kc d", p=128), "b2", [Twuv])
                    for t in range(SEQ // 512):
                        rk, ti = t // 8, t % 8
                        ld(KT[64:96, t * 512:(t + 1) * 512], latG[ti][rk * 544 + 512:rk * 544 + 544, :], "b3", [TKT], [TlatG[ti]])
                    for t in range(SEQ // 512):
                        rk, ti = t // 8, t % 8
                        ltb, Tltb = lt_bufs[t % 4]
                        ld(ltb[:], latG[ti][rk * 544 + 256:rk * 544 + 512, :].rearrange("(kc p) t -> p kc t", p=128), "b4_%d" % (t % 4), [Tltb], [TlatG[ti]])
                        bk, Tbk = pb[t % 2]
                        for kc in range(2):
                            P(lambda e: e.matmul(bk[0:64, :], lhsT=wuk[:, kc, 0:64], rhs=ltb[:, kc, :], start=(kc == 0), stop=(kc == 1)), [Twuk, Tltb], [Tbk])
                        A(lambda e: e.activation(out=KT[0:64, t * 512:(t + 1) * 512], in_=bk[0:64, :], func=AF.Copy), [Tbk], [TKT])
                        bv, Tbv = pb[2 + t % 2]
                        for j in range(4):
                            for kc in range(2):
                                P(lambda e: e.matmul(bv[:, j * 64:(j + 1) * 64], lhsT=ltb[:, kc, j * 128:(j + 1) * 128], rhs=wuv[:, kc, :], start=(kc == 0), stop=(kc == 1)), [Tltb, Twuv], [Tbv])
                        V(lambda e: e.tensor_copy(out=Vg[:, t * 4:(t + 1) * 4, 0:64], in_=bv[:, 0:256].rearrange("p (j d) -> p j d", j=4)), [Tbv], [TVg])
                    NQ = SEQ // 512
                    units = [(qi, kti) for qi in range(NQ) for kti in range(4 * qi + 4)]
                    LOOK = 2

                    def q_loads(qi):
                        rk, ti = qi // 8, qi % 8
                        cqb, Tcqb = cq_bufs[qi % 2]
                        tCb, TtCb = tC_bufs[qi % 2]
                        tSb, TtSb = tS_bufs[qi % 2]
                        ld(cqb[:], latG[ti][rk * 544:rk * 544 + 256, :].rearrange("(kc p) t -> p kc t", p=128), "b5_%d" % (qi % 2), [Tcqb], [TlatG[ti]])
                        ld(tCb[64:96, :], tabGC[64:96, qi * 512:(qi + 1) * 512], "b6_%d" % (qi % 2), [TtCb], [TtabG[qi // 2]])
                        ld(tSb[64:96, :], tabGS[64:96, qi * 512:(qi + 1) * 512], "b7_%d" % (qi % 2), [TtSb], [TtabG[qi // 2]])

                    def q_make(qi):
                        cqb, Tcqb = cq_bufs[qi % 2]
                        tCb, TtCb = tC_bufs[qi % 2]
                        tSb, TtSb = tS_bufs[qi % 2]
                        qb, Tqb = qb_bufs[qi % 2]
                        make_q(wuq, Twuq, 0, lambda kc: cqb[:, kc, :], Tcqb, tCb[64:96, :], tSb[64:96, :], [TtCb, TtSb], 512, qb, Tqb)

                    s_banks = [pb[2], pb[3], pb[6]]

                    def emit_qk(ui):
                        qi, kti = units[ui]
                        d = kti - 4 * qi
                        cs = 0 if d < 0 else d * 128
                        qb, Tqb = qb_bufs[qi % 2]
                        sbank, Tsbank = s_banks[ui % 3]
                        P(lambda e: e.matmul(sbank[:, cs:512], lhsT=KT[0:96, kti * 128:(kti + 1) * 128], rhs=qb[0:96, cs:512], start=True, stop=True), [TKT, Tqb], [Tsbank])

                    def emit_exp_pv(ui):
                        qi, kti = units[ui]
                        nkt = 4 * qi + 4
                        d = kti - 4 * qi
                        cs = 0 if d < 0 else d * 128
                        sbank, Tsbank = s_banks[ui % 3]
                        ptb, Tptb = pt_bufs[ui % 4]
                        obank, Tobank = pb[qi % 2]
                        A(lambda e: e.activation(out=ptb[:, cs:512], in_=sbank[:, cs:512], func=AF.Exp, scale=ATTN_SCALE), [Tsbank], [Tptb])
                        if d >= 0:
                            V(lambda e: e.memset(ptb[64:128, cs:cs + 64], 0.0), [], [Tptb])
                        P(lambda e: e.matmul(obank[:, cs:512], lhsT=Vg[:, kti, :], rhs=ptb[:, cs:512], start=(kti == 0), stop=(kti == nkt - 1), skip_group_check=True), [TVg, Tptb], [Tobank])

                    def do_finish1(fq):
                        finish_o1(pb[fq % 2][0], pb[fq % 2][1], 512, fq % 2)

                    def do_finish2(fq):
                        finish_o2(512, fq % 2, oL[fq // 8][hh * 64:(hh + 1) * 64, (fq % 8) * 512:(fq % 8 + 1) * 512], ToL[fq // 8])
                        if hh == 1 and fq % 8 == 7:
                            allgather(oL[fq // 8], oG[fq // 8], [ToL[fq // 8]], [ToG[fq // 8]])

                    if l == 0 and hh == 1:
                        bg_run(len(bg_steps))
                    q_loads(0)
                    q_loads(1)
                    q_make(0)
                    for ui in range(min(LOOK, len(units))):
                        emit_qk(ui)
                    pending = []
                    for ui, (qi, kti) in enumerate(units):
                        if kti == 0:
                            if qi + 2 < NQ:
                                q_loads(qi + 2)
                            if qi + 1 < NQ:
                                q_make(qi + 1)
                        if ui + LOOK < len(units):
                            emit_qk(ui + LOOK)
                        emit_exp_pv(ui)
                        if l == 0 and hh == 0 and ui % 2 == 0:
                            bg_run(1)
                        if kti == 4 * qi + 3:
                            pending.append((ui + 2, 0, qi))
                            pending.append((ui + 12, 1, qi))
                            pending.sort()
                        while pending and pending[0][0] <= ui:
                            _, kind, fq = pending.pop(0)
                            (do_finish1 if kind == 0 else do_finish2)(fq)
                    for _, kind, fq in sorted(pending):
                        (do_finish1 if kind == 0 else do_finish2)(fq)

                fw._need(fw.pool, ("cc", fw.dma_sems["cc"][1]))
                ld(cqS[:], latS[0:256, :].rearrange("(kc p) t -> p kc t", p=128), "s0", [TcqS], [Tdram])
                ld(tCs[64:96, :], tabLC[64:96, NTOK:NROW], "s1", [TtCs], [Tdram])
                ld(tSs[64:96, :], tabLS[64:96, NTOK:NROW], "s2", [TtSs], [Tdram])
                for s in range(2):
                    ldc(ckt[:], cckv[l, s].rearrange("(t p) d -> p t d", p=128), "s3", [Tckt])
                    ldc(krp[:, 0:16, 64:96], ckr[l, s].rearrange("(t p) d -> p t d", p=128), "s4", [Tkrp])
                    ld(cT[:, :, PAST:PAST + 32], latS[256:512, s * 32:(s + 1) * 32].rearrange("(kc p) t -> p kc t", p=128), "s5", [TcT], [Tdram])
                    ld(krS[:, 0:32], latS[512:544, s * 32:(s + 1) * 32], "s6", [TkrS], [Tdram])
                    pTc = pT[:].rearrange("p (b t) -> p b t", b=8)
                    for t4 in range(4):
                        for tt_ in range(4):
                            for kc in range(2):
                                P(lambda e: e.transpose(out=pTc[:, tt_ * 2 + kc, :], in_=ckt[:, t4 * 4 + tt_, kc * 128:(kc + 1) * 128], identity=idb[:]), [Tckt, Tidb], [TpT])
                        for kc in range(2):
                            V(lambda e: e.tensor_copy(out=cT[:, kc, t4 * 512:(t4 + 1) * 512].rearrange("p (t c) -> p t c", t=4), in_=pTc[:, kc::2, :]), [TpT], [TcT])
                    for h in range(8):
                        (swq, Tswq, swk, Tswk, swv, Tswv) = (wuq, Twuq, wuk, Twuk, wuv, Twuv) if h % 2 == 0 else (wuq2, Twuq2, wuk2, Twuk2, wuv2, Twuv2)
                        ldc(swq[:], w_uq_all[l, :, h].rearrange("(kc p) v d -> p kc (v d)", p=128), "b0_%d" % (h % 2), [Tswq])
                        ldc(swk[:, :, 0:64], w_uk_all[l, :, h, :].rearrange("(kc p) d -> p kc d", p=128), "b1_%d" % (h % 2), [Tswk])
                        ldc(swv[:], w_uv_all[l, :, h, :].rearrange("(kc p) d -> p kc d", p=128), "b2_%d" % (h % 2), [Tswv])
                        for t4 in range(4):
                            bk, Tbk = pb[t4 % 2]
                            for j in range(4):
                                tix = t4 * 4 + j
                                for kc in range(2):
                                    P(lambda e: e.matmul(bk[0:96, j * 128:(j + 1) * 128], lhsT=swk[:, kc, 0:96], rhs=cT[:, kc, tix * 128:(tix + 1) * 128], start=(kc == 0), stop=False, skip_group_check=True), [Tswk, TcT], [Tbk])
                                P(lambda e: e.matmul(bk[0:96, j * 128:(j + 1) * 128], lhsT=krp[:, tix, :], rhs=idb[:], start=False, stop=True, skip_group_check=True), [Tkrp, Tidb], [Tbk])
                            A(lambda e: e.activation(out=KTs[:, t4 * 512:(t4 + 1) * 512], in_=bk[0:96, :], func=AF.Copy), [Tbk], [TKTs])
                        bk, Tbk = pb[0]
                        for kc in range(2):
                            P(lambda e: e.matmul(bk[0:64, 0:32], lhsT=swk[:, kc, 0:64], rhs=cT[:, kc, PAST:PAST + 32], start=(kc == 0), stop=(kc == 1)), [Tswk, TcT], [Tbk])
                        A(lambda e: e.activation(out=KTs[0:64, PAST:PAST + 32], in_=bk[0:64, 0:32], func=AF.Copy), [Tbk], [TKTs])
                        fw.dma(sp, KTs[64:96, PAST:PAST + 32], latS[512:544, s * 32:(s + 1) * 32], "s7", reads=[Tdram], writes=[TKTs])
                        for t4 in range(5):
                            bv, Tbv = pb[2 + t4 % 2]
                            nt = 4 if t4 < 4 else 1
                            for j in range(nt):
                                tix = t4 * 4 + j
                                kp = 128 if tix < 16 else 32
                                for kc in range(2):
                                    P(lambda e: e.matmul(bv[0:kp, j * 64:(j + 1) * 64], lhsT=cT[:, kc, tix * 128:tix * 128 + kp], rhs=swv[:, kc, :], start=(kc == 0), stop=(kc == 1)), [TcT, Tswv], [Tbv])
                            if t4 < 4:
                                V(lambda e: e.tensor_copy(out=Vs[:, t4 * 4:(t4 + 1) * 4, 0:64], in_=bv[:, 0:256].rearrange("p (j d) -> p j d", j=4)), [Tbv], [TVs])
                            else:
                                V(lambda e: e.tensor_copy(out=Vs[0:32, 16, 0:64], in_=bv[0:32, 0:64]), [Tbv], [TVs])
                        qb, Tqb = qb_bufs[h % 2]
                        make_q(swq, Tswq, 0, lambda kc: cqS[:, kc, s * 32:(s + 1) * 32], TcqS, tCs[64:96, s * 32:(s + 1) * 32], tSs[64:96, s * 32:(s + 1) * 32], [TtCs, TtSs], 32, qb, Tqb)
                        obank, Tobank = pb[h % 2]
                        sA, TsA = pb[2]
                        sB, TsB = pb[3]
                        pall, Tpall = pall_bufs[h % 2]
                        for kti in range(16):
                            P(lambda e: e.matmul(sA[:, kti * 32:(kti + 1) * 32], lhsT=KTs[0:96, kti * 128:(kti + 1) * 128], rhs=qb[0:96, 0:32], start=True, stop=True, skip_group_check=True), [TKTs, Tqb], [TsA])
                        P(lambda e: e.matmul(sB[0:32, 0:32], lhsT=KTs[0:96, PAST:PAST + 32], rhs=qb[0:96, 0:32], start=True, stop=True), [TKTs, Tqb], [TsB])
                        A(lambda e: e.activation(out=pall[:, 0:512], in_=sA[:, :], func=AF.Exp, scale=ATTN_SCALE), [TsA], [Tpall])
                        A(lambda e: e.activation(out=pall[0:32, 512:544], in_=sB[0:32, 0:32], func=AF.Exp, scale=ATTN_SCALE), [TsB], [Tpall])
                        for kti in range(17):
                            kp = 128 if kti < 16 else 32
                            P(lambda e: e.matmul(obank[:, 0:32], lhsT=Vs[0:kp, kti, :], rhs=pall[0:kp, kti * 32:(kti + 1) * 32], start=(kti == 0), stop=(kti == 16)), [TVs, Tpall], [Tobank])
                        finish_o(obank, Tobank, 32, oS[h * 64:(h + 1) * 64, s * 32:(s + 1) * 32])
                fw.barrier_all()
            if l == 0:
                tg.close()

            if KSTOP == "B":
                return nc

            with ExitStack() as ph0:
                gg_, Tgg = mk(ph0, "sb", "ggl", [64, 4, 516], F32)
                Rs, TRs = mk(ph0, "sb", "Rs", [64, 512], F32)
                cf, Tcf = mk(ph0, "sb", "cf", [64, 4], F32)
                se, Tse = mk(ph0, "sb", "se", [64, 512], F32)
                ld(gg_[:], glaGa.rearrange("(j k) c -> k j c", j=4), "c14", [Tgg], [TglaG])
                V(lambda e: e.memset(Rs[:], 0.0), [], [TRs])
                for j in range(4):
                    V(lambda e: e.tensor_scalar(out=cf[:], in0=gg_[:, j, 512:516], scalar1=-1.0, scalar2=None, op0=ALU.add), [Tgg], [Tcf])
                    V(lambda e: e.tensor_scalar(out=cf[:], in0=cf[:], scalar1=sel[0:64, 4 + j:5 + j], scalar2=None, op0=ALU.mult), [Tcf, Tsel], [Tcf])
                    V(lambda e: e.tensor_scalar(out=cf[:], in0=cf[:], scalar1=1.0, scalar2=None, op0=ALU.add), [Tcf], [Tcf])
                    V(lambda e: e.tensor_tensor(out=Rs[:].rearrange("k (h v) -> k h v", h=4), in0=Rs[:].rearrange("k (h v) -> k h v", h=4), in1=cf[:, :].unsqueeze(2).broadcast_to([64, 4, 128]), op=ALU.mult), [TRs, Tcf], [TRs])
                    V(lambda e: e.scalar_tensor_tensor(out=Rs[:], in0=gg_[:, j, 0:512], scalar=sel[0:64, 4 + j:5 + j], in1=Rs[:], op0=ALU.mult, op1=ALU.add), [Tgg, Tsel, TRs], [TRs])
                V(lambda e: e.tensor_copy(out=Rb[:], in_=Rs[:]), [TRs], [TRb])
                V(lambda e: e.tensor_tensor(out=se[:].rearrange("k (h v) -> k h v", h=4), in0=Rs[:].rearrange("k (h v) -> k h v", h=4), in1=eBc[:, :].unsqueeze(2).broadcast_to([64, 4, 128]), op=ALU.mult), [TRs, TeBc], [Tse])
                V(lambda e: e.tensor_tensor(out=se[:], in0=se[:], in1=Sst[:], op=ALU.add), [Tse, TSst], [Tse])
                stq(gla_out[l, 0], se[:], "c15", [Tse])
                fw.barrier_all()

            with ExitStack() as ph:
                wo, Two = mk(ph, "sb", "wo", [128, 8, D], BF16)
                wd, Twd = mk(ph, "sb", "wd", [128, 22, D], BF16)
                NWG = 4
                wg_bufs = [mk(ph, "sb", "wgb%d" % i, [128, 8, 256], BF16) for i in range(NWG)]
                gft, Tgft = mk(ph, "sb", "gft", [128, D], F32)
                gon, Tgon = mk(ph, "sb", "gon", [128, 128], F32)
                xt_b = [mk(ph, "sb", "xtc%d" % i, [128, 4, D], F32)[0] for i in range(2)]
                Txt_b = [[Trk("xt%d_%d" % (i, s_)) for s_ in range(4)] for i in range(2)]
                cand_bufs = [mk(ph, "sb", "cand%d" % i, [128, 4, 512], BF16) for i in range(2)]
                cat_b = [mk(ph, "sb", "catT%d" % i, [128, 8, 512], BF16)[0] for i in range(2)]
                Tcat_b = [[Trk("cat%d_%d" % (i, s_)) for s_ in range(4)] for i in range(2)]
                hid, Thid = mk(ph, "sb", "hid", [128, 22, 512], BF16)
                olc, Tolc = mk(ph, "sb", "olc", [128, 512], F32)
                sgc, Tsgc = mk(ph, "sb", "sgc", [128, 512], F32)
                qhc_b = [mk(ph, "sb", "qhc%d" % i, [64, 4, 512], BF16) for i in range(2)]
                sq, Tsq = mk(ph, "sb", "sqc", [128, 512], F32)
                ss4, Tss4 = mk(ph, "sb", "ss4", [128, 4], F32)
                ogb_b = [mk(ph, "sb", "ogb%d" % i, [128, 512], BF16) for i in range(2)]
                junk, Tjunk = mk(ph, "sb", "junkc", [128, D], BF16)
                ss, Tss = mk(ph, "sb", "ssc", [128, 1], F32)
                hb_b = [mk(ph, "sb", "hbc%d" % i, [128, D], BF16) for i in range(2)]
                sa_bufs = [mk(ph, "sb", "sa%d" % i, [128, 512], F32) for i in range(2)]
                yt_b = [mk(ph, "sb", "yt%d" % i, [128, D], F32) for i in range(2)]

                ld(wo[:], woB[l].rearrange("(kc p) n -> p kc n", p=128), "c10", [Two], [Twc])
                ld(wd[:], wdB[l].rearrange("(kc p) n -> p kc n", p=128), "c11", [Twd], [Twc])
                ld(gft[:], g_ffn[l, 0:1, :].partition_broadcast(128), "c12", [Tgft])
                ld(gon[:], g_on[l, 0:1, :].partition_broadcast(128), "c13", [Tgon])

                wg_ctr = [0]
                NT = len(TILES)

                def P_load(i):
                    row0, T, PS, CL, is_s = TILES[i]
                    nsub = T // PS
                    src = x_in if l == 0 else xs
                    xt = xt_b[i % 2]
                    ld(xt[0:PS, 0:nsub, :], src[row0:row0 + T, :].rearrange("(s p) d -> p s d", p=PS), "c16_%d" % (i % 2), Txt_b[i % 2][0:nsub], [Tdram])
                    if not is_s:
                        qhc, Tqhc = qhc_b[i % 2]
                        ld(qhc[:, :, 0:T], qhT[:, :, row0:row0 + T], "c19_%d" % (i % 2), [Tqhc], [Tdram])

                def P_select(i):
                    row0, T, PS, CL, is_s = TILES[i]
                    nsub = T // PS
                    catT = cat_b[i % 2]
                    Tc = Tcat_b[i % 2][0:nsub]
                    if is_s:
                        ld(catT[:, 0:4, 0:T], oS.rearrange("(kc p) t -> p kc t", p=128), "c17", Tc, [Tdram])
                    else:
                        for j in range(4):
                            cand, Tcand = cand_bufs[j % 2]
                            ld(cand[:], oG[j][:, row0:row0 + T].rearrange("(kc p) t -> p kc t", p=128), "c18_%d" % (j % 2), [Tcand], [ToG[j]])
                            if j == 0:
                                V(lambda e: e.tensor_scalar(out=catT[:, 0:4, :], in0=cand[:], scalar1=sel[:, 0:1], scalar2=None, op0=ALU.mult), [Tcand, Tsel], Tc)
                            else:
                                V(lambda e: e.scalar_tensor_tensor(out=catT[:, 0:4, :], in0=cand[:], scalar=sel[:, j:j + 1], in1=catT[:, 0:4, :], op0=ALU.mult, op1=ALU.add), [Tcand, Tsel] + Tc, Tc)

                def P1(i, s):
                    row0, T, PS, CL, is_s = TILES[i]
                    r0 = row0 + s * PS
                    ogb, Togb = ogb_b[s % 2]
                    ld(olc[0:PS, :], oloc[r0:r0 + PS, :], "c20", [Tolc], [Tdram])
                    ld(sgc[0:PS, :], sgg[r0:r0 + PS, :], "c21", [Tsgc], [Tdram])
                    if not is_s:
                        qhc, Tqhc = qhc_b[i % 2]
                        bc_, Tbc_ = pb[6]
                        for h in range(4):
                            P(lambda e: e.matmul(bc_[0:PS, h * 128:(h + 1) * 128], lhsT=qhc[:, h, s * PS:(s + 1) * PS], rhs=Rb[:, h * 128:(h + 1) * 128], start=True, stop=True), [Tqhc, TRb], [Tbc_])
                        V(lambda e: e.tensor_tensor(out=olc[0:PS, :], in0=olc[0:PS, :], in1=bc_[0:PS, :], op=ALU.add), [Tolc, Tbc_], [Tolc])
                    V(lambda e: e.tensor_tensor(out=sq[0:PS, :], in0=olc[0:PS, :], in1=olc[0:PS, :], op=ALU.mult), [Tolc], [Tsq])
                    V(lambda e: e.tensor_reduce(out=ss4[0:PS, :], in_=sq[0:PS, :].rearrange("p (h d) -> p h d", h=4), axis=AX.X, op=ALU.add), [Tsq], [Tss4])
                    rstd_inplace(ss4[0:PS, :], Tss4, 1.0 / 128)
                    V(lambda e: e.tensor_tensor(out=olc[0:PS, :].rearrange("p (h d) -> p h d", h=4), in0=olc[0:PS, :].rearrange("p (h d) -> p h d", h=4), in1=ss4[0:PS, :].unsqueeze(2).broadcast_to([PS, 4, 128]), op=ALU.mult), [Tolc, Tss4], [Tolc])
                    V(lambda e: e.tensor_tensor(out=sgc[0:PS, :].rearrange("p (h d) -> p h d", h=4), in0=sgc[0:PS, :].rearrange("p (h d) -> p h d", h=4), in1=gon[0:PS, :].unsqueeze(1).broadcast_to([PS, 4, 128]), op=ALU.mult), [Tsgc, Tgon], [Tsgc])
                    V(lambda e: e.tensor_tensor(out=ogb[0:PS, :], in0=olc[0:PS, :], in1=sgc[0:PS, :], op=ALU.mult), [Tolc, Tsgc], [Togb])

                def P2(i, s):
                    row0, T, PS, CL, is_s = TILES[i]
                    ogb, Togb = ogb_b[s % 2]
                    catT = cat_b[i % 2]
                    pTl = pT[:, 0:512].rearrange("p (b t) -> p b t", b=4)
                    for b_ in range(4):
                        P(lambda e: e.transpose(out=pTl[:, b_, 0:PS], in_=ogb[0:PS, b_ * 128:(b_ + 1) * 128], identity=idb[0:PS, 0:PS]), [Togb, Tidb], [TpT])
                    V(lambda e: e.tensor_copy(out=catT[:, 4:8, s * PS:(s + 1) * PS], in_=pTl[:, :, 0:PS]), [TpT], [Tcat_b[i % 2][s]])

                def M(i, hooks):
                    row0, T, PS, CL, is_s = TILES[i]
                    nsub = T // PS
                    xt = xt_b[i % 2]
                    Txs = Txt_b[i % 2]
                    catT = cat_b[i % 2]
                    Tcs = Tcat_b[i % 2]
                    h2T = catT
                    pTv = pT[:].rearrange("p (c t) -> p c t", c=8)

                    def WO(s):
                        for n in range(2):
                            bo, Tbo = pb[(2 * s + n) % 4]
                            for kc in range(8):
                                P(lambda e: e.matmul(bo[0:PS, :], lhsT=catT[:, kc, s * PS:(s + 1) * PS], rhs=wo[:, kc, n * 512:(n + 1) * 512], start=(kc == 0), stop=(kc == 7)), [Tcs[s], Two], [Tbo])
                            V(lambda e: e.tensor_tensor(out=xt[0:PS, s, n * 512:(n + 1) * 512], in0=xt[0:PS, s, n * 512:(n + 1) * 512], in1=bo[0:PS, :], op=ALU.add), [Txs[s], Tbo], [Txs[s]])

                    def N_(s):
                        hb, Thb = hb_b[s % 2]
                        A(lambda e: e.activation(out=junk[0:PS, :], in_=xt[0:PS, s, :], func=AF.Square, accum_out=ss[0:PS, 0:1]), [Txs[s]], [Tjunk, Tss])
                        rstd_inplace(ss[0:PS, 0:1], Tss, 1.0 / D)
                        V(lambda e: e.scalar_tensor_tensor(out=hb[0:PS, :], in0=xt[0:PS, s, :], scalar=ss[0:PS, 0:1], in1=gft[0:PS, :], op0=ALU.mult, op1=ALU.mult), [Txs[s], Tss, Tgft], [Thb])

                    def T_(s):
                        hb, Thb = hb_b[s % 2]
                        for c in range(8):
                            P(lambda e: e.transpose(out=pTv[:, c, 0:PS], in_=hb[0:PS, c * 128:(c + 1) * 128], identity=idb[0:PS, 0:PS]), [Thb, Tidb], [TpT])
                        A(lambda e: e.activation(out=h2T[:, :, s * PS:(s + 1) * PS], in_=pTv[:, :, 0:PS], func=AF.Copy), [TpT], [Tcs[s]])

                    for s in range(nsub):
                        WO(s)
                        if s >= 1:
                            N_(s - 1)
                        if s >= 2:
                            T_(s - 2)
                    N_(nsub - 1)
                    if nsub >= 2:
                        T_(nsub - 2)
                    T_(nsub - 1)
                    for m in range(22):
                        wgb, Twgb = wg_bufs[wg_ctr[0] % NWG]
                        fw.dma(pool, wgb[:], wguB[l][m].rearrange("p (kc c) -> p kc c", kc=8), "c22_%d" % (wg_ctr[0] % NWG), reads=[Twc], writes=[Twgb])
                        wg_ctr[0] += 1
                        ba, Tba = pb[4 + (m % 2)]
                        bu, Tbu = pb[2 * (m % 2)]
                        for kc in range(8):
                            P(lambda e: e.matmul(ba[:, 0:T], lhsT=wgb[:, kc, 0:128], rhs=h2T[:, kc, 0:T], start=(kc == 0), stop=(kc == 7)), [Twgb] + Tcs[0:nsub], [Tba])
                        for kc in range(8):
                            P(lambda e: e.matmul(bu[:, 0:T], lhsT=wgb[:, kc, 128:256], rhs=h2T[:, kc, 0:T], start=(kc == 0), stop=(kc == 7)), [Twgb] + Tcs[0:nsub], [Tbu])
                        sa, Tsa = sa_bufs[m % 2]
                        A(lambda e: e.activation(out=sa[:, 0:T], in_=ba[:, 0:T], func=AF.Silu), [Tba], [Tsa])
                        V(lambda e: e.tensor_tensor(out=hid[:, m, 0:T], in0=sa[:, 0:T], in1=bu[:, 0:T], op=ALU.mult), [Tsa, Tbu], [Thid])
                        for fn in hooks.get(m, []):
                            fn()
                    for s in range(nsub):
                        for n in range(2):
                            bo, Tbo = pb[1 + 2 * ((2 * s + n) % 2)]
                            for m in range(22):
                                P(lambda e: e.matmul(bo[0:PS, :], lhsT=hid[:, m, s * PS:(s + 1) * PS], rhs=wd[:, m, n * 512:(n + 1) * 512], start=(m == 0), stop=(m == 21)), [Thid, Twd], [Tbo])
                            V(lambda e: e.tensor_tensor(out=xt[0:PS, s, n * 512:(n + 1) * 512], in0=xt[0:PS, s, n * 512:(n + 1) * 512], in1=bo[0:PS, :], op=ALU.add), [Txs[s], Tbo], [Txs[s]])
                        if s == 0:
                            for fn in hooks.get(22, []):
                                fn()
                    if l == 0:
                        stq(xs[row0:row0 + T, :].rearrange("(s p) d -> p s d", p=PS), xt[0:PS, 0:nsub, :], "c23_%d" % (i % 2), Txs[0:nsub], [Tdram])
                    else:
                        for s in range(nsub):
                            yt, Tyt = yt_b[s % 2]
                            A(lambda e: e.activation(out=junk[0:PS, :], in_=xt[0:PS, s, :], func=AF.Square, accum_out=ss[0:PS, 0:1]), [Txs[s]], [Tjunk, Tss])
                            rstd_inplace(ss[0:PS, 0:1], Tss, 1.0 / D)
                            V(lambda e: e.scalar_tensor_tensor(out=yt[0:PS, :], in0=xt[0:PS, s, :], scalar=ss[0:PS, 0:1], in1=gfin[0:PS, :], op0=ALU.mult, op1=ALU.mult), [Txs[s], Tss, Tgfin], [Tyt])
                            stq(y_out[row0 + s * PS:row0 + (s + 1) * PS, :], yt[0:PS, :], "c24_%d" % (s % 2), [Tyt])

                P_load(0)
                P_select(0)
                for s in range(TILES[0][1] // TILES[0][2]):
                    P1(0, s)
                    P2(0, s)
                for i in range(NT):
                    hooks = {}
                    if i + 1 < NT:
                        nsub_n = TILES[i + 1][1] // TILES[i + 1][2]
                        hooks.setdefault(0, []).append(lambda i=i: P_load(i + 1))
                        hooks.setdefault(1, []).append(lambda i=i: P_select(i + 1))
                        for s in range(nsub_n):
                            hooks.setdefault(3 + 5 * s, []).append(lambda i=i, s=s: P1(i + 1, s))
                            hooks.setdefault(3 + 5 * s + 4, []).append(lambda i=i, s=s: P2(i + 1, s))
                    M(i, hooks)
                fw.barrier_all()

        fw.barrier_all()
        print("[kernel] instructions:", fw.n_instr, "semaphores:", fw.nsem)
    return nc


_NC_CACHE = {}


def _f32(a):
    return np.ascontiguousarray(np.asarray(a, dtype=np.float32))


def kernel(x_prompt, x_sample, cache_ckv, cache_krope, state_gla,
           g_attn, w_in, g_qn, w_uq, g_kvn, w_ukv, w_a2, b_a2, g_gla_on, w_o,
           g_ffn, w_gu, w_down, g_final):
    x_prompt = _f32(x_prompt); x_sample = _f32(x_sample)
    cache_ckv = _f32(cache_ckv); cache_krope = _f32(cache_krope); state_gla = _f32(state_gla)
    w_in = _f32(w_in); w_uq = _f32(w_uq); w_ukv = _f32(w_ukv); w_gu = _f32(w_gu)

    perm = np.concatenate([np.arange(16, 32), np.arange(0, 16)])
    w_in_x = np.concatenate([w_in, w_in[:, :, OKR:OKR + 32][:, :, perm]], axis=2)
    wq = w_uq.reshape(2, 256, 8, 96)
    wq_raw = wq
    wq_perm = np.concatenate([wq[..., :64], wq[..., 64:][..., perm]], axis=-1)
    w_uq_all = np.ascontiguousarray(np.stack([wq_raw, wq_perm], axis=3))
    wkv = w_ukv.reshape(2, 256, 8, 128)
    w_uk_all = np.ascontiguousarray(wkv[..., :64])
    w_uv_all = np.ascontiguousarray(wkv[..., 64:])
    wa = w_gu[:, :, :DFF].reshape(2, 8, 128, 22, 128)
    wu = w_gu[:, :, DFF:].reshape(2, 8, 128, 22, 128)
    w_gu_t = np.ascontiguousarray(np.concatenate([wa, wu], axis=-1).transpose(0, 3, 2, 1, 4))
    ident = np.eye(128, dtype=np.float32)
    jj, ii = np.meshgrid(np.arange(64), np.arange(64), indexing="ij")
    trilt = (jj <= ii).astype(np.float32)
    triut = (jj > ii).astype(np.float32)
    selmat = np.zeros((128, 64), np.float32)
    selmat[64 + np.arange(64), np.arange(64)] = 1.0
    half = 16
    inv = (10000.0 ** (-np.arange(half, dtype=np.float32) / half)).astype(np.float32)
    ropec = np.zeros((96, 2), np.float32)
    for r_ in range(96):
        ropec[r_, 0] = inv[r_ % 16]
        ropec[r_, 1] = -1.0 if (r_ % 32) < 16 else 1.0
    pos_g = np.arange(SEQ, dtype=np.float32)[None, :]
    g_cat = np.concatenate([_f32(g_qn), _f32(g_kvn)], axis=1)[:, None, :]

    shared = {
        "w_in": w_in_x, "w_uq_all": w_uq_all, "w_uk_all": w_uk_all, "w_uv_all": w_uv_all,
        "w_a2": _f32(w_a2), "b_a2": _f32(b_a2)[:, None, :], "g_attn": _f32(g_attn)[:, None, :],
        "g_ffn": _f32(g_ffn)[:, None, :], "g_final": _f32(g_final)[None, :], "g_cat": _f32(g_cat),
        "g_on": _f32(g_gla_on)[:, None, :], "w_o": _f32(w_o), "w_gu": w_gu_t, "w_dn": _f32(w_down),
        "ident": ident, "trilt": trilt, "triut": triut, "pos_g": pos_g, "ropec": ropec, "selmat": selmat,
    }
    in_maps = []
    for c in range(8):
        g, r = c // 4, c % 4
        xs_ = np.concatenate([x_prompt[g, r * NTOK:(r + 1) * NTOK], x_sample[2 * c:2 * c + 2].reshape(NS, D)], axis=0)
        selm = np.zeros((128, 8), np.float32)
        selm[:, r] = 1.0
        for j in range(4):
            selm[:, 4 + j] = 1.0 if j < r else 0.0
        pos_l = np.concatenate([np.arange(r * NTOK, (r + 1) * NTOK), PAST + np.arange(32), PAST + np.arange(32)]).astype(np.float32)[None, :]
        m = dict(shared)
        m.update({
            "x": np.ascontiguousarray(xs_),
            "cckv": np.ascontiguousarray(cache_ckv[:, 2 * c:2 * c + 2]),
            "ckr": np.ascontiguousarray(cache_krope[:, 2 * c:2 * c + 2]),
            "sgla": np.ascontiguousarray(state_gla[:, 2 * c:2 * c + 2]),
            "w_uq_loc": np.ascontiguousarray(w_uq_all[:, :, 2 * r:2 * r + 2]),
            "w_uk_loc": np.ascontiguousarray(w_uk_all[:, :, 2 * r:2 * r + 2]),
            "w_uv_loc": np.ascontiguousarray(w_uv_all[:, :, 2 * r:2 * r + 2]),
            "selm": selm, "pos_l": pos_l,
        })
        in_maps.append(m)

    if "nc" not in _NC_CACHE:
        _NC_CACHE["nc"] = build_program()
    res = run_bass_kernel_spmd(_NC_CACHE["nc"], in_maps, core_ids=list(range(8)))
    R = res.results

    y_p = np.zeros((2, SEQ, D), np.float32)
    y_s = np.zeros((16, 32, D), np.float32)
    ckv_p = np.zeros((2, 2, SEQ, 256), np.float32)
    kr_p = np.zeros((2, 2, SEQ, 32), np.float32)
    gla_p = np.zeros((2, 2, 4, 64, 128), np.float32)
    ckv_s = np.zeros((2, 16, 32, 256), np.float32)
    kr_s = np.zeros((2, 16, 32, 32), np.float32)
    gla_s = np.zeros((2, 16, 4, 64, 128), np.float32)
    for c in range(8):
        g, r = c // 4, c % 4
        o = R[c]
        y_p[g, r * NTOK:(r + 1) * NTOK] = o["y"][:NTOK]
        y_s[2 * c:2 * c + 2] = o["y"][NTOK:].reshape(2, 32, D)
        ckv_p[:, g, r * NTOK:(r + 1) * NTOK] = o["ckv_o"][:, :NTOK]
        kr_p[:, g, r * NTOK:(r + 1) * NTOK] = o["kr_o"][:, :NTOK]
        ckv_s[:, 2 * c:2 * c + 2] = o["ckv_o"][:, NTOK:].reshape(2, 2, 32, 256)
        kr_s[:, 2 * c:2 * c + 2] = o["kr_o"][:, NTOK:].reshape(2, 2, 32, 32)
        gl = o["gla_o"].reshape(2, 3, 64, 4, 128).transpose(0, 1, 3, 2, 4)
        if r == 3:
            gla_p[:, g] = gl[:, 0]
        gla_s[:, 2 * c] = gl[:, 1]
        gla_s[:, 2 * c + 1] = gl[:, 2]
    return (y_p, y_s, ckv_p, kr_p, gla_p, ckv_s, kr_s, gla_s)
```
